# Optimizing a Trainium2 kernel written in Bass

```python
import math
import jax, jax.numpy as jnp
from jax import lax
import numpy as np

D_MODEL = 1024
BATCH = 1
SEQ = 16384
DEPTH = 2

N_MIXERS = 2
N_A = (DEPTH + 1) // N_MIXERS
N_B = DEPTH // N_MIXERS

S5_WIDTH = D_MODEL
S5_GROUP = 16
S5_GROUPS = S5_WIDTH // S5_GROUP
S5_STATE = 64
S5_CHUNK = 128
S5_DT_MIN = 1e-3
S5_DT_MAX = 1e-1

N_HEADS = 8
HEAD_DIM = D_MODEL // N_HEADS
ATTN_WIDTH = N_HEADS * HEAD_DIM
MOBA_BLOCK = 256
MOBA_TOPK = 3
Q_BLOCK = 128
ROPE_THETA = 500000.0
ROT_DIM = HEAD_DIM // 4

D_FF = 2816
MACARON_WEIGHT = 0.5
EPS = 1e-6

kernel_name = 'hybrid_s5_moba_macaron'


def rms_norm(x, g):
    xf = x.astype(jnp.float32)
    y = xf * lax.rsqrt(jnp.mean(xf * xf, axis=-1, keepdims=True) + EPS)
    return (y * g.astype(jnp.float32)).astype(x.dtype)


def swiglu(h, w_gate, w_up, w_down):
    return (jax.nn.silu(h @ w_gate) * (h @ w_up)) @ w_down


def cmul(ar, ai, br, bi):
    return ar * br - ai * bi, ar * bi + ai * br


def s5_mixer(h, w_in, a_re, a_im, log_dt, b_re, b_im, c_re, c_im, d_skip, w_glu, w_out):
    f32 = jnp.float32
    bsz, seq, _ = h.shape
    u = (h @ w_in).astype(f32)
    dt = jnp.exp(log_dt.astype(f32))[:, None]
    a_re = a_re.astype(f32)
    a_im = a_im.astype(f32)
    mag = jnp.exp(a_re * dt)
    lam_re = mag * jnp.cos(a_im * dt)
    lam_im = mag * jnp.sin(a_im * dt)
    den = a_re * a_re + a_im * a_im
    num_re = lam_re - 1.0
    coef_re = (num_re * a_re + lam_im * a_im) / den
    coef_im = (lam_im * a_re - num_re * a_im) / den
    bb_re, bb_im = cmul(coef_re[..., None], coef_im[..., None], b_re.astype(f32), b_im.astype(f32))
    c_re = c_re.astype(f32)
    c_im = c_im.astype(f32)
    n_chunks = seq // S5_CHUNK
    u_chunks = u.reshape(bsz, n_chunks, S5_CHUNK, S5_GROUPS, S5_GROUP).transpose(1, 0, 2, 3, 4)
    lam_re_b = jnp.broadcast_to(lam_re, (bsz, S5_CHUNK, S5_GROUPS, S5_STATE))
    lam_im_b = jnp.broadcast_to(lam_im, (bsz, S5_CHUNK, S5_GROUPS, S5_STATE))

    def combine(left, right):
        a1r, a1i, b1r, b1i = left
        a2r, a2i, b2r, b2i = right
        ar, ai = cmul(a2r, a2i, a1r, a1i)
        br, bi = cmul(a2r, a2i, b1r, b1i)
        return ar, ai, br + b2r, bi + b2i

    def step(carry, u_c):
        h_re0, h_im0 = carry
        bu_re = jnp.einsum('bcgn,gpn->bcgp', u_c, bb_re)
        bu_im = jnp.einsum('bcgn,gpn->bcgp', u_c, bb_im)
        pr, pi, sr, si = lax.associative_scan(combine, (lam_re_b, lam_im_b, bu_re, bu_im), axis=1)
        cr, ci = cmul(pr, pi, h_re0[:, None], h_im0[:, None])
        st_re = sr + cr
        st_im = si + ci
        y = jnp.einsum('bcgp,gnp->bcgn', st_re, c_re) - jnp.einsum('bcgp,gnp->bcgn', st_im, c_im)
        return (st_re[:, -1], st_im[:, -1]), y

    init = (jnp.zeros((bsz, S5_GROUPS, S5_STATE), f32), jnp.zeros((bsz, S5_GROUPS, S5_STATE), f32))
    _, ys = lax.scan(step, init, u_chunks)
    y = ys.transpose(1, 0, 2, 3, 4).reshape(bsz, seq, S5_WIDTH) + d_skip.astype(f32) * u
    g = jax.nn.gelu(y)
    g = g * jax.nn.sigmoid(g @ w_glu.astype(f32))
    return g.astype(h.dtype) @ w_out


def partial_rope(x, cos, sin):
    half = ROT_DIM // 2
    x1 = x[..., :half]
    x2 = x[..., half:ROT_DIM]
    c = cos[None, :, None, :]
    s = sin[None, :, None, :]
    return jnp.concatenate([x1 * c - x2 * s, x2 * c + x1 * s, x[..., ROT_DIM:]], axis=-1)


def moba_mixer(h, w_qkv, q_gain, k_gain, w_out):
    f32 = jnp.float32
    bsz, seq, _ = h.shape
    qkv = (h @ w_qkv).reshape(bsz, seq, 3, N_HEADS, HEAD_DIM)
    q = rms_norm(qkv[:, :, 0], q_gain).astype(f32)
    k = rms_norm(qkv[:, :, 1], k_gain).astype(f32)
    v = qkv[:, :, 2].astype(f32)
    pos = jnp.arange(seq, dtype=f32)
    inv_freq = ROPE_THETA ** (-jnp.arange(0, ROT_DIM, 2, dtype=f32) / ROT_DIM)
    ang = pos[:, None] * inv_freq[None, :]
    cos, sin = jnp.cos(ang), jnp.sin(ang)
    q = partial_rope(q, cos, sin) * (HEAD_DIM ** -0.5)
    k = partial_rope(k, cos, sin)
    n_blocks = -(-seq // MOBA_BLOCK)
    pad = n_blocks * MOBA_BLOCK - seq
    k_pad = jnp.pad(k, ((0, 0), (0, pad), (0, 0), (0, 0)))
    v_pad = jnp.pad(v, ((0, 0), (0, pad), (0, 0), (0, 0)))
    k_blk = k_pad.reshape(bsz, n_blocks, MOBA_BLOCK, N_HEADS, HEAD_DIM)
    v_blk = v_pad.reshape(bsz, n_blocks, MOBA_BLOCK, N_HEADS, HEAD_DIM)
    k_mean = k_blk.mean(axis=2)
    k_bh = k_blk.transpose(0, 3, 1, 2, 4)
    v_bh = v_blk.transpose(0, 3, 1, 2, 4)
    top_k = min(MOBA_TOPK, n_blocks)
    b_idx = jnp.arange(bsz)[:, None, None]
    h_idx = jnp.arange(N_HEADS)[None, None, :]
    blk_ids = jnp.arange(n_blocks)

    def attend_chunk(c):
        start = c * Q_BLOCK
        q_c = lax.dynamic_slice_in_dim(q, start, Q_BLOCK, axis=1)
        q_pos = start + jnp.arange(Q_BLOCK)
        own = start // MOBA_BLOCK
        gate = jnp.einsum('bqhd,bnhd->bqhn', q_c, k_mean)
        gate = jnp.where(blk_ids < own, gate, -jnp.inf)
        _, sel = lax.top_k(gate, top_k)
        k_own = lax.dynamic_slice_in_dim(k_pad, own * MOBA_BLOCK, MOBA_BLOCK, axis=1)
        v_own = lax.dynamic_slice_in_dim(v_pad, own * MOBA_BLOCK, MOBA_BLOCK, axis=1)
        k_pos = own * MOBA_BLOCK + jnp.arange(MOBA_BLOCK)
        s_own = jnp.einsum('bqhd,bkhd->bqhk', q_c, k_own)
        causal = (k_pos[None, :] <= q_pos[:, None])[None, :, None, :]
        s_own = jnp.where(causal, s_own, -jnp.inf)
        scores = []
        for j in range(top_k):
            k_sel = k_bh[b_idx, h_idx, sel[..., j]]
            s_j = jnp.einsum('bqhd,bqhkd->bqhk', q_c, k_sel)
            scores.append(jnp.where(j < own, s_j, -jnp.inf))
        scores.append(s_own)
        p = jax.nn.softmax(jnp.concatenate(scores, axis=-1), axis=-1)
        p_parts = jnp.split(p, top_k + 1, axis=-1)
        out = jnp.einsum('bqhk,bkhd->bqhd', p_parts[-1], v_own)
        for j in range(top_k):
            v_sel = v_bh[b_idx, h_idx, sel[..., j]]
            out = out + jnp.einsum('bqhk,bqhkd->bqhd', p_parts[j], v_sel)
        return out

    outs = lax.map(attend_chunk, jnp.arange(seq // Q_BLOCK))
    o = outs.transpose(1, 0, 2, 3, 4).reshape(bsz, seq, ATTN_WIDTH).astype(h.dtype)
    return o @ w_out


def setup_inputs(seed: int = 0) -> dict:
    key = jax.random.key(seed)
    ks = jax.random.split(key, 24)
    f32 = jnp.float32
    nrm = lambda k, shape, scale: jax.random.normal(k, shape, f32) * scale
    x = nrm(ks[0], (BATCH, SEQ, D_MODEL), 1.0)
    ffn_norm = 1.0 + nrm(ks[1], (DEPTH, 2, D_MODEL), 0.02)
    ffn_w_gate = nrm(ks[2], (DEPTH, 2, D_MODEL, D_FF), D_MODEL ** -0.5)
    ffn_w_up = nrm(ks[3], (DEPTH, 2, D_MODEL, D_FF), D_MODEL ** -0.5)
    ffn_w_down = nrm(ks[4], (DEPTH, 2, D_FF, D_MODEL), D_FF ** -0.5)
    mix_norm = 1.0 + nrm(ks[5], (DEPTH, D_MODEL), 0.02)
    s5_w_in = nrm(ks[6], (N_A, D_MODEL, S5_WIDTH), D_MODEL ** -0.5)
    s5_a_re = -0.5 + nrm(ks[7], (N_A, S5_GROUPS, S5_STATE), 0.01)
    s5_a_im = math.pi * jnp.arange(S5_STATE, dtype=f32)[None, None, :] + nrm(ks[8], (N_A, S5_GROUPS, S5_STATE), 0.01)
    s5_log_dt = jax.random.uniform(ks[9], (N_A, S5_GROUPS), f32, math.log(S5_DT_MIN), math.log(S5_DT_MAX))
    s5_b_re = nrm(ks[10], (N_A, S5_GROUPS, S5_STATE, S5_GROUP), (2 * S5_GROUP) ** -0.5)
    s5_b_im = nrm(ks[11], (N_A, S5_GROUPS, S5_STATE, S5_GROUP), (2 * S5_GROUP) ** -0.5)
    s5_c_re = nrm(ks[12], (N_A, S5_GROUPS, S5_GROUP, S5_STATE), S5_STATE ** -0.5)
    s5_c_im = nrm(ks[13], (N_A, S5_GROUPS, S5_GROUP, S5_STATE), S5_STATE ** -0.5)
    s5_d = nrm(ks[14], (N_A, S5_WIDTH), 1.0)
    s5_w_glu = nrm(ks[15], (N_A, S5_WIDTH, S5_WIDTH), S5_WIDTH ** -0.5)
    s5_w_out = nrm(ks[16], (N_A, S5_WIDTH, D_MODEL), S5_WIDTH ** -0.5)
    moba_w_qkv = nrm(ks[17], (N_B, D_MODEL, 3 * ATTN_WIDTH), D_MODEL ** -0.5)
    moba_q_norm = 1.0 + nrm(ks[18], (N_B, HEAD_DIM), 0.02)
    moba_k_norm = 1.0 + nrm(ks[19], (N_B, HEAD_DIM), 0.02)
    moba_w_out = nrm(ks[20], (N_B, ATTN_WIDTH, D_MODEL), ATTN_WIDTH ** -0.5)
    return {'x': x, 'ffn_norm': ffn_norm, 'ffn_w_gate': ffn_w_gate, 'ffn_w_up': ffn_w_up,
            'ffn_w_down': ffn_w_down, 'mix_norm': mix_norm, 's5_w_in': s5_w_in,
            's5_a_re': s5_a_re, 's5_a_im': s5_a_im, 's5_log_dt': s5_log_dt,
            's5_b_re': s5_b_re, 's5_b_im': s5_b_im, 's5_c_re': s5_c_re, 's5_c_im': s5_c_im,
            's5_d': s5_d, 's5_w_glu': s5_w_glu, 's5_w_out': s5_w_out,
            'moba_w_qkv': moba_w_qkv, 'moba_q_norm': moba_q_norm, 'moba_k_norm': moba_k_norm,
            'moba_w_out': moba_w_out}


def reference(x, ffn_norm, ffn_w_gate, ffn_w_up, ffn_w_down, mix_norm, s5_w_in,
              s5_a_re, s5_a_im, s5_log_dt, s5_b_re, s5_b_im, s5_c_re, s5_c_im,
              s5_d, s5_w_glu, s5_w_out, moba_w_qkv, moba_q_norm, moba_k_norm, moba_w_out):
    h = x
    for layer in range(DEPTH):
        h = h + MACARON_WEIGHT * swiglu(rms_norm(h, ffn_norm[layer, 0]), ffn_w_gate[layer, 0],
                                        ffn_w_up[layer, 0], ffn_w_down[layer, 0])
        hn = rms_norm(h, mix_norm[layer])
        i = layer // N_MIXERS
        if layer % N_MIXERS == 0:
            h = h + s5_mixer(hn, s5_w_in[i], s5_a_re[i], s5_a_im[i], s5_log_dt[i], s5_b_re[i],
                             s5_b_im[i], s5_c_re[i], s5_c_im[i], s5_d[i], s5_w_glu[i], s5_w_out[i])
        else:
            h = h + moba_mixer(hn, moba_w_qkv[i], moba_q_norm[i], moba_k_norm[i], moba_w_out[i])
        h = h + MACARON_WEIGHT * swiglu(rms_norm(h, ffn_norm[layer, 1]), ffn_w_gate[layer, 1],
                                        ffn_w_up[layer, 1], ffn_w_down[layer, 1])
    return h
```

```python
import numpy as np
import concourse.bass as bass
import concourse.mybir as mybir
from concourse.bass_utils import run_bass_kernel_spmd
from contextlib import ExitStack

F32 = mybir.dt.float32
BF16 = mybir.dt.bfloat16
F32R = mybir.dt.float32r
AF = mybir.ActivationFunctionType
ALU = mybir.AluOpType
AX = mybir.AxisListType

NCORES = 8
SEQ = 16384
D = 1024
DFF = 2816
TOK = SEQ // NCORES
CH = 512
NCH = TOK // CH
KT = D // 128
FT = DFF // 128
FH = FT // 2
EPS = 1e-6


class KB:
    def __init__(self, nc, es):
        self.nc, self.es = nc, es
        self.E = dict(pe=nc.tensor, act=nc.scalar, dve=nc.vector, pool=nc.gpsimd, sp=nc.sync)
        self.psem = {}
        self.pcnt = {}
        self.waited = {}
        self.nsem = 0
        self.ntile = 0

    def newsem(self, name):
        self.nsem += 1
        return self.es.enter_context(self.nc.semaphore(f"{name}_{self.nsem}"))

    def sb(self, name, shape, dt=F32):
        self.ntile += 1
        return self.es.enter_context(self.nc.sbuf_tensor(f"{name}_{self.ntile}", list(shape), dt))

    def ps(self, name, shape, dt=F32):
        self.ntile += 1
        return self.es.enter_context(self.nc.psum_tensor(f"{name}_{self.ntile}", list(shape), dt))

    def mark(self, e, ins):
        if e not in self.psem:
            self.psem[e] = self.newsem("p_" + e)
            self.pcnt[e] = 0
        self.pcnt[e] += 1
        ins.then_inc(self.psem[e], 1)
        return (self.psem[e], self.pcnt[e], e)

    def wait(self, e, *evs):
        for ev in evs:
            if ev is None:
                continue
            if isinstance(ev, list):
                self.wait(e, *ev)
                continue
            sem, val, _src = ev
            key = (e, sem.name if hasattr(sem, "name") else id(sem))
            if self.waited.get(key, 0) >= val:
                continue
            self.waited[key] = val
            self.E[e].wait_ge(sem, val)


class DmaSlot:
    def __init__(self, kb, name):
        self.sem = kb.newsem("d_" + name)
        self.cnt = 0

    def dma(self, kb, q, out, in_):
        kb.E[q].dma_start(out=out, in_=in_).then_inc(self.sem, 16)
        self.cnt += 16
        return (self.sem, self.cnt, "dma")


class Ring:
    def __init__(self, bufs):
        self.bufs = bufs
        self.rel = [[] for _ in bufs]
        self.i = 0

    def get(self, kb, eng):
        i = self.i
        self.i = (i + 1) % len(self.bufs)
        kb.wait(eng, self.rel[i])
        self.rel[i] = []
        return i

    def release(self, i, *evs):
        self.rel[i].extend(evs)


class St:
    pass


def tile_w(w):
    K, N = w.shape
    return np.ascontiguousarray(w.reshape(K // 128, 128, N // 128, 128).transpose(2, 1, 0, 3))


def tile_wd(w):
    a = w.reshape(2, FH, 128, KT, 128)
    return np.ascontiguousarray(a.transpose(0, 3, 2, 1, 4))


def col_layout(v):
    return np.ascontiguousarray(v.reshape(-1, 128).T)


def tok_state(kb, n_gain_cols):
    nc = kb.nc
    S = St()
    S.hT = kb.sb("hT", [128, KT, TOK], F32)
    S.hn = kb.sb("hn", [128, KT, TOK], BF16)
    S.act = kb.sb("act", [128, FH, TOK], BF16)
    S.ones = kb.sb("ones", [128, 128], BF16)
    S.gains = kb.sb("gains", [128, n_gain_cols], F32)
    S.psum = Ring([kb.ps("bank", [128, CH], F32) for _ in range(8)])
    S.wgu = Ring([(kb.sb("wg", [128, KT, 128], BF16), kb.sb("wu", [128, KT, 128], BF16)) for _ in range(3)])
    S.wgu_slot = [DmaSlot(kb, "wgu") for _ in range(3)]
    S.wd = Ring([kb.sb("wd", [128, FH, 128], BF16) for _ in range(3)])
    S.wd_slot = [DmaSlot(kb, "wd") for _ in range(3)]
    S.sq = Ring([kb.sb("sq", [128, CH], BF16) for _ in range(3)])
    S.rt = Ring([kb.sb("rt", [128, CH], F32) for _ in range(2)])
    S.rstd = Ring([kb.sb("rstd", [128, CH], F32) for _ in range(2)])
    S.sg = Ring([kb.sb("sg", [128, CH], BF16) for _ in range(3)])
    S.stage = Ring([kb.sb("stage", [128, CH], F32) for _ in range(3)])
    S.stage_slot = [DmaSlot(kb, "stage") for _ in range(3)]
    S.h_ev = [None] * NCH
    S.hn_ev = [None] * NCH
    S.hn_free = None
    S.act_free = None
    S.misc = DmaSlot(kb, "misc")
    S.ev_ones = kb.mark("pool", nc.gpsimd.memset(S.ones[:], 1.0))
    return S


def rmsnorm(kb, S, gcol0):
    nc = kb.nc
    for c in range(NCH):
        cs = slice(c * CH, (c + 1) * CH)
        b = S.psum.get(kb, "pe")
        bank = S.psum.bufs[b]
        for f in range(KT):
            i = S.sq.get(kb, "act")
            kb.wait("act", S.h_ev[c])
            e = kb.mark("act", nc.scalar.activation(S.sq.bufs[i][:], S.hT[:, f, cs], AF.Square))
            kb.wait("pe", e, S.ev_ones)
            e2 = kb.mark("pe", nc.tensor.matmul(bank[:], S.ones[:], S.sq.bufs[i][:], start=(f == 0), stop=(f == KT - 1)))
            S.sq.release(i, e2)
        ri = S.rt.get(kb, "act")
        kb.wait("act", e2)
        e3 = kb.mark("act", nc.scalar.activation(S.rt.bufs[ri][:], bank[:], AF.Sqrt, bias=EPS, scale=1.0 / D))
        S.psum.release(b, e3)
        si = S.rstd.get(kb, "dve")
        kb.wait("dve", e3)
        e4 = kb.mark("dve", nc.vector.reciprocal(S.rstd.bufs[si][:], S.rt.bufs[ri][:]))
        S.rt.release(ri, e4)
        kb.wait("dve", S.hn_free, S.h_ev[c], e4, S.ev_gains)
        for f in range(KT):
            ins = nc.vector.scalar_tensor_tensor(S.hn[:, f, cs], S.hT[:, f, cs], S.gains[:, gcol0 + f:gcol0 + f + 1],
                                                 S.rstd.bufs[si][:], ALU.mult, ALU.mult)
        e5 = kb.mark("dve", ins)
        S.rstd.release(si, e5)
        S.hn_ev[c] = e5


def load_wtile(kb, S, ring, slots, pairs):
    i = ring.get(kb, "pool")
    ev = None
    bufs = ring.bufs[i]
    if not isinstance(bufs, tuple):
        bufs = (bufs,)
    for buf, src in zip(bufs, pairs):
        ev = slots[i].dma(kb, "pool", buf[:], src)
    return i, ev


def ffn(kb, S, wg, wu, wd, gcol0):
    nc = kb.nc
    rmsnorm(kb, S, gcol0)
    for half in range(2):
        act_ev = [[None] * NCH for _ in range(FH)]
        for fl in range(FH):
            ft = half * FH + fl
            wi, ev_w = load_wtile(kb, S, S.wgu, S.wgu_slot, (wg[ft], wu[ft]))
            wgt, wut = S.wgu.bufs[wi]
            for cp in range(2):
                banks = [S.psum.get(kb, "pe") for _ in range(4)]
                kb.wait("pe", ev_w)
                for mi, w in enumerate((wgt, wut)):
                    for k in range(KT):
                        for cc in range(2):
                            c = cp * 2 + cc
                            kb.wait("pe", S.hn_ev[c])
                            ins = nc.tensor.matmul(S.psum.bufs[banks[mi * 2 + cc]][:], w[:, k, :],
                                                   S.hn[:, k, c * CH:(c + 1) * CH], start=(k == 0), stop=(k == KT - 1))
                ev_mm = kb.mark("pe", ins)
                S.hn_free = ev_mm
                if cp == 1:
                    S.wgu.release(wi, ev_mm)
                for cc in range(2):
                    c = cp * 2 + cc
                    ti = S.sg.get(kb, "act")
                    kb.wait("act", ev_mm)
                    e1 = kb.mark("act", nc.scalar.activation(S.sg.bufs[ti][:], S.psum.bufs[banks[cc]][:], AF.Silu))
                    kb.wait("dve", e1, ev_mm, S.act_free)
                    e2 = kb.mark("dve", nc.vector.tensor_tensor(S.act[:, fl, c * CH:(c + 1) * CH], S.sg.bufs[ti][:],
                                                                S.psum.bufs[banks[2 + cc]][:], ALU.mult))
                    S.sg.release(ti, e2)
                    S.psum.release(banks[cc], e1)
                    S.psum.release(banks[2 + cc], e2)
                    act_ev[fl][c] = e2
        for dm in range(KT):
            wi, ev_w = load_wtile(kb, S, S.wd, S.wd_slot, (wd[half, dm],))
            wdt = S.wd.bufs[wi]
            for c in range(NCH):
                cs = slice(c * CH, (c + 1) * CH)
                b = S.psum.get(kb, "pe")
                kb.wait("pe", ev_w)
                for t in range(FH):
                    kb.wait("pe", act_ev[t][c])
                    ins = nc.tensor.matmul(S.psum.bufs[b][:], wdt[:, t, :], S.act[:, t, cs], start=(t == 0), stop=(t == FH - 1))
                ev = kb.mark("pe", ins)
                kb.wait("dve", ev, S.h_ev[c])
                e3 = kb.mark("dve", nc.vector.scalar_tensor_tensor(S.hT[:, dm, cs], S.psum.bufs[b][:], 0.5, S.hT[:, dm, cs],
                                                                   ALU.mult, ALU.add))
                S.psum.release(b, e3)
                S.h_ev[c] = e3
            S.wd.release(wi, ev)
            S.act_free = ev


def proj_fm(kb, S, wt, n_tiles, evac, opnd=None, opnd_ev=None, t0=0):
    nc = kb.nc
    if opnd is None:
        opnd, opnd_ev = S.hn, S.hn_ev
    for ot in range(t0, t0 + n_tiles):
        wi, ev_w = load_wtile(kb, S, S.wgu, S.wgu_slot, (wt[ot],))
        w = S.wgu.bufs[wi][0]
        for cp in range(2):
            banks = [S.psum.get(kb, "pe") for _ in range(2)]
            kb.wait("pe", ev_w)
            for k in range(KT):
                for cc in range(2):
                    c = cp * 2 + cc
                    kb.wait("pe", opnd_ev[c])
                    ins = nc.tensor.matmul(S.psum.bufs[banks[cc]][:], w[:, k, :], opnd[:, k, c * CH:(c + 1) * CH],
                                           start=(k == 0), stop=(k == KT - 1))
            ev_mm = kb.mark("pe", ins)
            if opnd is S.hn:
                S.hn_free = ev_mm
            else:
                S.act_free = ev_mm
            if cp == 1:
                S.wgu.release(wi, ev_mm)
            for cc in range(2):
                c = cp * 2 + cc
                rel = evac(ot, c, S.psum.bufs[banks[cc]], ev_mm)
                S.psum.release(banks[cc], rel)


def store_fm(kb, S, out_dram, eng="act"):
    nc = kb.nc

    def evac(ot, c, bank, ev_mm):
        i = S.stage.get(kb, eng)
        kb.wait(eng, ev_mm)
        if eng == "act":
            e = kb.mark("act", nc.scalar.copy(S.stage.bufs[i][:], bank[:]))
        else:
            e = kb.mark("dve", nc.vector.tensor_copy(S.stage.bufs[i][:], bank[:]))
        kb.wait("sp", e)
        ev = S.stage_slot[i].dma(kb, "sp", out_dram[ot * 128:(ot + 1) * 128, c * CH:(c + 1) * CH], S.stage.bufs[i][:])
        S.stage.release(i, ev)
        return e
    return evac


def load_hT(kb, S, xT):
    evs = []
    for f in range(KT):
        evs.append(S.misc.dma(kb, "sp", S.hT[:, f, :], xT[f * 128:(f + 1) * 128, :]))
    for c in range(NCH):
        S.h_ev[c] = evs[-1]


def store_hT(kb, S, outT):
    kb.wait("sp", [e for e in S.h_ev])
    evs = []
    for f in range(KT):
        evs.append(S.misc.dma(kb, "sp", outT[f * 128:(f + 1) * 128, :], S.hT[:, f, :]))
    return evs[-1]


def finish(kb, S):
    for sl in S.stage_slot + [S.misc]:
        if sl.cnt:
            kb.E["sp"].wait_ge(sl.sem, sl.cnt)


def build_phase_a():
    nc = bass.Bass("TRN2", target_bir_lowering=False)
    xT = nc.dram_tensor("xT", [D, TOK], F32, kind="ExternalInput").ap()
    gains = nc.dram_tensor("gains", [128, 16], F32, kind="ExternalInput").ap()
    wg = nc.dram_tensor("wg", [FT, 128, KT, 128], F32, kind="ExternalInput").ap()
    wu = nc.dram_tensor("wu", [FT, 128, KT, 128], F32, kind="ExternalInput").ap()
    wd = nc.dram_tensor("wd", [2, KT, 128, FH, 128], F32, kind="ExternalInput").ap()
    win = nc.dram_tensor("win", [KT, 128, KT, 128], F32, kind="ExternalInput").ap()
    h1T = nc.dram_tensor("h1T", [D, TOK], F32, kind="ExternalOutput").ap()
    uT = nc.dram_tensor("uT", [D, TOK], F32, kind="ExternalOutput").ap()
    with ExitStack() as es:
        kb = KB(nc, es)
        S = tok_state(kb, 16)
        S.ev_gains = S.misc.dma(kb, "sp", S.gains[:], gains)
        load_hT(kb, S, xT)
        ffn(kb, S, wg, wu, wd, 0)
        store_hT(kb, S, h1T)
        rmsnorm(kb, S, 8)
        proj_fm(kb, S, win, KT, store_fm(kb, S, uT))
        finish(kb, S)
    return nc


TRACE = False
LAST_NS = []


def _run(nc, in_maps):
    res = run_bass_kernel_spmd(nc, in_maps, core_ids=list(range(NCORES)), trace=TRACE)
    if TRACE:
        LAST_NS.append(res.exec_time_ns)
    return res


def run_phase_a(inp):
    nc = build_phase_a()
    x = inp["x"][0]
    gains = np.concatenate([col_layout(inp["ffn_norm"][0, 0]), col_layout(inp["mix_norm"][0])], axis=1)
    common = dict(gains=gains, wg=tile_w(inp["ffn_w_gate"][0, 0]), wu=tile_w(inp["ffn_w_up"][0, 0]),
                  wd=tile_wd(inp["ffn_w_down"][0, 0]), win=tile_w(inp["s5_w_in"][0]))
    in_maps = []
    for c in range(NCORES):
        m = dict(common)
        m["xT"] = np.ascontiguousarray(x[c * TOK:(c + 1) * TOK].T)
        in_maps.append(m)
    res = _run(nc, in_maps)
    h1 = np.concatenate([r["h1T"].T for r in res.results], axis=0)
    u = np.concatenate([r["uT"].T for r in res.results], axis=0)
    return h1, u


NBLK = SEQ // 256
NSEG = 8
SEGT = SEQ // NSEG
BIG = 1.0e30


def build_phase_d():
    nc = bass.Bass("TRN2", target_bir_lowering=False)
    qT_d = nc.dram_tensor("qT", [128, SEQ], F32, kind="ExternalInput").ap()
    kT_d = nc.dram_tensor("kT", [128, SEQ], F32, kind="ExternalInput").ap()
    v_d = nc.dram_tensor("v", [SEQ, 128], F32, kind="ExternalInput").ap()
    tri_d = nc.dram_tensor("tri", [128, 128], F32, kind="ExternalInput").ap()
    o_d = nc.dram_tensor("o", [SEQ, 128], F32, kind="ExternalOutput").ap()
    with ExitStack() as es:
        kb = KB(nc, es)
        attention(kb, qT_d, kT_d, v_d, tri_d, o_d)
    return nc


def attention(kb, qT_d, kT_d, v_d, tri_d, o_d):
    nc = kb.nc
    qT = kb.sb("qT", [128, SEQ], BF16)
    kT = kb.sb("kT", [128, SEQ], BF16)
    va = kb.sb("va", [128, SEQ // 128, 130], BF16)
    tri = kb.sb("tri", [128, 128], BF16)
    kms = kb.sb("kms", [128, NBLK], F32)
    kmT = kb.sb("kmT", [128, NBLK], BF16)
    gate = Ring([kb.sb("gate", [128, 2, NBLK], F32) for _ in range(2)])
    m8 = Ring([kb.sb("m8", [128, 2, 8], F32) for _ in range(2)])
    sel = Ring([kb.sb("sel", [128, 2, NBLK], F32) for _ in range(2)])
    pT = Ring([kb.sb("pT", [128, 512], BF16) for _ in range(5)])
    oacc = Ring([kb.sb("oacc", [128, 2, 130], F32) for _ in range(2)])
    rec = Ring([kb.sb("rec", [128, 2], F32) for _ in range(2)])
    ostage = Ring([kb.sb("ostage", [128, 2, 128], F32) for _ in range(2)])
    ostage_slot = [DmaSlot(kb, "ost") for _ in range(2)]
    psS = Ring([kb.ps("psS", [128, 512], F32) for _ in range(4)])
    psO = Ring([kb.ps("psO", [128, 512], F32) for _ in range(3)])
    psG = Ring([kb.ps("psG", [128, 512], F32) for _ in range(1)])
    seg = [DmaSlot(kb, "seg") for _ in range(NSEG)]
    cslot = DmaSlot(kb, "const")

    ev_tri = cslot.dma(kb, "pool", tri[:], tri_d)
    ev_ones = kb.mark("pool", nc.gpsimd.memset(va[:, :, 128:130], 1.0))
    ev_ginit = None
    for g in gate.bufs:
        ev_ginit = kb.mark("pool", nc.gpsimd.memset(g[:], -BIG))
    seg_ev = []
    v_t = v_d.rearrange("(t p) d -> p t d", p=128)
    for s in range(NSEG):
        cs = slice(s * SEGT, (s + 1) * SEGT)
        ts = slice(s * SEGT // 128, (s + 1) * SEGT // 128)
        seg[s].dma(kb, "pool", kT[:, cs], kT_d[:, cs])
        seg[s].dma(kb, "pool", qT[:, cs], qT_d[:, cs])
        seg_ev.append(seg[s].dma(kb, "pool", va[:, ts, 0:128], v_t[:, ts, :]))
    km_ev = []
    bps = NBLK // NSEG
    for s in range(NSEG):
        kb.wait("dve", seg_ev[s])
        e = kb.mark("dve", nc.vector.tensor_reduce(kms[:, s * bps:(s + 1) * bps],
                                                   kT[:, s * SEGT:(s + 1) * SEGT].rearrange("p (b k) -> p b k", k=256),
                                                   AX.X, ALU.add))
        kb.wait("dve", e)
        km_ev.append(kb.mark("dve", nc.vector.tensor_scalar(kmT[:, s * bps:(s + 1) * bps], kms[:, s * bps:(s + 1) * bps],
                                                            1.0 / 256, None, ALU.mult)))

    SKEW = 2
    tasks = []
    for j in range(NBLK):
        P = St()
        P.j, P.q0 = j, 256 * j
        P.si = P.ev_sel = P.ai = P.ev_acc = None
        for b in [-1] + list(range(j)):
            tasks.append((P, b))

    def emit_gate(P):
        j, q0 = P.j, P.q0
        gb = psG.get(kb, "pe")
        kb.wait("pe", km_ev[(j - 1) // bps])
        for cq in range(2):
            ins = nc.tensor.matmul(psG.bufs[gb][:, cq * 64:cq * 64 + j], qT[:, q0 + cq * 128:q0 + (cq + 1) * 128],
                                   kmT[:, 0:j], start=True, stop=True)
        ev_g = kb.mark("pe", ins)
        gi = gate.get(kb, "dve")
        kb.wait("dve", ev_g, ev_ginit)
        eg = kb.mark("dve", nc.vector.tensor_copy(gate.bufs[gi][:, :, 0:j],
                                                  psG.bufs[gb][:, 0:128].rearrange("p (c b) -> p c b", b=64)[:, :, 0:j]))
        psG.release(gb, eg)
        mi = m8.get(kb, "dve")
        kb.wait("dve", eg)
        for cq in range(2):
            em = kb.mark("dve", nc.vector.max(m8.bufs[mi][:, cq, :], gate.bufs[gi][:, cq, :]))
        P.si = sel.get(kb, "dve")
        kb.wait("dve", em)
        for cq in range(2):
            P.ev_sel = kb.mark("dve", nc.vector.tensor_scalar(sel.bufs[P.si][:, cq, 0:j], gate.bufs[gi][:, cq, 0:j],
                                                              m8.bufs[mi][:, cq, 2:3], None, ALU.is_ge))
        gate.release(gi, P.ev_sel)
        m8.release(mi, P.ev_sel)

    def emit_s(P, b, T):
        j, q0 = P.j, P.q0
        kb.wait("pe", seg_ev[q0 // SEGT])
        qs = qT[:, q0:q0 + 256]
        sb_ = psS.get(kb, "pe")
        S_ = psS.bufs[sb_]
        T.pi = pT.get(kb, "act")
        P_ = pT.bufs[T.pi]
        if b < 0:
            if j >= 1:
                emit_gate(P)
            tA, tB = 2 * j, 2 * j + 1
            nc.tensor.matmul(S_[:, 0:256], kT[:, tA * 128:(tA + 1) * 128], qs, start=True, stop=True)
            ev_s = kb.mark("pe", nc.tensor.matmul(S_[:, 256:384], kT[:, tB * 128:(tB + 1) * 128], qT[:, q0 + 128:q0 + 256],
                                                  start=True, stop=True))
            kb.wait("act", ev_s)
            ev_p = kb.mark("act", nc.scalar.activation(P_[:, 0:384], S_[:, 0:384], AF.Exp))
            psS.release(sb_, ev_p)
            kb.wait("pool", ev_p, ev_tri)
            nc.gpsimd.tensor_tensor(P_[:, 0:128], P_[:, 0:128], tri[:], ALU.mult)
            T.ev_p = kb.mark("pool", nc.gpsimd.tensor_tensor(P_[:, 256:384], P_[:, 256:384], tri[:], ALU.mult))
        else:
            nc.tensor.matmul(S_[:, 0:256], kT[:, (2 * b) * 128:(2 * b + 1) * 128], qs, start=True, stop=True)
            ev_s = kb.mark("pe", nc.tensor.matmul(S_[:, 256:512], kT[:, (2 * b + 1) * 128:(2 * b + 2) * 128], qs,
                                                  start=True, stop=True))
            kb.wait("act", ev_s)
            T.ev_p = kb.mark("act", nc.scalar.activation(P_[:], S_[:], AF.Exp))
            psS.release(sb_, T.ev_p)

    def emit_rest(P, b, T):
        j, q0 = P.j, P.q0
        P_ = pT.bufs[T.pi]
        ob = psO.get(kb, "pe")
        O_ = psO.bufs[ob]
        kb.wait("pe", T.ev_p, ev_ones)
        if b < 0:
            tA, tB = 2 * j, 2 * j + 1
            nc.tensor.matmul(O_[:, 0:129], P_[:, 0:128], va[:, tA, 0:129], start=True, stop=True)
            nc.tensor.matmul(O_[:, 256:385], P_[:, 128:256], va[:, tA, 0:129], start=True, stop=False)
            ev_o = kb.mark("pe", nc.tensor.matmul(O_[:, 256:385], P_[:, 256:384], va[:, tB, 0:129], start=False, stop=True))
            pT.release(T.pi, ev_o)
            P.ai = oacc.get(kb, "dve")
            A_ = oacc.bufs[P.ai]
            kb.wait("dve", ev_o)
            P.ev_acc = kb.mark("dve", nc.vector.tensor_copy(A_[:, :, 0:129],
                                                            O_[:].rearrange("p (c x) -> p c x", x=256)[:, :, 0:129]))
            psO.release(ob, P.ev_acc)
        else:
            A_ = oacc.bufs[P.ai]
            nc.tensor.matmul(O_[:, 0:129], P_[:, 0:128], va[:, 2 * b, 0:129], start=True, stop=False)
            nc.tensor.matmul(O_[:, 0:129], P_[:, 256:384], va[:, 2 * b + 1, 0:129], start=False, stop=True)
            nc.tensor.matmul(O_[:, 256:385], P_[:, 128:256], va[:, 2 * b, 0:129], start=True, stop=False)
            ev_o = kb.mark("pe", nc.tensor.matmul(O_[:, 256:385], P_[:, 384:512], va[:, 2 * b + 1, 0:129], start=False, stop=True))
            pT.release(T.pi, ev_o)
            kb.wait("dve", ev_o, P.ev_sel, P.ev_acc)
            for cq in range(2):
                P.ev_acc = kb.mark("dve", nc.vector.scalar_tensor_tensor(A_[:, cq, 0:129], O_[:, cq * 256:cq * 256 + 129],
                                                                         sel.bufs[P.si][:, cq, b:b + 1], A_[:, cq, 0:129],
                                                                         ALU.mult, ALU.add))
            psO.release(ob, P.ev_acc)
        if b == j - 1:
            if P.si is not None:
                sel.release(P.si, P.ev_acc)
            ri = rec.get(kb, "dve")
            kb.wait("dve", P.ev_acc)
            er = kb.mark("dve", nc.vector.reciprocal(rec.bufs[ri][:], A_[:, :, 128]))
            oi = ostage.get(kb, "dve")
            kb.wait("dve", er)
            for cq in range(2):
                eo = kb.mark("dve", nc.vector.tensor_scalar(ostage.bufs[oi][:, cq, :], A_[:, cq, 0:128],
                                                            rec.bufs[ri][:, cq:cq + 1], None, ALU.mult))
            oacc.release(P.ai, eo)
            rec.release(ri, eo)
            kb.wait("sp", eo)
            ev = ostage_slot[oi].dma(kb, "sp", o_d[q0:q0 + 256, :].rearrange("(c p) d -> p c d", p=128), ostage.bufs[oi][:])
            ostage.release(oi, ev)

    tst = [St() for _ in tasks]
    for n in range(len(tasks) + SKEW):
        if n >= SKEW:
            P, b = tasks[n - SKEW]
            emit_rest(P, b, tst[n - SKEW])
        if n < len(tasks):
            P, b = tasks[n]
            emit_s(P, b, tst[n])
    for sl in ostage_slot:
        kb.E["sp"].wait_ge(sl.sem, sl.cnt)


def run_phase_d(q, k, v):
    nc = build_phase_d()
    tri = np.triu(np.ones((128, 128), np.float32))
    in_maps = []
    for h in range(NCORES):
        in_maps.append(dict(qT=np.ascontiguousarray(q[:, h, :].T), kT=np.ascontiguousarray(k[:, h, :].T),
                            v=np.ascontiguousarray(v[:, h, :]), tri=tri))
    res = _run(nc, in_maps)
    return np.concatenate([r["o"] for r in res.results], axis=1)


GPC = 8
NCK = SEQ // 8
NLV = 11
NKV = 24
SW = 384 + 2 * NLV
TWO_PI = 6.283185307179586
MAGIC = 12582912.0


def build_phase_b():
    nc = bass.Bass("TRN2", target_bir_lowering=False)
    dr = {}
    for n, shp in (("U", [GPC, 128, NCK]), ("a_re", [64, GPC]), ("a_im", [64, GPC]), ("ldt", [64, GPC]),
                   ("b_re", [64, GPC, 16]), ("b_im", [64, GPC, 16]), ("c_re", [64, GPC, 16]), ("c_im", [64, GPC, 16]),
                   ("kvec", [64, NKV]), ("selLo", [64, 128]), ("selHi", [64, 128]), ("ident", [128, 128]),
                   ("swap", [128, 128]), ("tmask", [128, 128])):
        dr[n] = nc.dram_tensor(n, shp, F32, kind="ExternalInput").ap()
    Y = nc.dram_tensor("Y", [GPC, 128, NCK], F32, kind="ExternalOutput").ap()
    with ExitStack() as es:
        kb = KB(nc, es)
        s5_scan(kb, dr, Y)
    return nc


def s5_scan(kb, dr, Y):
    nc = kb.nc
    G = GPC
    misc = DmaSlot(kb, "s5misc")
    P = {}
    ev_in = None
    for n in ("a_re", "a_im", "ldt", "b_re", "b_im", "c_re", "c_im", "kvec", "selLo", "selHi", "ident", "swap", "tmask"):
        shp = list(dr[n].shape)
        P[n] = kb.sb("p_" + n, shp, F32)
        ev_in = misc.dma(kb, "sp", P[n][:], dr[n])
    st = St()
    st.last = ev_in

    def V(fn, *a, **k):
        kb.wait("dve", st.last)
        st.last = kb.mark("dve", fn(*a, **k))

    def A(*a, **k):
        kb.wait("act", st.last)
        st.last = kb.mark("act", nc.scalar.activation(*a, **k))

    vt, vs, stt = nc.vector.tensor_tensor, nc.vector.tensor_scalar, nc.vector.scalar_tensor_tensor
    t = lambda n, shp: kb.sb("s5_" + n, shp, F32)
    dt, ar, th = t("dt", [64, G]), t("ar", [64, G]), t("th", [64, G])
    kar, kth, mag = t("kar", [64, G, NKV]), t("kth", [64, G, NKV]), t("mag", [64, G, NKV])
    r1, r2, sn, cs = t("r1", [64, G, NKV]), t("r2", [64, G, NKV]), t("sn", [64, G, NKV]), t("cs", [64, G, NKV])
    pr, pi_ = t("pr", [64, G, NKV]), t("pi", [64, G, NKV])
    A("dummy" and dt[:], P["ldt"][:], AF.Exp)
    V(vt, ar[:], P["a_re"][:], dt[:], ALU.mult)
    V(vt, th[:], P["a_im"][:], dt[:], ALU.mult)
    kv_b = P["kvec"][:].unsqueeze(1).to_broadcast([64, G, NKV])
    V(vt, kar[:], ar[:].unsqueeze(2).to_broadcast([64, G, NKV]), kv_b, ALU.mult)
    V(vt, kth[:], th[:].unsqueeze(2).to_broadcast([64, G, NKV]), kv_b, ALU.mult)
    A(mag[:], kar[:], AF.Exp)
    for off, dst in ((0.0, sn), (TWO_PI / 4, cs)):
        V(vs, r1[:], kth[:], off, 1.0 / TWO_PI, ALU.add, ALU.mult)
        V(vs, r2[:], r1[:], MAGIC, MAGIC, ALU.add, ALU.subtract)
        V(vs, r1[:], kth[:], off, None, ALU.add)
        V(stt, r1[:], r2[:], -TWO_PI, r1[:], ALU.mult, ALU.add)
        V(vs, r1[:], r1[:], -3.14159265, 3.14159265, ALU.max, ALU.min)
        A(dst[:], r1[:], AF.Sin)
    V(vt, pr[:], mag[:], cs[:], ALU.mult)
    V(vt, pi_[:], mag[:], sn[:], ALU.mult)
    lam_re, lam_im = pr[:, :, 16], pi_[:, :, 16]
    den, rden, nre, t1, t2, cre, cim = (t(n, [64, G]) for n in ("den", "rden", "nre", "t1", "t2", "cre", "cim"))
    V(vt, t1[:], P["a_re"][:], P["a_re"][:], ALU.mult)
    V(vt, den[:], P["a_im"][:], P["a_im"][:], ALU.mult)
    V(vt, den[:], den[:], t1[:], ALU.add)
    V(nc.vector.reciprocal, rden[:], den[:])
    V(vs, nre[:], lam_re, -1.0, None, ALU.add)
    V(vt, t1[:], nre[:], P["a_re"][:], ALU.mult)
    V(vt, t2[:], lam_im, P["a_im"][:], ALU.mult)
    V(vt, t1[:], t1[:], t2[:], ALU.add)
    V(vt, cre[:], t1[:], rden[:], ALU.mult)
    V(vt, t1[:], lam_im, P["a_re"][:], ALU.mult)
    V(vt, t2[:], nre[:], P["a_im"][:], ALU.mult)
    V(vt, t1[:], t1[:], t2[:], ALU.subtract)
    V(vt, cim[:], t1[:], rden[:], ALU.mult)
    bbr, bbi, u1, u2 = (t(n, [64, G, 16]) for n in ("bbr", "bbi", "u1", "u2"))
    cre_b = cre[:].unsqueeze(2).to_broadcast([64, G, 16])
    cim_b = cim[:].unsqueeze(2).to_broadcast([64, G, 16])
    V(vt, u1[:], P["b_re"][:], cre_b, ALU.mult)
    V(vt, u2[:], P["b_im"][:], cim_b, ALU.mult)
    V(vt, bbr[:], u1[:], u2[:], ALU.subtract)
    V(vt, u1[:], P["b_im"][:], cre_b, ALU.mult)
    V(vt, u2[:], P["b_re"][:], cim_b, ALU.mult)
    V(vt, bbi[:], u1[:], u2[:], ALU.add)
    LO, HI = t("LO", [64, G, SW]), t("HI", [64, G, SW])
    w1, w2 = t("w1", [64, G, 8, 16]), t("w2", [64, G, 8, 16])
    shp4 = [64, G, 8, 16]

    def cplx(k0, xr, xi, out_r, out_i, neg_im=False):
        pwr = pr[:, :, k0:k0 + 8].unsqueeze(3).to_broadcast(shp4)
        pwi = pi_[:, :, k0:k0 + 8].unsqueeze(3).to_broadcast(shp4)
        xr_b = xr[:].unsqueeze(2).to_broadcast(shp4)
        xi_b = xi[:].unsqueeze(2).to_broadcast(shp4)
        V(vt, w1[:], pwr, xr_b, ALU.mult)
        V(vt, w2[:], pwi, xi_b, ALU.mult)
        V(vt, out_r, w1[:], w2[:], ALU.subtract)
        V(vt, w1[:], pwr, xi_b, ALU.mult)
        V(vt, w2[:], pwi, xr_b, ALU.mult)
        if neg_im:
            V(vt, w1[:], w1[:], w2[:], ALU.add)
            V(vs, out_i, w1[:], -1.0, None, ALU.mult)
        else:
            V(vt, out_i, w1[:], w2[:], ALU.add)

    v4 = lambda T_, c0: T_[:, :, c0:c0 + 128].rearrange("p g (s m) -> p g s m", m=16)
    cplx(0, bbr, bbi, v4(LO, 0), v4(HI, 0))
    cplx(8, bbr, bbi, v4(LO, 128), v4(HI, 128))
    cplx(16, P["c_re"], P["c_im"], v4(LO, 256), v4(HI, 256), neg_im=True)
    qr, qi, q1, q2 = t("qr", [64, G, NLV]), t("qi", [64, G, NLV]), t("q1", [64, G]), t("q2", [64, G])
    V(nc.vector.tensor_copy, qr[:, :, 0], pr[:, :, 23])
    V(nc.vector.tensor_copy, qi[:, :, 0], pi_[:, :, 23])
    for j in range(NLV - 1):
        V(vt, q1[:], qr[:, :, j], qr[:, :, j], ALU.mult)
        V(vt, q2[:], qi[:, :, j], qi[:, :, j], ALU.mult)
        V(vt, qr[:, :, j + 1], q1[:], q2[:], ALU.subtract)
        V(stt, qi[:, :, j + 1], qr[:, :, j], 2.0, qi[:, :, j], ALU.mult, ALU.mult)
    V(nc.vector.tensor_copy, LO[:, :, 384:384 + NLV], qr[:])
    V(nc.vector.tensor_copy, HI[:, :, 384:384 + NLV], qr[:])
    V(nc.vector.tensor_copy, LO[:, :, 384 + NLV:SW], qi[:])
    V(vs, HI[:, :, 384 + NLV:SW], qi[:], -1.0, None, ALU.mult)
    ev_setup = st.last

    psum = Ring([kb.ps("s5bank", [128, 512], F32) for _ in range(8)])
    stk = [kb.sb("stk", [128, SW], F32) for _ in range(G)]
    BzT = [kb.sb("BzT", [128, 128], F32R) for _ in range(G)]
    TzT = [kb.sb("TzT", [128, 128], F32R) for _ in range(G)]
    CzT = [kb.sb("CzT", [128, 128], F32R) for _ in range(G)]
    AT = [kb.sb("AT", [128, NLV, 128], F32R) for _ in range(G)]
    atmp = Ring([kb.sb("atmp", [128, 128], F32) for _ in range(2)])
    mat_ev = [None] * G
    for g in range(G):
        b = psum.get(kb, "pe")
        kb.wait("pe", ev_setup)
        nc.tensor.matmul(psum.bufs[b][:, 0:SW], P["selLo"][:], LO[:, g, :], start=True, stop=False)
        e = kb.mark("pe", nc.tensor.matmul(psum.bufs[b][:, 0:SW], P["selHi"][:], HI[:, g, :], start=False, stop=True))
        kb.wait("act", e)
        e_stk = kb.mark("act", nc.scalar.copy(stk[g][:], psum.bufs[b][:, 0:SW]))
        psum.release(b, e_stk)
        b = psum.get(kb, "pe")
        kb.wait("pe", e_stk)
        nc.tensor.transpose(psum.bufs[b][:, 0:128], stk[g][:, 0:128], P["ident"][:])
        e = kb.mark("pe", nc.tensor.matmul(psum.bufs[b][:, 128:256], stk[g][:, 128:256], stk[g][:, 256:384], start=True, stop=True))
        kb.wait("act", e)
        e1 = kb.mark("act", nc.scalar.copy(BzT[g][:], psum.bufs[b][:, 0:128]))
        kb.wait("dve", e, e_stk)
        e2 = kb.mark("dve", nc.vector.tensor_tensor(TzT[g][:], psum.bufs[b][:, 128:256], P["tmask"][:], ALU.mult))
        psum.release(b, e1, e2)
        e3 = kb.mark("act", nc.scalar.copy(CzT[g][:], stk[g][:, 256:384]))
        for j in range(NLV):
            ti = atmp.get(kb, "dve")
            e4 = kb.mark("dve", nc.vector.tensor_scalar(atmp.bufs[ti][:], P["ident"][:], stk[g][:, 384 + j:385 + j], None, ALU.mult))
            kb.wait("dve", e4)
            e5 = kb.mark("dve", nc.vector.scalar_tensor_tensor(AT[g][:, j, :], P["swap"][:], stk[g][:, 384 + NLV + j:385 + NLV + j],
                                                               atmp.bufs[ti][:], ALU.mult, ALU.add))
            atmp.release(ti, e5)
        mat_ev[g] = [e1, e2, e3, e5]

    Ubuf = Ring([kb.sb("Ubuf", [128, NCK], F32R) for _ in range(3)])
    Uslot = [DmaSlot(kb, "U") for _ in range(3)]
    Xb = [kb.sb("Xb", [128, 2 + NCK], F32R) for _ in range(2)]
    ystage = Ring([kb.sb("ystage", [128, 512], F32) for _ in range(3)])
    yslot = [DmaSlot(kb, "y") for _ in range(3)]
    x_free = [None, None]
    zt = kb.sb("zt", [128, 2], F32)
    ez = kb.mark("pool", nc.gpsimd.memset(zt[:], 0.0))
    kb.wait("pool", ez)
    for xb in Xb:
        ev_pad = kb.mark("pool", nc.gpsimd.tensor_copy(xb[:, 0:2], zt[:]))
    NB = NCK // 512
    for g0 in range(0, G, 2):
        grp = []
        for gi in range(2):
            g = g0 + gi
            R = St()
            R.g, R.X = g, Xb[gi]
            R.ui = Ubuf.get(kb, "pool")
            R.ev_u = Uslot[R.ui].dma(kb, "pool", Ubuf.bufs[R.ui][:], dr["U"][g])
            R.U = Ubuf.bufs[R.ui]
            grp.append(R)
        for gi, R in enumerate(grp):
            R.ev_x = ev_pad
            evs = []
            for bi in range(NB):
                b = psum.get(kb, "pe")
                kb.wait("pe", R.ev_u, mat_ev[R.g])
                e = kb.mark("pe", nc.tensor.matmul(psum.bufs[b][:], BzT[R.g][:], R.U[:, bi * 512:(bi + 1) * 512], start=True, stop=True))
                kb.wait("act", e, x_free[gi])
                e2 = kb.mark("act", nc.scalar.copy(R.X[:, 2 + bi * 512:2 + (bi + 1) * 512], psum.bufs[b][:]))
                psum.release(b, e2)
                evs.append(e2)
            R.ev_x = [R.ev_x] + evs
        for j in range(NLV):
            d = 1 << j
            for gi, R in enumerate(grp):
                jobs = []
                for bi in range(NB):
                    lo = max(bi * 512, d) if d > 1 else bi * 512
                    hi = (bi + 1) * 512
                    if lo >= hi:
                        continue
                    b = psum.get(kb, "pe")
                    kb.wait("pe", R.ev_x)
                    e = kb.mark("pe", nc.tensor.matmul(psum.bufs[b][:, lo - bi * 512:512], AT[R.g][:, j, :],
                                                       R.X[:, 2 + lo - d:2 + hi - d], start=True, stop=True))
                    jobs.append((b, lo, hi, bi))
                evs = []
                kb.wait("dve", e, R.ev_x)
                for (b, lo, hi, bi) in jobs:
                    e2 = kb.mark("dve", nc.vector.tensor_tensor(R.X[:, 2 + lo:2 + hi], psum.bufs[b][:, lo - bi * 512:512],
                                                                R.X[:, 2 + lo:2 + hi], ALU.add))
                    psum.release(b, e2)
                    evs.append(e2)
                R.ev_x = evs
        for gi, R in enumerate(grp):
            for bi in range(NB):
                b = psum.get(kb, "pe")
                kb.wait("pe", R.ev_x)
                nc.tensor.matmul(psum.bufs[b][:], TzT[R.g][:], R.U[:, bi * 512:(bi + 1) * 512], start=True, stop=False)
                e = kb.mark("pe", nc.tensor.matmul(psum.bufs[b][:], CzT[R.g][:], R.X[:, 1 + bi * 512:1 + (bi + 1) * 512],
                                                   start=False, stop=True))
                yi = ystage.get(kb, "act")
                kb.wait("act", e)
                e2 = kb.mark("act", nc.scalar.copy(ystage.bufs[yi][:], psum.bufs[b][:]))
                psum.release(b, e2)
                kb.wait("sp", e2)
                ev = yslot[yi].dma(kb, "sp", Y[R.g, :, bi * 512:(bi + 1) * 512], ystage.bufs[yi][:])
                ystage.release(yi, ev)
            Ubuf.release(R.ui, e)
            x_free[gi] = e
    for sl in yslot:
        kb.E["sp"].wait_ge(sl.sem, sl.cnt)


def s5_consts():
    kvec = np.array([7, 6, 5, 4, 3, 2, 1, 0, -1, -2, -3, -4, -5, -6, -7, -8, 1, 2, 3, 4, 5, 6, 7, 8], np.float32)
    c = {}
    c["kvec"] = np.tile(kvec[None, :], (64, 1))
    c["selLo"] = np.concatenate([np.eye(64, dtype=np.float32), np.zeros((64, 64), np.float32)], axis=1)
    c["selHi"] = np.concatenate([np.zeros((64, 64), np.float32), np.eye(64, dtype=np.float32)], axis=1)
    c["ident"] = np.eye(128, dtype=np.float32)
    sw = np.zeros((128, 128), np.float32)
    sw[np.arange(64), np.arange(64) + 64] = 1.0
    sw[np.arange(64) + 64, np.arange(64)] = 1.0
    c["swap"] = sw
    i = np.arange(128)
    c["tmask"] = (i[None, :] // 16 >= i[:, None] // 16).astype(np.float32)
    return c


def s5_in_maps(inp, u):
    Uall = np.ascontiguousarray(u.reshape(NCK, 8, 64, 16).transpose(2, 1, 3, 0)).reshape(64, 128, NCK)
    cst = s5_consts()
    maps = []
    for c in range(NCORES):
        gs = slice(c * GPC, (c + 1) * GPC)
        m = dict(cst)
        m["U"] = np.ascontiguousarray(Uall[gs])
        m["a_re"] = np.ascontiguousarray(inp["s5_a_re"][0][gs].T)
        m["a_im"] = np.ascontiguousarray(inp["s5_a_im"][0][gs].T)
        m["ldt"] = np.ascontiguousarray(np.tile(inp["s5_log_dt"][0][gs][None, :], (64, 1)))
        m["b_re"] = np.ascontiguousarray(inp["s5_b_re"][0][gs].transpose(1, 0, 2))
        m["b_im"] = np.ascontiguousarray(inp["s5_b_im"][0][gs].transpose(1, 0, 2))
        m["c_re"] = np.ascontiguousarray(inp["s5_c_re"][0][gs].transpose(2, 0, 1))
        m["c_im"] = np.ascontiguousarray(inp["s5_c_im"][0][gs].transpose(2, 0, 1))
        maps.append(m)
    return maps


def run_phase_b(inp, u):
    nc = build_phase_b()
    res = _run(nc, s5_in_maps(inp, u))
    Yall = np.concatenate([r["Y"] for r in res.results], axis=0)
    return np.ascontiguousarray(Yall.reshape(64, 8, 16, NCK).transpose(3, 1, 0, 2)).reshape(SEQ, 1024)


C_GAINS = 8 * 5 + 8


def build_phase_c():
    nc = bass.Bass("TRN2", target_bir_lowering=False)
    I = lambda n, shp: nc.dram_tensor(n, shp, F32, kind="ExternalInput").ap()
    O = lambda n, shp: nc.dram_tensor(n, shp, F32, kind="ExternalOutput").ap()
    h1T, uT, yT = I("h1T", [D, TOK]), I("uT", [D, TOK]), I("yT", [D, TOK])
    gains = I("gains", [128, C_GAINS])
    wglu, wout = I("wglu", [KT, 128, KT, 128]), I("wout", [KT, 128, KT, 128])
    ffw = [(I(f"wg{i}", [FT, 128, KT, 128]), I(f"wu{i}", [FT, 128, KT, 128]), I(f"wd{i}", [2, KT, 128, FH, 128])) for i in range(2)]
    wqk = I("wqk", [16, 128, KT, 128])
    wv = I("wv", [128, KT, D])
    pos = I("pos", [32, TOK])
    rmT = I("rmT", [32, 32])
    h4T = O("h4T", [D, TOK])
    qkT = O("qkT", [16, 128, TOK])
    v_o = O("v", [TOK, D])
    with ExitStack() as es:
        kb = KB(nc, es)
        S = tok_state(kb, C_GAINS)
        S.ev_gains = S.misc.dma(kb, "sp", S.gains[:], gains)
        load_hT(kb, S, h1T)
        s5_post(kb, S, uT, yT, wglu, wout, gcol_d=0)
        ffn(kb, S, *ffw[0], 8)
        ffn(kb, S, *ffw[1], 16)
        rmsnorm(kb, S, 24)
        store_hT(kb, S, h4T)
        qkv(kb, S, wqk, wv, pos, rmT, qkT, v_o, gcol=32)
        finish(kb, S)
    return nc


def s5_post(kb, S, uT, yT, wglu, wout, gcol_d):
    nc = kb.nc
    yu = S.act[:, 8:11, :].bitcast(F32)
    bufs = [yu[:, i, j * CH:(j + 1) * CH] for i in range(3) for j in range(2)]
    ring = Ring([(bufs[2 * i], bufs[2 * i + 1]) for i in range(3)])
    slots = [DmaSlot(kb, "yu") for _ in range(3)]
    g_ev = [None] * NCH
    for c in range(NCH):
        cs = slice(c * CH, (c + 1) * CH)
        for f in range(KT):
            i = ring.get(kb, "sp")
            kb.wait("sp", S.act_free)
            yb, ub = ring.bufs[i]
            slots[i].dma(kb, "sp", yb, yT[f * 128:(f + 1) * 128, cs])
            ev = slots[i].dma(kb, "sp", ub, uT[f * 128:(f + 1) * 128, cs])
            kb.wait("dve", ev, S.ev_gains)
            e1 = kb.mark("dve", nc.vector.scalar_tensor_tensor(yb, ub, S.gains[:, gcol_d + f:gcol_d + f + 1], yb, ALU.mult, ALU.add))
            kb.wait("act", e1, S.hn_free)
            e2 = kb.mark("act", nc.scalar.activation(S.hn[:, f, cs], yb, AF.Gelu_apprx_tanh))
            ring.release(i, e2)
        g_ev[c] = e2
    g2 = S.act[:, 0:KT, :]
    g2_ev = [None] * NCH

    def evac_glu(ot, c, bank, ev_mm):
        cs = slice(c * CH, (c + 1) * CH)
        ti = S.sg.get(kb, "act")
        kb.wait("act", ev_mm)
        e1 = kb.mark("act", nc.scalar.activation(S.sg.bufs[ti][:], bank[:], AF.Sigmoid))
        kb.wait("dve", e1, S.act_free)
        e2 = kb.mark("dve", nc.vector.tensor_tensor(g2[:, ot, cs], S.sg.bufs[ti][:], S.hn[:, ot, cs], ALU.mult))
        S.sg.release(ti, e2)
        g2_ev[c] = e2 if ot == KT - 1 else g2_ev[c]
        return e1
    proj_fm(kb, S, wglu, KT, evac_glu, opnd=S.hn, opnd_ev=g_ev)

    def evac_out(ot, c, bank, ev_mm):
        cs = slice(c * CH, (c + 1) * CH)
        kb.wait("dve", ev_mm, S.h_ev[c])
        e = kb.mark("dve", nc.vector.tensor_tensor(S.hT[:, ot, cs], bank[:], S.hT[:, ot, cs], ALU.add))
        S.h_ev[c] = e
        return e
    proj_fm(kb, S, wout, KT, evac_out, opnd=g2, opnd_ev=g2_ev)


def qkv(kb, S, wqk, wv, pos, rmT, qkT, v_o, gcol):
    nc = kb.nc
    st = St()
    st.last = S.ev_gains

    def V(fn, *a, **k):
        kb.wait("dve", st.last)
        st.last = kb.mark("dve", fn(*a, **k))

    def A(*a, **k):
        kb.wait("act", st.last)
        st.last = kb.mark("act", nc.scalar.activation(*a, **k))
    vt, vs, stt = nc.vector.tensor_tensor, nc.vector.tensor_scalar, nc.vector.scalar_tensor_tensor
    kb.wait("sp", S.act_free)
    tabs = S.act[0:32, 4:8, :].bitcast(F32)
    Ct = tabs[:, 0:2, :].rearrange("p a b -> p (a b)")
    Sn = tabs[:, 2:4, :].rearrange("p a b -> p (a b)")
    wk = S.act[0:32, 8:11, :].bitcast(F32)
    ang = wk[:, 0:2, :].rearrange("p a b -> p (a b)")
    r1 = kb.sb("rope_r1", [32, TOK], F32)
    rm = kb.sb("rmT", [32, 32], F32R)
    gq = kb.sb("gq", [128, 2], F32)
    ev_pos = S.misc.dma(kb, "sp", r1[:], pos)
    ev_rm = S.misc.dma(kb, "pool", rm[:], rmT)
    st.last = [st.last, ev_pos, S.act_free]
    invf = S.gains[0:32, gcol + 2:gcol + 3]
    V(vs, ang, r1[:], invf, None, ALU.mult)
    for off, dst in ((0.0, Sn), (TWO_PI / 4, Ct)):
        V(vs, r1[:], ang, off, 1.0 / TWO_PI, ALU.add, ALU.mult)
        V(vs, r1[:], r1[:], MAGIC, MAGIC, ALU.add, ALU.subtract)
        V(vs, dst, ang, off, None, ALU.add)
        V(stt, dst, r1[:], -TWO_PI, dst, ALU.mult, ALU.add)
        V(vs, dst, dst, -3.14159265, 3.14159265, ALU.max, ALU.min)
        A(dst, dst, AF.Sin)
    V(vs, gq[:, 0:1], S.gains[:, gcol:gcol + 1], 128.0 ** -0.5, None, ALU.mult)
    V(nc.vector.tensor_copy, gq[:, 1:2], S.gains[:, gcol + 1:gcol + 2])
    ev_tab = st.last
    qn_ring = Ring([kb.sb("qn", [128, CH], F32R) for _ in range(2)])
    t1_ring = Ring([kb.sb("ropet", [32, CH], F32) for _ in range(2)])

    def evac_qk(ot, c, bank, ev_mm):
        cs = slice(c * CH, (c + 1) * CH)
        which = 0 if ot < 8 else 1
        i = S.sq.get(kb, "act")
        kb.wait("act", ev_mm)
        e = kb.mark("act", nc.scalar.activation(S.sq.bufs[i][:], bank[:], AF.Square))
        b2 = S.psum.get(kb, "pe")
        kb.wait("pe", e)
        e2 = kb.mark("pe", nc.tensor.matmul(S.psum.bufs[b2][:], S.ones[:], S.sq.bufs[i][:], start=True, stop=True))
        S.sq.release(i, e2)
        ri = S.rt.get(kb, "act")
        kb.wait("act", e2)
        e3 = kb.mark("act", nc.scalar.activation(S.rt.bufs[ri][:], S.psum.bufs[b2][:], AF.Sqrt, bias=EPS, scale=1.0 / 128))
        S.psum.release(b2, e3)
        si = S.rstd.get(kb, "dve")
        kb.wait("dve", e3)
        e4 = kb.mark("dve", nc.vector.reciprocal(S.rstd.bufs[si][:], S.rt.bufs[ri][:]))
        S.rt.release(ri, e4)
        qi = qn_ring.get(kb, "dve")
        qn = qn_ring.bufs[qi]
        kb.wait("dve", e4, ev_mm, ev_tab)
        e5 = kb.mark("dve", nc.vector.scalar_tensor_tensor(qn[:], bank[:], gq[:, which:which + 1], S.rstd.bufs[si][:],
                                                           ALU.mult, ALU.mult))
        S.rstd.release(si, e5)
        b3 = S.psum.get(kb, "pe")
        kb.wait("pe", e5, ev_rm)
        e6 = kb.mark("pe", nc.tensor.matmul(S.psum.bufs[b3][0:32, :], rm[:], qn[0:32, :], start=True, stop=True))
        oi = S.stage.get(kb, "act")
        ob = S.stage.bufs[oi]
        kb.wait("act", e5)
        e7 = kb.mark("act", nc.scalar.copy(ob[:], qn[:]))
        ti = t1_ring.get(kb, "pool")
        kb.wait("pool", e5, ev_tab)
        e8 = kb.mark("pool", nc.gpsimd.tensor_tensor(t1_ring.bufs[ti][:], qn[0:32, :], Ct[:, cs], ALU.mult))
        kb.wait("dve", e6, e7, e8)
        e9 = kb.mark("dve", nc.vector.tensor_tensor(ob[0:32, :], S.psum.bufs[b3][0:32, :], Sn[:, cs], ALU.mult))
        kb.wait("dve", e9)
        e10 = kb.mark("dve", nc.vector.tensor_tensor(ob[0:32, :], ob[0:32, :], t1_ring.bufs[ti][:], ALU.add))
        S.psum.release(b3, e9)
        t1_ring.release(ti, e10)
        qn_ring.release(qi, e6, e7, e8)
        kb.wait("sp", e10, e7)
        ev = S.stage_slot[oi].dma(kb, "sp", qkT[ot, :, cs], ob[:])
        S.stage.release(oi, ev)
        return e5
    proj_fm(kb, S, wqk, 16, evac_qk)
    wvt = S.act[:, 0:4, :].rearrange("p a (b c) -> p (a b) c", c=D)
    kb.wait("pool", S.act_free)
    ev_wv = S.misc.dma(kb, "pool", wvt, wv)
    for tt in range(TOK // 128):
        c = tt * 128 // CH
        for half in range(2):
            b = S.psum.get(kb, "pe")
            kb.wait("pe", ev_wv, S.hn_ev[c])
            for k in range(KT):
                ins = nc.tensor.matmul(S.psum.bufs[b][:], S.hn[:, k, tt * 128:(tt + 1) * 128], wvt[:, k, half * 512:(half + 1) * 512],
                                       start=(k == 0), stop=(k == KT - 1))
            e = kb.mark("pe", ins)
            S.hn_free = e
            oi = S.stage.get(kb, "act")
            kb.wait("act", e)
            e2 = kb.mark("act", nc.scalar.copy(S.stage.bufs[oi][:], S.psum.bufs[b][:]))
            S.psum.release(b, e2)
            kb.wait("sp", e2)
            ev = S.stage_slot[oi].dma(kb, "sp", v_o[tt * 128:(tt + 1) * 128, half * 512:(half + 1) * 512], S.stage.bufs[oi][:])
            S.stage.release(oi, ev)
    S.act_free = e


def rope_consts():
    inv_freq = (500000.0 ** (-np.arange(0, 32, 2, dtype=np.float32) / 32)).astype(np.float32)
    invf = np.concatenate([inv_freq, inv_freq])
    rmT = np.zeros((32, 32), np.float32)
    for i in range(16):
        rmT[16 + i, i] = -1.0
        rmT[i, 16 + i] = 1.0
    return invf, rmT


def run_phase_c(inp, h1, u, yscan):
    nc = build_phase_c()
    invf, rmT = rope_consts()
    gains = np.zeros((128, C_GAINS), np.float32)
    gains[:, 0:8] = col_layout(inp["s5_d"][0])
    gains[:, 8:16] = col_layout(inp["ffn_norm"][0, 1])
    gains[:, 16:24] = col_layout(inp["ffn_norm"][1, 0])
    gains[:, 24:32] = col_layout(inp["mix_norm"][1])
    gains[:, 32] = inp["moba_q_norm"][0]
    gains[:, 33] = inp["moba_k_norm"][0]
    gains[0:32, 34] = invf
    wqkv = inp["moba_w_qkv"][0]
    common = dict(gains=gains, wglu=tile_w(inp["s5_w_glu"][0]), wout=tile_w(inp["s5_w_out"][0]),
                  wg0=tile_w(inp["ffn_w_gate"][0, 1]), wu0=tile_w(inp["ffn_w_up"][0, 1]), wd0=tile_wd(inp["ffn_w_down"][0, 1]),
                  wg1=tile_w(inp["ffn_w_gate"][1, 0]), wu1=tile_w(inp["ffn_w_up"][1, 0]), wd1=tile_wd(inp["ffn_w_down"][1, 0]),
                  wqk=tile_w(wqkv[:, 0:2048]),
                  wv=np.ascontiguousarray(wqkv[:, 2048:3072].reshape(KT, 128, D).transpose(1, 0, 2)), rmT=rmT)
    maps = []
    for c in range(NCORES):
        ts = slice(c * TOK, (c + 1) * TOK)
        m = dict(common)
        m["h1T"] = np.ascontiguousarray(h1[ts].T)
        m["uT"] = np.ascontiguousarray(u[ts].T)
        m["yT"] = np.ascontiguousarray(yscan[ts].T)
        m["pos"] = np.ascontiguousarray(np.tile(np.arange(c * TOK, (c + 1) * TOK, dtype=np.float32)[None, :], (32, 1)))
        maps.append(m)
    res = _run(nc, maps)
    h4 = np.concatenate([r["h4T"].T for r in res.results], axis=0)
    qk = np.concatenate([r["qkT"] for r in res.results], axis=2)
    v = np.concatenate([r["v"] for r in res.results], axis=0)
    q = np.ascontiguousarray(qk[0:8].transpose(2, 0, 1))
    k = np.ascontiguousarray(qk[8:16].transpose(2, 0, 1))
    return h4, q, k, v.reshape(SEQ, 8, 128)


def build_phase_e():
    nc = bass.Bass("TRN2", target_bir_lowering=False)
    I = lambda n, shp: nc.dram_tensor(n, shp, F32, kind="ExternalInput").ap()
    h4T, oT = I("h4T", [D, TOK]), I("oT", [D, TOK])
    gains = I("gains", [128, 8])
    wo = I("wo", [KT, 128, KT, 128])
    wg, wu, wd = I("wg", [FT, 128, KT, 128]), I("wu", [FT, 128, KT, 128]), I("wd", [2, KT, 128, FH, 128])
    outT = nc.dram_tensor("outT", [D, TOK], F32, kind="ExternalOutput").ap()
    with ExitStack() as es:
        kb = KB(nc, es)
        S = tok_state(kb, 8)
        S.ev_gains = S.misc.dma(kb, "sp", S.gains[:], gains)
        load_hT(kb, S, h4T)
        attn_out(kb, S, oT, wo)
        ffn(kb, S, wg, wu, wd, 0)
        store_hT(kb, S, outT)
        finish(kb, S)
    return nc


def attn_out(kb, S, oT, wo):
    nc = kb.nc
    oslot = DmaSlot(kb, "oT")
    kb.wait("pool", S.hn_free)
    for f in range(KT):
        ev = oslot.dma(kb, "pool", S.hn[:, f, :], oT[f * 128:(f + 1) * 128, :])
    o_ev = [ev] * NCH

    def evac_out(ot, c, bank, ev_mm):
        cs = slice(c * CH, (c + 1) * CH)
        kb.wait("dve", ev_mm, S.h_ev[c])
        e = kb.mark("dve", nc.vector.tensor_tensor(S.hT[:, ot, cs], bank[:], S.hT[:, ot, cs], ALU.add))
        S.h_ev[c] = e
        return e
    proj_fm(kb, S, wo, KT, evac_out, opnd=S.hn, opnd_ev=o_ev)


def run_phase_e(inp, h4, o):
    nc = build_phase_e()
    common = dict(gains=col_layout(inp["ffn_norm"][1, 1]), wo=tile_w(inp["moba_w_out"][0]),
                  wg=tile_w(inp["ffn_w_gate"][1, 1]), wu=tile_w(inp["ffn_w_up"][1, 1]), wd=tile_wd(inp["ffn_w_down"][1, 1]))
    maps = []
    for c in range(NCORES):
        ts = slice(c * TOK, (c + 1) * TOK)
        m = dict(common)
        m["h4T"] = np.ascontiguousarray(h4[ts].T)
        m["oT"] = np.ascontiguousarray(o[ts].T)
        maps.append(m)
    res = _run(nc, maps)
    return np.concatenate([r["outT"].T for r in res.results], axis=0)


def kernel(**inp):
    inp = {k: np.asarray(v, dtype=np.float32) for k, v in inp.items()}
    h1, u = run_phase_a(inp)
    ysc = run_phase_b(inp, u)
    h4, q, k, v = run_phase_c(inp, h1, u, ysc)
    o = run_phase_d(q, k, v)
    out = run_phase_e(inp, h4, o)
    return out.reshape(1, SEQ, D).astype(np.float32)
```

```python
import numpy as np
import concourse.bass as bass
import concourse.mybir as mybir
from concourse.bass_utils import run_bass_kernel_spmd
from contextlib import ExitStack

F32 = mybir.dt.float32
BF16 = mybir.dt.bfloat16
F32R = mybir.dt.float32r
AF = mybir.ActivationFunctionType
ALU = mybir.AluOpType
AX = mybir.AxisListType

NCORES = 8
SEQ = 16384
D = 1024
DFF = 2816
TOK = SEQ // NCORES
CH = 512
NCH = TOK // CH
KT = D // 128
FT = DFF // 128
FH = FT // 2
EPS = 1e-6


class KB:
    uid = [0]

    def __init__(self, nc, es):
        self.nc, self.es = nc, es
        self.E = dict(pe=nc.tensor, act=nc.scalar, dve=nc.vector, pool=nc.gpsimd, sp=nc.sync)
        self.psem = {}
        self.pcnt = {}
        self.waited = {}
        self.nsem = 0
        self.ntile = 0
        self.slots = []

    def newsem(self, name):
        self.nsem += 1
        KB.uid[0] += 1
        return self.es.enter_context(self.nc.semaphore(f"{name}_{KB.uid[0]}"))

    def sb(self, name, shape, dt=F32):
        KB.uid[0] += 1
        return self.es.enter_context(self.nc.sbuf_tensor(f"{name}_{KB.uid[0]}", list(shape), dt))

    def ps(self, name, shape, dt=F32):
        KB.uid[0] += 1
        return self.es.enter_context(self.nc.psum_tensor(f"{name}_{KB.uid[0]}", list(shape), dt))

    def mark(self, e, ins):
        if e not in self.psem:
            self.psem[e] = self.newsem("p_" + e)
            self.pcnt[e] = 0
        self.pcnt[e] += 1
        ins.then_inc(self.psem[e], 1)
        return (self.psem[e], self.pcnt[e], e)

    def wait(self, e, *evs):
        for ev in evs:
            if ev is None:
                continue
            if isinstance(ev, list):
                self.wait(e, *ev)
                continue
            sem, val, _src = ev
            key = (e, sem.name if hasattr(sem, "name") else id(sem))
            if self.waited.get(key, 0) >= val:
                continue
            self.waited[key] = val
            self.E[e].wait_ge(sem, val)


def finish_all(kb, eng):
    for sl in kb.slots:
        if sl.cnt:
            kb.E[eng].wait_ge(sl.sem, sl.cnt)


class DmaSlot:
    def __init__(self, kb, name):
        self.sem = kb.newsem("d_" + name)
        self.cnt = 0
        kb.slots.append(self)

    def dma(self, kb, q, out, in_):
        kb.E[q].dma_start(out=out, in_=in_).then_inc(self.sem, 16)
        self.cnt += 16
        return (self.sem, self.cnt, "dma")


class Ring:
    def __init__(self, bufs):
        self.bufs = bufs
        self.rel = [[] for _ in bufs]
        self.i = 0

    def get(self, kb, eng):
        i = self.i
        self.i = (i + 1) % len(self.bufs)
        kb.wait(eng, self.rel[i])
        self.rel[i] = []
        return i

    def release(self, i, *evs):
        self.rel[i].extend(evs)


class St:
    pass


def tile_w(w):
    K, N = w.shape
    return np.ascontiguousarray(w.reshape(K // 128, 128, N // 128, 128).transpose(2, 1, 0, 3))


def tile_wd(w):
    a = w.reshape(2, FH, 128, KT, 128)
    return np.ascontiguousarray(a.transpose(0, 3, 2, 1, 4))


def col_layout(v):
    return np.ascontiguousarray(v.reshape(-1, 128).T)


def tok_state(kb, n_gain_cols):
    nc = kb.nc
    S = St()
    S.hT = kb.sb("hT", [128, KT, TOK], F32)
    S.hn = kb.sb("hn", [128, KT, TOK], BF16)
    S.act = kb.sb("act", [128, FH, TOK], BF16)
    S.ones = kb.sb("ones", [128, 128], BF16)
    S.gains = kb.sb("gains", [128, n_gain_cols], F32)
    S.psum = Ring([kb.ps("bank", [128, CH], F32) for _ in range(8)])
    S.wgu = Ring([(kb.sb("wg", [128, KT, 128], BF16), kb.sb("wu", [128, KT, 128], BF16)) for _ in range(3)])
    S.wgu_slot = [DmaSlot(kb, "wgu") for _ in range(3)]
    S.wd = Ring([kb.sb("wd", [128, FH, 128], BF16) for _ in range(3)])
    S.wd_slot = [DmaSlot(kb, "wd") for _ in range(3)]
    S.sq = Ring([kb.sb("sq", [128, CH], BF16) for _ in range(3)])
    S.rt = Ring([kb.sb("rt", [128, CH], F32) for _ in range(2)])
    S.rstd = Ring([kb.sb("rstd", [128, CH], F32) for _ in range(2)])
    S.sg = Ring([kb.sb("sg", [128, CH], BF16) for _ in range(3)])
    S.stage = Ring([kb.sb("stage", [128, CH], F32) for _ in range(3)])
    S.stage_slot = [DmaSlot(kb, "stage") for _ in range(3)]
    S.h_ev = [None] * NCH
    S.hn_ev = [None] * NCH
    S.hn_free = None
    S.act_free = None
    S.misc = DmaSlot(kb, "misc")
    S.ev_ones = kb.mark("pool", nc.gpsimd.memset(S.ones[:], 1.0))
    return S


def rmsnorm(kb, S, gcol0):
    nc = kb.nc
    for c in range(NCH):
        cs = slice(c * CH, (c + 1) * CH)
        b = S.psum.get(kb, "pe")
        bank = S.psum.bufs[b]
        for f in range(KT):
            i = S.sq.get(kb, "act")
            kb.wait("act", S.h_ev[c])
            e = kb.mark("act", nc.scalar.activation(S.sq.bufs[i][:], S.hT[:, f, cs], AF.Square))
            kb.wait("pe", e, S.ev_ones)
            e2 = kb.mark("pe", nc.tensor.matmul(bank[:], S.ones[:], S.sq.bufs[i][:], start=(f == 0), stop=(f == KT - 1)))
            S.sq.release(i, e2)
        ri = S.rt.get(kb, "act")
        kb.wait("act", e2)
        e3 = kb.mark("act", nc.scalar.activation(S.rt.bufs[ri][:], bank[:], AF.Sqrt, bias=EPS, scale=1.0 / D))
        S.psum.release(b, e3)
        si = S.rstd.get(kb, "dve")
        kb.wait("dve", e3)
        e4 = kb.mark("dve", nc.vector.reciprocal(S.rstd.bufs[si][:], S.rt.bufs[ri][:]))
        S.rt.release(ri, e4)
        kb.wait("dve", S.hn_free, S.h_ev[c], e4, S.ev_gains)
        for f in range(KT):
            ins = nc.vector.scalar_tensor_tensor(S.hn[:, f, cs], S.hT[:, f, cs], S.gains[:, gcol0 + f:gcol0 + f + 1],
                                                 S.rstd.bufs[si][:], ALU.mult, ALU.mult)
        e5 = kb.mark("dve", ins)
        S.rstd.release(si, e5)
        S.hn_ev[c] = e5


def load_wtile(kb, S, ring, slots, pairs):
    i = ring.get(kb, "pool")
    ev = None
    bufs = ring.bufs[i]
    if not isinstance(bufs, tuple):
        bufs = (bufs,)
    for buf, src in zip(bufs, pairs):
        ev = slots[i].dma(kb, "pool", buf[:], src)
    return i, ev


def ffn(kb, S, wg, wu, wd, gcol0):
    nc = kb.nc
    rmsnorm(kb, S, gcol0)
    for half in range(2):
        act_ev = [[None] * NCH for _ in range(FH)]
        for fl in range(FH):
            ft = half * FH + fl
            wi, ev_w = load_wtile(kb, S, S.wgu, S.wgu_slot, (wg[ft], wu[ft]))
            wgt, wut = S.wgu.bufs[wi]
            for cp in range(2):
                banks = [S.psum.get(kb, "pe") for _ in range(4)]
                kb.wait("pe", ev_w)
                for mi, w in enumerate((wgt, wut)):
                    for k in range(KT):
                        for cc in range(2):
                            c = cp * 2 + cc
                            kb.wait("pe", S.hn_ev[c])
                            ins = nc.tensor.matmul(S.psum.bufs[banks[mi * 2 + cc]][:], w[:, k, :],
                                                   S.hn[:, k, c * CH:(c + 1) * CH], start=(k == 0), stop=(k == KT - 1))
                ev_mm = kb.mark("pe", ins)
                S.hn_free = ev_mm
                if cp == 1:
                    S.wgu.release(wi, ev_mm)
                for cc in range(2):
                    c = cp * 2 + cc
                    ti = S.sg.get(kb, "act")
                    kb.wait("act", ev_mm)
                    e1 = kb.mark("act", nc.scalar.activation(S.sg.bufs[ti][:], S.psum.bufs[banks[cc]][:], AF.Silu))
                    kb.wait("dve", e1, ev_mm, S.act_free)
                    e2 = kb.mark("dve", nc.vector.tensor_tensor(S.act[:, fl, c * CH:(c + 1) * CH], S.sg.bufs[ti][:],
                                                                S.psum.bufs[banks[2 + cc]][:], ALU.mult))
                    S.sg.release(ti, e2)
                    S.psum.release(banks[cc], e1)
                    S.psum.release(banks[2 + cc], e2)
                    act_ev[fl][c] = e2
        for dm in range(KT):
            wi, ev_w = load_wtile(kb, S, S.wd, S.wd_slot, (wd[half, dm],))
            wdt = S.wd.bufs[wi]
            for c in range(NCH):
                cs = slice(c * CH, (c + 1) * CH)
                b = S.psum.get(kb, "pe")
                kb.wait("pe", ev_w)
                for t in range(FH):
                    kb.wait("pe", act_ev[t][c])
                    ins = nc.tensor.matmul(S.psum.bufs[b][:], wdt[:, t, :], S.act[:, t, cs], start=(t == 0), stop=(t == FH - 1))
                ev = kb.mark("pe", ins)
                kb.wait("dve", ev, S.h_ev[c])
                e3 = kb.mark("dve", nc.vector.scalar_tensor_tensor(S.hT[:, dm, cs], S.psum.bufs[b][:], 0.5, S.hT[:, dm, cs],
                                                                   ALU.mult, ALU.add))
                S.psum.release(b, e3)
                S.h_ev[c] = e3
            S.wd.release(wi, ev)
            S.act_free = ev


def proj_fm(kb, S, wt, n_tiles, evac, opnd=None, opnd_ev=None, t0=0, defer=False, ring=None):
    nc = kb.nc
    if opnd is None:
        opnd, opnd_ev = S.hn, S.hn_ev
    pend = []
    ring = ring or S.psum
    for ot in range(t0, t0 + n_tiles):
        wi, ev_w = load_wtile(kb, S, S.wgu, S.wgu_slot, (wt[ot],))
        w = S.wgu.bufs[wi][0]
        for cp in range(2):
            banks = [ring.get(kb, "pe") for _ in range(2)]
            kb.wait("pe", ev_w)
            for k in range(KT):
                for cc in range(2):
                    c = cp * 2 + cc
                    kb.wait("pe", opnd_ev[c])
                    ins = nc.tensor.matmul(ring.bufs[banks[cc]][:], w[:, k, :], opnd[:, k, c * CH:(c + 1) * CH],
                                           start=(k == 0), stop=(k == KT - 1))
            ev_mm = kb.mark("pe", ins)
            if opnd is S.hn:
                S.hn_free = ev_mm
            else:
                S.act_free = ev_mm
            if cp == 1:
                S.wgu.release(wi, ev_mm)
            pend.append((ot, cp, banks, ev_mm))
            while len(pend) > (1 if defer else 0):
                pot, pcp, pbanks, pev = pend.pop(0)
                for cc in range(2):
                    rel = evac(pot, pcp * 2 + cc, ring.bufs[pbanks[cc]], pev)
                    ring.release(pbanks[cc], rel)
    while pend:
        pot, pcp, pbanks, pev = pend.pop(0)
        for cc in range(2):
            rel = evac(pot, pcp * 2 + cc, ring.bufs[pbanks[cc]], pev)
            ring.release(pbanks[cc], rel)


def store_fm(kb, S, out_dram, eng="act"):
    nc = kb.nc

    def evac(ot, c, bank, ev_mm):
        i = S.stage.get(kb, eng)
        kb.wait(eng, ev_mm)
        if eng == "act":
            e = kb.mark("act", nc.scalar.copy(S.stage.bufs[i][:], bank[:]))
        else:
            e = kb.mark("dve", nc.vector.tensor_copy(S.stage.bufs[i][:], bank[:]))
        kb.wait("sp", e)
        ev = S.stage_slot[i].dma(kb, "sp", out_dram[ot * 128:(ot + 1) * 128, c * CH:(c + 1) * CH], S.stage.bufs[i][:])
        S.stage.release(i, ev)
        return e
    return evac


def load_hT(kb, S, xT):
    evs = []
    for f in range(KT):
        evs.append(S.misc.dma(kb, "sp", S.hT[:, f, :], xT[f * 128:(f + 1) * 128, :]))
    for c in range(NCH):
        S.h_ev[c] = evs[-1]


def store_hT(kb, S, outT):
    kb.wait("sp", [e for e in S.h_ev])
    evs = []
    for f in range(KT):
        evs.append(S.misc.dma(kb, "sp", outT[f * 128:(f + 1) * 128, :], S.hT[:, f, :]))
    return evs[-1]


def finish(kb, S):
    for sl in S.stage_slot + [S.misc]:
        if sl.cnt:
            kb.E["sp"].wait_ge(sl.sem, sl.cnt)


def build_phase_a():
    nc = bass.Bass("TRN2", target_bir_lowering=False)
    xT = nc.dram_tensor("xT", [D, TOK], F32, kind="ExternalInput").ap()
    gains = nc.dram_tensor("gains", [128, 16], F32, kind="ExternalInput").ap()
    wg = nc.dram_tensor("wg", [FT, 128, KT, 128], F32, kind="ExternalInput").ap()
    wu = nc.dram_tensor("wu", [FT, 128, KT, 128], F32, kind="ExternalInput").ap()
    wd = nc.dram_tensor("wd", [2, KT, 128, FH, 128], F32, kind="ExternalInput").ap()
    win = nc.dram_tensor("win", [KT, 128, KT, 128], F32, kind="ExternalInput").ap()
    h1T = nc.dram_tensor("h1T", [D, TOK], F32, kind="ExternalOutput").ap()
    uT = nc.dram_tensor("uT", [D, TOK], F32, kind="ExternalOutput").ap()
    with ExitStack() as es:
        kb = KB(nc, es)
        S = tok_state(kb, 16)
        S.ev_gains = S.misc.dma(kb, "sp", S.gains[:], gains)
        load_hT(kb, S, xT)
        ffn(kb, S, wg, wu, wd, 0)
        store_hT(kb, S, h1T)
        rmsnorm(kb, S, 8)
        proj_fm(kb, S, win, KT, store_fm(kb, S, uT))
        finish(kb, S)
    return nc


TRACE = False
LAST_NS = []


def _run(nc, in_maps):
    res = run_bass_kernel_spmd(nc, in_maps, core_ids=list(range(NCORES)), trace=TRACE)
    if TRACE:
        LAST_NS.append(res.exec_time_ns)
    return res


def run_phase_a(inp):
    nc = build_phase_a()
    x = inp["x"][0]
    gains = np.concatenate([col_layout(inp["ffn_norm"][0, 0]), col_layout(inp["mix_norm"][0])], axis=1)
    common = dict(gains=gains, wg=tile_w(inp["ffn_w_gate"][0, 0]), wu=tile_w(inp["ffn_w_up"][0, 0]),
                  wd=tile_wd(inp["ffn_w_down"][0, 0]), win=tile_w(inp["s5_w_in"][0]))
    in_maps = []
    for c in range(NCORES):
        m = dict(common)
        m["xT"] = np.ascontiguousarray(x[c * TOK:(c + 1) * TOK].T)
        in_maps.append(m)
    res = _run(nc, in_maps)
    h1 = np.concatenate([r["h1T"].T for r in res.results], axis=0)
    u = np.concatenate([r["uT"].T for r in res.results], axis=0)
    return h1, u


NBLK = SEQ // 256
NSEG = 8
SEGT = SEQ // NSEG
BIG = 1.0e30


def build_phase_d():
    nc = bass.Bass("TRN2", target_bir_lowering=False)
    qT_d = nc.dram_tensor("qT", [128, SEQ], F32, kind="ExternalInput").ap()
    kT_d = nc.dram_tensor("kT", [128, SEQ], F32, kind="ExternalInput").ap()
    v_d = nc.dram_tensor("v", [SEQ, 128], F32, kind="ExternalInput").ap()
    tri_d = nc.dram_tensor("tri", [128, 128], F32, kind="ExternalInput").ap()
    o_d = nc.dram_tensor("o", [SEQ, 128], F32, kind="ExternalOutput").ap()
    with ExitStack() as es:
        kb = KB(nc, es)
        attention(kb, qT_d, kT_d, v_d, tri_d, o_d)
    return nc


def attention(kb, qT_d, kT_d, v_d, tri_d, o_d, loader=None, o_store=None):
    nc = kb.nc
    qT = kb.sb("qT", [128, SEQ], BF16)
    kT = kb.sb("kT", [128, SEQ], BF16)
    va = kb.sb("va", [128, SEQ // 128, 130], BF16)
    tri = kb.sb("tri", [128, 128], BF16)
    kms = kb.sb("kms", [128, NBLK], F32)
    kmT = kb.sb("kmT", [128, NBLK], BF16)
    gate = Ring([kb.sb("gate", [128, 2, NBLK], F32) for _ in range(2)])
    m8 = Ring([kb.sb("m8", [128, 2, 8], F32) for _ in range(2)])
    sel = Ring([kb.sb("sel", [128, 2, NBLK], F32) for _ in range(2)])
    pT = Ring([kb.sb("pT", [128, 1024], BF16) for _ in range(3)])
    oacc = Ring([kb.sb("oacc", [128, 2, 130], F32) for _ in range(2)])
    rec = Ring([kb.sb("rec", [128, 2], F32) for _ in range(2)])
    ostage = Ring([kb.sb("ostage", [128, 2, 128], F32) for _ in range(2)])
    ostage_slot = [DmaSlot(kb, "ost") for _ in range(2)]
    psS = Ring([kb.ps("psS", [128, 1024], F32) for _ in range(2)])
    psO = Ring([kb.ps("psO", [128, 512], F32) for _ in range(3)])
    psG = Ring([kb.ps("psG", [128, 512], F32) for _ in range(1)])
    seg = [DmaSlot(kb, "seg") for _ in range(NSEG)]
    cslot = DmaSlot(kb, "const")

    ev_tri = cslot.dma(kb, "pool", tri[:], tri_d)
    ev_ones = kb.mark("pool", nc.gpsimd.memset(va[:, :, 128:130], 1.0))
    ev_ginit = None
    for g in gate.bufs:
        ev_ginit = kb.mark("pool", nc.gpsimd.memset(g[:], -BIG))
    seg_ev = []
    if loader is not None:
        seg_ev = loader(qT, kT, va, seg)
    else:
        v_t = v_d.rearrange("(t p) d -> p t d", p=128)
        for s in range(NSEG):
            cs = slice(s * SEGT, (s + 1) * SEGT)
            ts = slice(s * SEGT // 128, (s + 1) * SEGT // 128)
            seg[s].dma(kb, "pool", kT[:, cs], kT_d[:, cs])
            seg[s].dma(kb, "pool", qT[:, cs], qT_d[:, cs])
            seg_ev.append(seg[s].dma(kb, "pool", va[:, ts, 0:128], v_t[:, ts, :]))
    if o_store is not None:
        oTb = Ring([kb.sb("oTb", [128, 256], BF16) for _ in range(2)])
        oT_slot = [DmaSlot(kb, "oTb") for _ in range(2)]
        identA = kb.sb("identA", [128, 128], F32)
        ev_identA = DmaSlot(kb, "identA").dma(kb, "sp", identA[:], o_store["ident"])
    km_ev = []
    bps = NBLK // NSEG
    for s in range(NSEG):
        kb.wait("dve", seg_ev[s])
        e = kb.mark("dve", nc.vector.tensor_reduce(kms[:, s * bps:(s + 1) * bps],
                                                   kT[:, s * SEGT:(s + 1) * SEGT].rearrange("p (b k) -> p b k", k=256),
                                                   AX.X, ALU.add))
        kb.wait("dve", e)
        km_ev.append(kb.mark("dve", nc.vector.tensor_scalar(kmT[:, s * bps:(s + 1) * bps], kms[:, s * bps:(s + 1) * bps],
                                                            1.0 / 256, None, ALU.mult)))

    SKEW = 1
    tasks = []
    for j in range(NBLK):
        P = St()
        P.j, P.q0 = j, 256 * j
        P.si = P.ev_sel = P.ai = P.ev_acc = None
        tasks.append((P, -1))
        for b in range(0, j, 2):
            tasks.append((P, list(range(b, min(b + 2, j)))))

    def emit_gate(P):
        j, q0 = P.j, P.q0
        gb = psG.get(kb, "pe")
        kb.wait("pe", km_ev[(j - 1) // bps])
        for cq in range(2):
            ins = nc.tensor.matmul(psG.bufs[gb][:, cq * 64:cq * 64 + j], qT[:, q0 + cq * 128:q0 + (cq + 1) * 128],
                                   kmT[:, 0:j], start=True, stop=True)
        ev_g = kb.mark("pe", ins)
        gi = gate.get(kb, "dve")
        kb.wait("dve", ev_g, ev_ginit)
        eg = kb.mark("dve", nc.vector.tensor_copy(gate.bufs[gi][:, :, 0:j],
                                                  psG.bufs[gb][:, 0:128].rearrange("p (c b) -> p c b", b=64)[:, :, 0:j]))
        psG.release(gb, eg)
        mi = m8.get(kb, "dve")
        kb.wait("dve", eg)
        for cq in range(2):
            em = kb.mark("dve", nc.vector.max(m8.bufs[mi][:, cq, :], gate.bufs[gi][:, cq, :]))
        P.si = sel.get(kb, "dve")
        kb.wait("dve", em)
        for cq in range(2):
            P.ev_sel = kb.mark("dve", nc.vector.tensor_scalar(sel.bufs[P.si][:, cq, 0:j], gate.bufs[gi][:, cq, 0:j],
                                                              m8.bufs[mi][:, cq, 2:3], None, ALU.is_ge))
        gate.release(gi, P.ev_sel)
        m8.release(mi, P.ev_sel)

    def emit_s(P, b, T):
        j, q0 = P.j, P.q0
        kb.wait("pe", seg_ev[q0 // SEGT])
        qs = qT[:, q0:q0 + 256]
        sb_ = psS.get(kb, "pe")
        S_ = psS.bufs[sb_]
        T.pi = pT.get(kb, "act")
        P_ = pT.bufs[T.pi]
        if b == -1:
            if j >= 1:
                emit_gate(P)
            tA, tB = 2 * j, 2 * j + 1
            nc.tensor.matmul(S_[:, 0:256], kT[:, tA * 128:(tA + 1) * 128], qs, start=True, stop=True)
            ev_s = kb.mark("pe", nc.tensor.matmul(S_[:, 256:384], kT[:, tB * 128:(tB + 1) * 128], qT[:, q0 + 128:q0 + 256],
                                                  start=True, stop=True))
            kb.wait("act", ev_s)
            ev_p = kb.mark("act", nc.scalar.activation(P_[:, 0:384], S_[:, 0:384], AF.Exp))
            psS.release(sb_, ev_p)
            kb.wait("pool", ev_p, ev_tri)
            nc.gpsimd.tensor_tensor(P_[:, 0:128], P_[:, 0:128], tri[:], ALU.mult)
            T.ev_p = kb.mark("pool", nc.gpsimd.tensor_tensor(P_[:, 256:384], P_[:, 256:384], tri[:], ALU.mult))
        else:
            for i, bb in enumerate(b):
                nc.tensor.matmul(S_[:, i * 512:i * 512 + 256], kT[:, (2 * bb) * 128:(2 * bb + 1) * 128], qs, start=True, stop=True)
                ins = nc.tensor.matmul(S_[:, i * 512 + 256:i * 512 + 512], kT[:, (2 * bb + 1) * 128:(2 * bb + 2) * 128], qs,
                                       start=True, stop=True)
            ev_s = kb.mark("pe", ins)
            kb.wait("act", ev_s)
            w = 512 * len(b)
            T.ev_p = kb.mark("act", nc.scalar.activation(P_[:, 0:w], S_[:, 0:w], AF.Exp))
            psS.release(sb_, T.ev_p)

    def emit_rest(P, b, T):
        j, q0 = P.j, P.q0
        P_ = pT.bufs[T.pi]
        if b == -1:
            ob = psO.get(kb, "pe")
            O_ = psO.bufs[ob]
            kb.wait("pe", T.ev_p, ev_ones)
            tA, tB = 2 * j, 2 * j + 1
            nc.tensor.matmul(O_[:, 0:129], P_[:, 0:128], va[:, tA, 0:129], start=True, stop=True)
            nc.tensor.matmul(O_[:, 256:385], P_[:, 128:256], va[:, tA, 0:129], start=True, stop=False)
            ev_o = kb.mark("pe", nc.tensor.matmul(O_[:, 256:385], P_[:, 256:384], va[:, tB, 0:129], start=False, stop=True))
            pT.release(T.pi, ev_o)
            P.ai = oacc.get(kb, "dve")
            A_ = oacc.bufs[P.ai]
            kb.wait("dve", ev_o)
            P.ev_acc = kb.mark("dve", nc.vector.tensor_copy(A_[:, :, 0:129],
                                                            O_[:].rearrange("p (c x) -> p c x", x=256)[:, :, 0:129]))
            psO.release(ob, P.ev_acc)
            last = (j == 0)
        else:
            A_ = oacc.bufs[P.ai]
            kb.wait("pe", T.ev_p, ev_ones)
            for i, bb in enumerate(b):
                ob = psO.get(kb, "pe")
                O_ = psO.bufs[ob]
                o = i * 512
                nc.tensor.matmul(O_[:, 0:129], P_[:, o:o + 128], va[:, 2 * bb, 0:129], start=True, stop=False)
                nc.tensor.matmul(O_[:, 0:129], P_[:, o + 256:o + 384], va[:, 2 * bb + 1, 0:129], start=False, stop=True)
                nc.tensor.matmul(O_[:, 256:385], P_[:, o + 128:o + 256], va[:, 2 * bb, 0:129], start=True, stop=False)
                ev_o = kb.mark("pe", nc.tensor.matmul(O_[:, 256:385], P_[:, o + 384:o + 512], va[:, 2 * bb + 1, 0:129], start=False, stop=True))
                kb.wait("dve", ev_o, P.ev_sel, P.ev_acc)
                for cq in range(2):
                    P.ev_acc = kb.mark("dve", nc.vector.scalar_tensor_tensor(A_[:, cq, 0:129], O_[:, cq * 256:cq * 256 + 129],
                                                                             sel.bufs[P.si][:, cq, bb:bb + 1], A_[:, cq, 0:129],
                                                                             ALU.mult, ALU.add))
                psO.release(ob, P.ev_acc)
            pT.release(T.pi, ev_o)
            last = (b[-1] == j - 1)
        if last:
            if P.si is not None:
                sel.release(P.si, P.ev_acc)
            ri = rec.get(kb, "dve")
            kb.wait("dve", P.ev_acc)
            er = kb.mark("dve", nc.vector.reciprocal(rec.bufs[ri][:], A_[:, :, 128]))
            oi = ostage.get(kb, "dve")
            kb.wait("dve", er)
            for cq in range(2):
                eo = kb.mark("dve", nc.vector.tensor_scalar(ostage.bufs[oi][:, cq, :], A_[:, cq, 0:128],
                                                            rec.bufs[ri][:, cq:cq + 1], None, ALU.mult))
            oacc.release(P.ai, eo)
            rec.release(ri, eo)
            if o_store is not None:
                tb = psG.get(kb, "pe")
                kb.wait("pe", eo, ev_identA)
                for cq in range(2):
                    ins = nc.tensor.transpose(psG.bufs[tb][:, cq * 128:(cq + 1) * 128], ostage.bufs[oi][:, cq, :], identA[:])
                et = kb.mark("pe", ins)
                ostage.release(oi, et)
                ti = oTb.get(kb, "act")
                kb.wait("act", et)
                ec = kb.mark("act", nc.scalar.copy(oTb.bufs[ti][:], psG.bufs[tb][:, 0:256]))
                psG.release(tb, ec)
                kb.wait("sp", ec)
                ev = oT_slot[ti].dma(kb, "sp", o_store["dst"](q0), oTb.bufs[ti][:])
                oTb.release(ti, ev)
            else:
                kb.wait("sp", eo)
                ev = ostage_slot[oi].dma(kb, "sp", o_d[q0:q0 + 256, :].rearrange("(c p) d -> p c d", p=128), ostage.bufs[oi][:])
                ostage.release(oi, ev)

    tst = [St() for _ in tasks]
    for n in range(len(tasks) + SKEW):
        if n < len(tasks):
            P, b = tasks[n]
            emit_s(P, b, tst[n])
        if n >= SKEW:
            P, b = tasks[n - SKEW]
            emit_rest(P, b, tst[n - SKEW])
    for sl in ostage_slot + (oT_slot if o_store is not None else []):
        if sl.cnt:
            kb.E["sp"].wait_ge(sl.sem, sl.cnt)


def run_phase_d(q, k, v):
    nc = build_phase_d()
    tri = np.triu(np.ones((128, 128), np.float32))
    in_maps = []
    for h in range(NCORES):
        in_maps.append(dict(qT=np.ascontiguousarray(q[:, h, :].T), kT=np.ascontiguousarray(k[:, h, :].T),
                            v=np.ascontiguousarray(v[:, h, :]), tri=tri))
    res = _run(nc, in_maps)
    return np.concatenate([r["o"] for r in res.results], axis=1)


GPC = 8
NCK = SEQ // 8
NLV = 11
NKV = 24
SW = 384 + 2 * NLV
TWO_PI = 6.283185307179586
MAGIC = 12582912.0


def build_phase_b():
    nc = bass.Bass("TRN2", target_bir_lowering=False)
    dr = {}
    for n, shp in (("U", [GPC, 128, NCK]), ("a_re", [64, GPC]), ("a_im", [64, GPC]), ("ldt", [64, GPC]),
                   ("b_re", [64, GPC, 16]), ("b_im", [64, GPC, 16]), ("c_re", [64, GPC, 16]), ("c_im", [64, GPC, 16]),
                   ("kvec", [64, NKV]), ("selLo", [64, 128]), ("selHi", [64, 128]), ("ident", [128, 128]),
                   ("swap", [128, 128]), ("tmask", [128, 128])):
        dr[n] = nc.dram_tensor(n, shp, F32, kind="ExternalInput").ap()
    Y = nc.dram_tensor("Y", [GPC, 128, NCK], F32, kind="ExternalOutput").ap()
    with ExitStack() as es:
        kb = KB(nc, es)
        s5_scan(kb, dr, Y)
    return nc


def s5_scan(kb, dr, Y, u_load=None, y_store=None):
    nc = kb.nc
    G = GPC
    misc = DmaSlot(kb, "s5misc")
    P = {}
    ev_in = None
    for n in ("a_re", "a_im", "ldt", "b_re", "b_im", "c_re", "c_im", "kvec", "selLo", "selHi", "ident", "swap", "tmask"):
        shp = list(dr[n].shape)
        P[n] = kb.sb("p_" + n, shp, F32)
        ev_in = misc.dma(kb, "sp", P[n][:], dr[n])
    st = St()
    st.last = ev_in

    def V(fn, *a, **k):
        kb.wait("dve", st.last)
        st.last = kb.mark("dve", fn(*a, **k))

    def A(*a, **k):
        kb.wait("act", st.last)
        st.last = kb.mark("act", nc.scalar.activation(*a, **k))

    vt, vs, stt = nc.vector.tensor_tensor, nc.vector.tensor_scalar, nc.vector.scalar_tensor_tensor
    t = lambda n, shp: kb.sb("s5_" + n, shp, F32)
    dt, ar, th = t("dt", [64, G]), t("ar", [64, G]), t("th", [64, G])
    kar, kth, mag = t("kar", [64, G, NKV]), t("kth", [64, G, NKV]), t("mag", [64, G, NKV])
    r1, r2, sn, cs = t("r1", [64, G, NKV]), t("r2", [64, G, NKV]), t("sn", [64, G, NKV]), t("cs", [64, G, NKV])
    pr, pi_ = t("pr", [64, G, NKV]), t("pi", [64, G, NKV])
    A("dummy" and dt[:], P["ldt"][:], AF.Exp)
    V(vt, ar[:], P["a_re"][:], dt[:], ALU.mult)
    V(vt, th[:], P["a_im"][:], dt[:], ALU.mult)
    kv_b = P["kvec"][:].unsqueeze(1).to_broadcast([64, G, NKV])
    V(vt, kar[:], ar[:].unsqueeze(2).to_broadcast([64, G, NKV]), kv_b, ALU.mult)
    V(vt, kth[:], th[:].unsqueeze(2).to_broadcast([64, G, NKV]), kv_b, ALU.mult)
    A(mag[:], kar[:], AF.Exp)
    for off, dst in ((0.0, sn), (TWO_PI / 4, cs)):
        V(vs, r1[:], kth[:], off, 1.0 / TWO_PI, ALU.add, ALU.mult)
        V(vs, r2[:], r1[:], MAGIC, MAGIC, ALU.add, ALU.subtract)
        V(vs, r1[:], kth[:], off, None, ALU.add)
        V(stt, r1[:], r2[:], -TWO_PI, r1[:], ALU.mult, ALU.add)
        V(vs, r1[:], r1[:], -3.14159265, 3.14159265, ALU.max, ALU.min)
        A(dst[:], r1[:], AF.Sin)
    V(vt, pr[:], mag[:], cs[:], ALU.mult)
    V(vt, pi_[:], mag[:], sn[:], ALU.mult)
    lam_re, lam_im = pr[:, :, 16], pi_[:, :, 16]
    den, rden, nre, t1, t2, cre, cim = (t(n, [64, G]) for n in ("den", "rden", "nre", "t1", "t2", "cre", "cim"))
    V(vt, t1[:], P["a_re"][:], P["a_re"][:], ALU.mult)
    V(vt, den[:], P["a_im"][:], P["a_im"][:], ALU.mult)
    V(vt, den[:], den[:], t1[:], ALU.add)
    V(nc.vector.reciprocal, rden[:], den[:])
    V(vs, nre[:], lam_re, -1.0, None, ALU.add)
    V(vt, t1[:], nre[:], P["a_re"][:], ALU.mult)
    V(vt, t2[:], lam_im, P["a_im"][:], ALU.mult)
    V(vt, t1[:], t1[:], t2[:], ALU.add)
    V(vt, cre[:], t1[:], rden[:], ALU.mult)
    V(vt, t1[:], lam_im, P["a_re"][:], ALU.mult)
    V(vt, t2[:], nre[:], P["a_im"][:], ALU.mult)
    V(vt, t1[:], t1[:], t2[:], ALU.subtract)
    V(vt, cim[:], t1[:], rden[:], ALU.mult)
    bbr, bbi, u1, u2 = (t(n, [64, G, 16]) for n in ("bbr", "bbi", "u1", "u2"))
    cre_b = cre[:].unsqueeze(2).to_broadcast([64, G, 16])
    cim_b = cim[:].unsqueeze(2).to_broadcast([64, G, 16])
    V(vt, u1[:], P["b_re"][:], cre_b, ALU.mult)
    V(vt, u2[:], P["b_im"][:], cim_b, ALU.mult)
    V(vt, bbr[:], u1[:], u2[:], ALU.subtract)
    V(vt, u1[:], P["b_im"][:], cre_b, ALU.mult)
    V(vt, u2[:], P["b_re"][:], cim_b, ALU.mult)
    V(vt, bbi[:], u1[:], u2[:], ALU.add)
    LO, HI = t("LO", [64, G, SW]), t("HI", [64, G, SW])
    w1, w2 = t("w1", [64, G, 8, 16]), t("w2", [64, G, 8, 16])
    shp4 = [64, G, 8, 16]

    def cplx(k0, xr, xi, out_r, out_i, neg_im=False):
        pwr = pr[:, :, k0:k0 + 8].unsqueeze(3).to_broadcast(shp4)
        pwi = pi_[:, :, k0:k0 + 8].unsqueeze(3).to_broadcast(shp4)
        xr_b = xr[:].unsqueeze(2).to_broadcast(shp4)
        xi_b = xi[:].unsqueeze(2).to_broadcast(shp4)
        V(vt, w1[:], pwr, xr_b, ALU.mult)
        V(vt, w2[:], pwi, xi_b, ALU.mult)
        V(vt, out_r, w1[:], w2[:], ALU.subtract)
        V(vt, w1[:], pwr, xi_b, ALU.mult)
        V(vt, w2[:], pwi, xr_b, ALU.mult)
        if neg_im:
            V(vt, w1[:], w1[:], w2[:], ALU.add)
            V(vs, out_i, w1[:], -1.0, None, ALU.mult)
        else:
            V(vt, out_i, w1[:], w2[:], ALU.add)

    v4 = lambda T_, c0: T_[:, :, c0:c0 + 128].rearrange("p g (s m) -> p g s m", m=16)
    cplx(0, bbr, bbi, v4(LO, 0), v4(HI, 0))
    cplx(8, bbr, bbi, v4(LO, 128), v4(HI, 128))
    cplx(16, P["c_re"], P["c_im"], v4(LO, 256), v4(HI, 256), neg_im=True)
    qr, qi, q1, q2 = t("qr", [64, G, NLV]), t("qi", [64, G, NLV]), t("q1", [64, G]), t("q2", [64, G])
    V(nc.vector.tensor_copy, qr[:, :, 0], pr[:, :, 23])
    V(nc.vector.tensor_copy, qi[:, :, 0], pi_[:, :, 23])
    for j in range(NLV - 1):
        V(vt, q1[:], qr[:, :, j], qr[:, :, j], ALU.mult)
        V(vt, q2[:], qi[:, :, j], qi[:, :, j], ALU.mult)
        V(vt, qr[:, :, j + 1], q1[:], q2[:], ALU.subtract)
        V(stt, qi[:, :, j + 1], qr[:, :, j], 2.0, qi[:, :, j], ALU.mult, ALU.mult)
    V(nc.vector.tensor_copy, LO[:, :, 384:384 + NLV], qr[:])
    V(nc.vector.tensor_copy, HI[:, :, 384:384 + NLV], qr[:])
    V(nc.vector.tensor_copy, LO[:, :, 384 + NLV:SW], qi[:])
    V(vs, HI[:, :, 384 + NLV:SW], qi[:], -1.0, None, ALU.mult)
    ev_setup = st.last

    psum = Ring([kb.ps("s5bank", [128, 512], F32) for _ in range(8)])
    stk = [kb.sb("stk", [128, SW], F32) for _ in range(G)]
    BzT = [kb.sb("BzT", [128, 128], F32R) for _ in range(G)]
    TzT = [kb.sb("TzT", [128, 128], F32R) for _ in range(G)]
    CzT = [kb.sb("CzT", [128, 128], F32R) for _ in range(G)]
    AT = [kb.sb("AT", [128, NLV, 128], F32R) for _ in range(G)]
    atmp = Ring([kb.sb("atmp", [128, 128], F32) for _ in range(2)])
    mat_ev = [None] * G
    for g in range(G):
        b = psum.get(kb, "pe")
        kb.wait("pe", ev_setup)
        nc.tensor.matmul(psum.bufs[b][:, 0:SW], P["selLo"][:], LO[:, g, :], start=True, stop=False)
        e = kb.mark("pe", nc.tensor.matmul(psum.bufs[b][:, 0:SW], P["selHi"][:], HI[:, g, :], start=False, stop=True))
        kb.wait("act", e)
        e_stk = kb.mark("act", nc.scalar.copy(stk[g][:], psum.bufs[b][:, 0:SW]))
        psum.release(b, e_stk)
        b = psum.get(kb, "pe")
        kb.wait("pe", e_stk)
        nc.tensor.transpose(psum.bufs[b][:, 0:128], stk[g][:, 0:128], P["ident"][:])
        e = kb.mark("pe", nc.tensor.matmul(psum.bufs[b][:, 128:256], stk[g][:, 128:256], stk[g][:, 256:384], start=True, stop=True))
        kb.wait("act", e)
        e1 = kb.mark("act", nc.scalar.copy(BzT[g][:], psum.bufs[b][:, 0:128]))
        kb.wait("dve", e, e_stk)
        e2 = kb.mark("dve", nc.vector.tensor_tensor(TzT[g][:], psum.bufs[b][:, 128:256], P["tmask"][:], ALU.mult))
        psum.release(b, e1, e2)
        e3 = kb.mark("act", nc.scalar.copy(CzT[g][:], stk[g][:, 256:384]))
        for j in range(NLV):
            ti = atmp.get(kb, "dve")
            e4 = kb.mark("dve", nc.vector.tensor_scalar(atmp.bufs[ti][:], P["ident"][:], stk[g][:, 384 + j:385 + j], None, ALU.mult))
            kb.wait("dve", e4)
            e5 = kb.mark("dve", nc.vector.scalar_tensor_tensor(AT[g][:, j, :], P["swap"][:], stk[g][:, 384 + NLV + j:385 + NLV + j],
                                                               atmp.bufs[ti][:], ALU.mult, ALU.add))
            atmp.release(ti, e5)
        mat_ev[g] = [e1, e2, e3, e5]

    Ubuf = Ring([kb.sb("Ubuf", [128, NCK], F32R) for _ in range(3)])
    Uslot = [DmaSlot(kb, "U") for _ in range(3)]
    Xb = [kb.sb("Xb", [128, 2 + NCK], F32R) for _ in range(2)]
    ystage = Ring([kb.sb("ystage", [128, 512], F32) for _ in range(3)])
    yslot = [DmaSlot(kb, "y") for _ in range(3)]
    x_free = [None, None]
    zt = kb.sb("zt", [128, 2], F32)
    ez = kb.mark("pool", nc.gpsimd.memset(zt[:], 0.0))
    kb.wait("pool", ez)
    for xb in Xb:
        ev_pad = kb.mark("pool", nc.gpsimd.tensor_copy(xb[:, 0:2], zt[:]))
    NB = NCK // 512
    for g0 in range(0, G, 2):
        grp = []
        for gi in range(2):
            g = g0 + gi
            R = St()
            R.g, R.X = g, Xb[gi]
            R.ui = Ubuf.get(kb, "pool")
            if u_load is not None:
                R.ev_u = u_load(g, Ubuf.bufs[R.ui], Uslot[R.ui])
            else:
                R.ev_u = Uslot[R.ui].dma(kb, "pool", Ubuf.bufs[R.ui][:], dr["U"][g])
            R.U = Ubuf.bufs[R.ui]
            grp.append(R)
        for gi, R in enumerate(grp):
            R.ev_x = ev_pad
            evs = []
            for bi in range(NB):
                b = psum.get(kb, "pe")
                kb.wait("pe", R.ev_u, mat_ev[R.g])
                e = kb.mark("pe", nc.tensor.matmul(psum.bufs[b][:], BzT[R.g][:], R.U[:, bi * 512:(bi + 1) * 512], start=True, stop=True))
                kb.wait("act", e, x_free[gi])
                e2 = kb.mark("act", nc.scalar.copy(R.X[:, 2 + bi * 512:2 + (bi + 1) * 512], psum.bufs[b][:]))
                psum.release(b, e2)
                evs.append(e2)
            R.ev_x = [R.ev_x] + evs
        for j in range(NLV):
            d = 1 << j
            for gi, R in enumerate(grp):
                jobs = []
                for bi in range(NB):
                    lo = max(bi * 512, d) if d > 1 else bi * 512
                    hi = (bi + 1) * 512
                    if lo >= hi:
                        continue
                    b = psum.get(kb, "pe")
                    kb.wait("pe", R.ev_x)
                    e = kb.mark("pe", nc.tensor.matmul(psum.bufs[b][:, lo - bi * 512:512], AT[R.g][:, j, :],
                                                       R.X[:, 2 + lo - d:2 + hi - d], start=True, stop=True))
                    jobs.append((b, lo, hi, bi))
                evs = []
                kb.wait("dve", e, R.ev_x)
                for (b, lo, hi, bi) in jobs:
                    e2 = kb.mark("dve", nc.vector.tensor_tensor(R.X[:, 2 + lo:2 + hi], psum.bufs[b][:, lo - bi * 512:512],
                                                                R.X[:, 2 + lo:2 + hi], ALU.add))
                    psum.release(b, e2)
                    evs.append(e2)
                R.ev_x = evs
        for gi, R in enumerate(grp):
            for bi in range(NB):
                b = psum.get(kb, "pe")
                kb.wait("pe", R.ev_x)
                nc.tensor.matmul(psum.bufs[b][:], TzT[R.g][:], R.U[:, bi * 512:(bi + 1) * 512], start=True, stop=False)
                e = kb.mark("pe", nc.tensor.matmul(psum.bufs[b][:], CzT[R.g][:], R.X[:, 1 + bi * 512:1 + (bi + 1) * 512],
                                                   start=False, stop=True))
                yi = ystage.get(kb, "act")
                kb.wait("act", e)
                e2 = kb.mark("act", nc.scalar.copy(ystage.bufs[yi][:], psum.bufs[b][:]))
                psum.release(b, e2)
                kb.wait("sp", e2)
                if y_store is not None:
                    ev = y_store(R.g, bi, ystage.bufs[yi], yslot[yi])
                else:
                    ev = yslot[yi].dma(kb, "sp", Y[R.g, :, bi * 512:(bi + 1) * 512], ystage.bufs[yi][:])
                ystage.release(yi, ev)
            Ubuf.release(R.ui, e)
            x_free[gi] = e
    for sl in yslot:
        kb.E["sp"].wait_ge(sl.sem, sl.cnt)


def s5_consts():
    kvec = np.array([7, 6, 5, 4, 3, 2, 1, 0, -1, -2, -3, -4, -5, -6, -7, -8, 1, 2, 3, 4, 5, 6, 7, 8], np.float32)
    c = {}
    c["kvec"] = np.tile(kvec[None, :], (64, 1))
    c["selLo"] = np.concatenate([np.eye(64, dtype=np.float32), np.zeros((64, 64), np.float32)], axis=1)
    c["selHi"] = np.concatenate([np.zeros((64, 64), np.float32), np.eye(64, dtype=np.float32)], axis=1)
    c["ident"] = np.eye(128, dtype=np.float32)
    sw = np.zeros((128, 128), np.float32)
    sw[np.arange(64), np.arange(64) + 64] = 1.0
    sw[np.arange(64) + 64, np.arange(64)] = 1.0
    c["swap"] = sw
    i = np.arange(128)
    c["tmask"] = (i[None, :] // 16 >= i[:, None] // 16).astype(np.float32)
    return c


def s5_in_maps(inp, u):
    Uall = np.ascontiguousarray(u.reshape(NCK, 8, 64, 16).transpose(2, 1, 3, 0)).reshape(64, 128, NCK)
    cst = s5_consts()
    maps = []
    for c in range(NCORES):
        gs = slice(c * GPC, (c + 1) * GPC)
        m = dict(cst)
        m["U"] = np.ascontiguousarray(Uall[gs])
        m["a_re"] = np.ascontiguousarray(inp["s5_a_re"][0][gs].T)
        m["a_im"] = np.ascontiguousarray(inp["s5_a_im"][0][gs].T)
        m["ldt"] = np.ascontiguousarray(np.tile(inp["s5_log_dt"][0][gs][None, :], (64, 1)))
        m["b_re"] = np.ascontiguousarray(inp["s5_b_re"][0][gs].transpose(1, 0, 2))
        m["b_im"] = np.ascontiguousarray(inp["s5_b_im"][0][gs].transpose(1, 0, 2))
        m["c_re"] = np.ascontiguousarray(inp["s5_c_re"][0][gs].transpose(2, 0, 1))
        m["c_im"] = np.ascontiguousarray(inp["s5_c_im"][0][gs].transpose(2, 0, 1))
        maps.append(m)
    return maps


def run_phase_b(inp, u):
    nc = build_phase_b()
    res = _run(nc, s5_in_maps(inp, u))
    Yall = np.concatenate([r["Y"] for r in res.results], axis=0)
    return np.ascontiguousarray(Yall.reshape(64, 8, 16, NCK).transpose(3, 1, 0, 2)).reshape(SEQ, 1024)


C_GAINS = 8 * 5 + 8


def build_phase_c():
    nc = bass.Bass("TRN2", target_bir_lowering=False)
    I = lambda n, shp: nc.dram_tensor(n, shp, F32, kind="ExternalInput").ap()
    O = lambda n, shp: nc.dram_tensor(n, shp, F32, kind="ExternalOutput").ap()
    h1T, uT, yT = I("h1T", [D, TOK]), I("uT", [D, TOK]), I("yT", [D, TOK])
    gains = I("gains", [128, C_GAINS])
    wglu, wout = I("wglu", [KT, 128, KT, 128]), I("wout", [KT, 128, KT, 128])
    ffw = [(I(f"wg{i}", [FT, 128, KT, 128]), I(f"wu{i}", [FT, 128, KT, 128]), I(f"wd{i}", [2, KT, 128, FH, 128])) for i in range(2)]
    wqk = I("wqk", [16, 128, KT, 128])
    wv = I("wv", [128, KT, D])
    pos = I("pos", [32, TOK])
    rmT = I("rmT", [32, 32])
    h4T = O("h4T", [D, TOK])
    qkT = O("qkT", [16, 128, TOK])
    v_o = O("v", [TOK, D])
    with ExitStack() as es:
        kb = KB(nc, es)
        S = tok_state(kb, C_GAINS)
        S.ev_gains = S.misc.dma(kb, "sp", S.gains[:], gains)
        load_hT(kb, S, h1T)
        s5_post(kb, S, uT, yT, wglu, wout, gcol_d=0)
        ffn(kb, S, *ffw[0], 8)
        ffn(kb, S, *ffw[1], 16)
        rmsnorm(kb, S, 24)
        store_hT(kb, S, h4T)
        qkv(kb, S, wqk, wv, pos, rmT, qkT, v_o, gcol=32)
        finish(kb, S)
    return nc


def s5_post(kb, S, uT, yT, wglu, wout, gcol_d):
    nc = kb.nc
    yu = S.act[:, 8:11, :].bitcast(F32)
    bufs = [yu[:, i, j * CH:(j + 1) * CH] for i in range(3) for j in range(2)]
    ring = Ring([(bufs[2 * i], bufs[2 * i + 1]) for i in range(3)])
    slots = [DmaSlot(kb, "yu") for _ in range(3)]
    g_ev = [None] * NCH
    for c in range(NCH):
        cs = slice(c * CH, (c + 1) * CH)
        for f in range(KT):
            i = ring.get(kb, "sp")
            kb.wait("sp", S.act_free)
            yb, ub = ring.bufs[i]
            slots[i].dma(kb, "sp", yb, yT[f * 128:(f + 1) * 128, cs])
            ev = slots[i].dma(kb, "sp", ub, uT[f * 128:(f + 1) * 128, cs])
            kb.wait("dve", ev, S.ev_gains)
            e1 = kb.mark("dve", nc.vector.scalar_tensor_tensor(yb, ub, S.gains[:, gcol_d + f:gcol_d + f + 1], yb, ALU.mult, ALU.add))
            kb.wait("act", e1, S.hn_free)
            e2 = kb.mark("act", nc.scalar.activation(S.hn[:, f, cs], yb, AF.Gelu_apprx_tanh))
            ring.release(i, e2)
        g_ev[c] = e2
    s5_post_proj(kb, S, wglu, wout, g_ev)


def s5_post_proj(kb, S, wglu, wout, g_ev):
    nc = kb.nc
    g2 = S.act[:, 0:KT, :]
    g2_ev = [None] * NCH

    def evac_glu(ot, c, bank, ev_mm):
        cs = slice(c * CH, (c + 1) * CH)
        ti = S.sg.get(kb, "act")
        kb.wait("act", ev_mm)
        e1 = kb.mark("act", nc.scalar.activation(S.sg.bufs[ti][:], bank[:], AF.Sigmoid))
        kb.wait("dve", e1, S.act_free)
        e2 = kb.mark("dve", nc.vector.tensor_tensor(g2[:, ot, cs], S.sg.bufs[ti][:], S.hn[:, ot, cs], ALU.mult))
        S.sg.release(ti, e2)
        g2_ev[c] = e2 if ot == KT - 1 else g2_ev[c]
        return e1
    proj_fm(kb, S, wglu, KT, evac_glu, opnd=S.hn, opnd_ev=g_ev)

    def evac_out(ot, c, bank, ev_mm):
        cs = slice(c * CH, (c + 1) * CH)
        kb.wait("dve", ev_mm, S.h_ev[c])
        e = kb.mark("dve", nc.vector.tensor_tensor(S.hT[:, ot, cs], bank[:], S.hT[:, ot, cs], ALU.add))
        S.h_ev[c] = e
        return e
    proj_fm(kb, S, wout, KT, evac_out, opnd=g2, opnd_ev=g2_ev)


def qkv(kb, S, wqk, wv, pos, rmT, qkT, v_o, gcol, fused=None):
    nc = kb.nc
    st = St()
    st.last = S.ev_gains

    def V(fn, *a, **k):
        kb.wait("dve", st.last)
        st.last = kb.mark("dve", fn(*a, **k))

    def A(*a, **k):
        kb.wait("act", st.last)
        st.last = kb.mark("act", nc.scalar.activation(*a, **k))
    vt, vs, stt = nc.vector.tensor_tensor, nc.vector.tensor_scalar, nc.vector.scalar_tensor_tensor
    kb.wait("sp", S.act_free)
    tabs = S.act[0:32, 4:8, :].bitcast(F32)
    Ct = tabs[:, 0:2, :].rearrange("p a b -> p (a b)")
    Sn = tabs[:, 2:4, :].rearrange("p a b -> p (a b)")
    wk = S.act[0:32, 8:11, :].bitcast(F32)
    ang = wk[:, 0:2, :].rearrange("p a b -> p (a b)")
    r1 = kb.sb("rope_r1", [32, TOK], F32)
    rm = kb.sb("rmT", [32, 32], F32R)
    gq = kb.sb("gq", [128, 2], F32)
    ev_pos = S.misc.dma(kb, "sp", r1[:], pos)
    ev_rm = S.misc.dma(kb, "pool", rm[:], rmT)
    st.last = [st.last, ev_pos, S.act_free]
    invf = S.gains[0:32, gcol + 2:gcol + 3]
    V(vs, ang, r1[:], invf, None, ALU.mult)
    for off, dst in ((0.0, Sn), (TWO_PI / 4, Ct)):
        V(vs, r1[:], ang, off, 1.0 / TWO_PI, ALU.add, ALU.mult)
        V(vs, r1[:], r1[:], MAGIC, MAGIC, ALU.add, ALU.subtract)
        V(vs, dst, ang, off, None, ALU.add)
        V(stt, dst, r1[:], -TWO_PI, dst, ALU.mult, ALU.add)
        V(vs, dst, dst, -3.14159265, 3.14159265, ALU.max, ALU.min)
        A(dst, dst, AF.Sin)
    V(vs, gq[:, 0:1], S.gains[:, gcol:gcol + 1], 128.0 ** -0.5, None, ALU.mult)
    V(nc.vector.tensor_copy, gq[:, 1:2], S.gains[:, gcol + 1:gcol + 2])
    ev_tab = st.last
    qn_ring = Ring([kb.sb("qn", [128, CH], F32R) for _ in range(2)])
    t1_ring = Ring([kb.sb("ropet", [32, CH], F32) for _ in range(2)])
    t2_ring = Ring([kb.sb("ropet2", [32, CH], F32) for _ in range(2)])

    mainr = Ring(S.psum.bufs[0:4])
    mainr.rel = [list(r) for r in S.psum.rel[0:4]]
    auxr = Ring(S.psum.bufs[4:8])
    auxr.rel = [list(r) for r in S.psum.rel[4:8]]

    def evac_qk(ot, c, bank, ev_mm):
        cs = slice(c * CH, (c + 1) * CH)
        which = 0 if ot < 8 else 1
        i = S.sq.get(kb, "act")
        kb.wait("act", ev_mm)
        e = kb.mark("act", nc.scalar.activation(S.sq.bufs[i][:], bank[:], AF.Square))
        b2 = auxr.get(kb, "pe")
        kb.wait("pe", e)
        e2 = kb.mark("pe", nc.tensor.matmul(auxr.bufs[b2][:], S.ones[:], S.sq.bufs[i][:], start=True, stop=True))
        S.sq.release(i, e2)
        ri = S.rt.get(kb, "act")
        kb.wait("act", e2)
        e3 = kb.mark("act", nc.scalar.activation(S.rt.bufs[ri][:], auxr.bufs[b2][:], AF.Sqrt, bias=EPS, scale=1.0 / 128))
        auxr.release(b2, e3)
        si = S.rstd.get(kb, "dve")
        kb.wait("dve", e3)
        e4 = kb.mark("dve", nc.vector.reciprocal(S.rstd.bufs[si][:], S.rt.bufs[ri][:]))
        S.rt.release(ri, e4)
        qi = qn_ring.get(kb, "dve")
        qn = qn_ring.bufs[qi]
        kb.wait("dve", e4, ev_mm, ev_tab)
        e5 = kb.mark("dve", nc.vector.scalar_tensor_tensor(qn[:], bank[:], gq[:, which:which + 1], S.rstd.bufs[si][:],
                                                           ALU.mult, ALU.mult))
        S.rstd.release(si, e5)
        b3 = auxr.get(kb, "pe")
        kb.wait("pe", e5, ev_rm)
        e6 = kb.mark("pe", nc.tensor.matmul(auxr.bufs[b3][0:32, :], rm[:], qn[0:32, :], start=True, stop=True))
        if fused is not None:
            oi = fused["ring"].get(kb, "act")
            ob = fused["ring"].bufs[oi]
        else:
            oi = S.stage.get(kb, "act")
            ob = S.stage.bufs[oi]
        kb.wait("act", e5)
        e7 = kb.mark("act", nc.scalar.copy(ob[:], qn[:]))
        ti = t1_ring.get(kb, "pool")
        kb.wait("pool", e5, ev_tab)
        e8 = kb.mark("pool", nc.gpsimd.tensor_tensor(t1_ring.bufs[ti][:], qn[0:32, :], Ct[:, cs], ALU.mult))
        kb.wait("dve", e6, e7, e8)
        t2i = t2_ring.get(kb, "dve")
        e9 = kb.mark("dve", nc.vector.tensor_tensor(t2_ring.bufs[t2i][:], auxr.bufs[b3][0:32, :], Sn[:, cs], ALU.mult))
        kb.wait("dve", e9)
        e10 = kb.mark("dve", nc.vector.tensor_tensor(ob[0:32, :], t2_ring.bufs[t2i][:], t1_ring.bufs[ti][:], ALU.add))
        auxr.release(b3, e9)
        t1_ring.release(ti, e10)
        t2_ring.release(t2i, e10)
        qn_ring.release(qi, e6, e7, e8)
        kb.wait("sp", e10, e7)
        if fused is not None:
            ev = fused["slots"][oi].dma(kb, "sp", fused["qk_dst"](ot, c), ob[:])
            fused["ring"].release(oi, ev)
        else:
            ev = S.stage_slot[oi].dma(kb, "sp", qkT[ot, :, cs], ob[:])
            S.stage.release(oi, ev)
        return e5
    proj_fm(kb, S, wqk, 16, evac_qk, defer=True, ring=mainr)
    S.psum.rel = mainr.rel + auxr.rel
    S.psum.i = 0
    wvt = S.act[:, 0:4, :].rearrange("p a (b c) -> p (a b) c", c=D)
    kb.wait("pool", S.act_free)
    ev_wv = S.misc.dma(kb, "pool", wvt, wv)
    for tt in range(TOK // 128):
        c = tt * 128 // CH
        for half in range(2):
            b = S.psum.get(kb, "pe")
            kb.wait("pe", ev_wv, S.hn_ev[c])
            for k in range(KT):
                ins = nc.tensor.matmul(S.psum.bufs[b][:], S.hn[:, k, tt * 128:(tt + 1) * 128], wvt[:, k, half * 512:(half + 1) * 512],
                                       start=(k == 0), stop=(k == KT - 1))
            e = kb.mark("pe", ins)
            S.hn_free = e
            if fused is not None:
                oi = fused["ring"].get(kb, "act")
                kb.wait("act", e)
                e2 = kb.mark("act", nc.scalar.copy(fused["ring"].bufs[oi][:], S.psum.bufs[b][:]))
                S.psum.release(b, e2)
                kb.wait("sp", e2)
                ev = fused["slots"][oi].dma(kb, "sp", fused["v_dst"](tt, half),
                                            fused["ring"].bufs[oi][:].rearrange("p (h d) -> p h d", d=128))
                fused["ring"].release(oi, ev)
                continue
            oi = S.stage.get(kb, "act")
            kb.wait("act", e)
            e2 = kb.mark("act", nc.scalar.copy(S.stage.bufs[oi][:], S.psum.bufs[b][:]))
            S.psum.release(b, e2)
            kb.wait("sp", e2)
            ev = S.stage_slot[oi].dma(kb, "sp", v_o[tt * 128:(tt + 1) * 128, half * 512:(half + 1) * 512], S.stage.bufs[oi][:])
            S.stage.release(oi, ev)
    S.act_free = e


def rope_consts():
    inv_freq = (500000.0 ** (-np.arange(0, 32, 2, dtype=np.float32) / 32)).astype(np.float32)
    invf = np.concatenate([inv_freq, inv_freq])
    rmT = np.zeros((32, 32), np.float32)
    for i in range(16):
        rmT[16 + i, i] = -1.0
        rmT[i, 16 + i] = 1.0
    return invf, rmT


def run_phase_c(inp, h1, u, yscan):
    nc = build_phase_c()
    invf, rmT = rope_consts()
    gains = np.zeros((128, C_GAINS), np.float32)
    gains[:, 0:8] = col_layout(inp["s5_d"][0])
    gains[:, 8:16] = col_layout(inp["ffn_norm"][0, 1])
    gains[:, 16:24] = col_layout(inp["ffn_norm"][1, 0])
    gains[:, 24:32] = col_layout(inp["mix_norm"][1])
    gains[:, 32] = inp["moba_q_norm"][0]
    gains[:, 33] = inp["moba_k_norm"][0]
    gains[0:32, 34] = invf
    wqkv = inp["moba_w_qkv"][0]
    common = dict(gains=gains, wglu=tile_w(inp["s5_w_glu"][0]), wout=tile_w(inp["s5_w_out"][0]),
                  wg0=tile_w(inp["ffn_w_gate"][0, 1]), wu0=tile_w(inp["ffn_w_up"][0, 1]), wd0=tile_wd(inp["ffn_w_down"][0, 1]),
                  wg1=tile_w(inp["ffn_w_gate"][1, 0]), wu1=tile_w(inp["ffn_w_up"][1, 0]), wd1=tile_wd(inp["ffn_w_down"][1, 0]),
                  wqk=tile_w(wqkv[:, 0:2048]),
                  wv=np.ascontiguousarray(wqkv[:, 2048:3072].reshape(KT, 128, D).transpose(1, 0, 2)), rmT=rmT)
    maps = []
    for c in range(NCORES):
        ts = slice(c * TOK, (c + 1) * TOK)
        m = dict(common)
        m["h1T"] = np.ascontiguousarray(h1[ts].T)
        m["uT"] = np.ascontiguousarray(u[ts].T)
        m["yT"] = np.ascontiguousarray(yscan[ts].T)
        m["pos"] = np.ascontiguousarray(np.tile(np.arange(c * TOK, (c + 1) * TOK, dtype=np.float32)[None, :], (32, 1)))
        maps.append(m)
    res = _run(nc, maps)
    h4 = np.concatenate([r["h4T"].T for r in res.results], axis=0)
    qk = np.concatenate([r["qkT"] for r in res.results], axis=2)
    v = np.concatenate([r["v"] for r in res.results], axis=0)
    q = np.ascontiguousarray(qk[0:8].transpose(2, 0, 1))
    k = np.ascontiguousarray(qk[8:16].transpose(2, 0, 1))
    return h4, q, k, v.reshape(SEQ, 8, 128)


def build_phase_e():
    nc = bass.Bass("TRN2", target_bir_lowering=False)
    I = lambda n, shp: nc.dram_tensor(n, shp, F32, kind="ExternalInput").ap()
    h4T, oT = I("h4T", [D, TOK]), I("oT", [D, TOK])
    gains = I("gains", [128, 8])
    wo = I("wo", [KT, 128, KT, 128])
    wg, wu, wd = I("wg", [FT, 128, KT, 128]), I("wu", [FT, 128, KT, 128]), I("wd", [2, KT, 128, FH, 128])
    outT = nc.dram_tensor("outT", [D, TOK], F32, kind="ExternalOutput").ap()
    with ExitStack() as es:
        kb = KB(nc, es)
        S = tok_state(kb, 8)
        S.ev_gains = S.misc.dma(kb, "sp", S.gains[:], gains)
        load_hT(kb, S, h4T)
        attn_out(kb, S, oT, wo)
        ffn(kb, S, wg, wu, wd, 0)
        store_hT(kb, S, outT)
        finish(kb, S)
    return nc


def attn_out(kb, S, oT, wo, o_gather=None):
    nc = kb.nc
    oslot = DmaSlot(kb, "oT")
    kb.wait("pool", S.hn_free)
    for f in range(KT):
        if o_gather is not None:
            ev = o_gather(f, S.hn[:, f, :], oslot)
        else:
            ev = oslot.dma(kb, "pool", S.hn[:, f, :], oT[f * 128:(f + 1) * 128, :])
    o_ev = [ev] * NCH

    def evac_out(ot, c, bank, ev_mm):
        cs = slice(c * CH, (c + 1) * CH)
        kb.wait("dve", ev_mm, S.h_ev[c])
        e = kb.mark("dve", nc.vector.tensor_tensor(S.hT[:, ot, cs], bank[:], S.hT[:, ot, cs], ALU.add))
        S.h_ev[c] = e
        return e
    proj_fm(kb, S, wo, KT, evac_out, opnd=S.hn, opnd_ev=o_ev)


def run_phase_e(inp, h4, o):
    nc = build_phase_e()
    common = dict(gains=col_layout(inp["ffn_norm"][1, 1]), wo=tile_w(inp["moba_w_out"][0]),
                  wg=tile_w(inp["ffn_w_gate"][1, 1]), wu=tile_w(inp["ffn_w_up"][1, 1]), wd=tile_wd(inp["ffn_w_down"][1, 1]))
    maps = []
    for c in range(NCORES):
        ts = slice(c * TOK, (c + 1) * TOK)
        m = dict(common)
        m["h4T"] = np.ascontiguousarray(h4[ts].T)
        m["oT"] = np.ascontiguousarray(o[ts].T)
        maps.append(m)
    res = _run(nc, maps)
    return np.concatenate([r["outT"].T for r in res.results], axis=0)


def kernel_unfused(inp):
    h1, u = run_phase_a(inp)
    ysc = run_phase_b(inp, u)
    h4, q, k, v = run_phase_c(inp, h1, u, ysc)
    o = run_phase_d(q, k, v)
    out = run_phase_e(inp, h4, o)
    return out.reshape(1, SEQ, D).astype(np.float32)


import concourse.bass as _bass_mod
I32 = mybir.dt.int32
NG_ALL = 16 + C_GAINS + 8


def gather_rows(kb, slot, out_ap, gath_ap, idx_col):
    ins = kb.nc.gpsimd.indirect_dma_start(out=out_ap, out_offset=None, in_=gath_ap,
                                          in_offset=_bass_mod.IndirectOffsetOnAxis(ap=idx_col, axis=0))
    ins.then_inc(slot.sem, 16)
    slot.cnt += 16
    return (slot.sem, slot.cnt, "dma")


def all_gather(kb, send, gath):
    finish_all(kb, "pool")
    sem = kb.newsem("ag")
    kb.nc.gpsimd.collective_compute("AllGather", ALU.bypass, replica_groups=[list(range(NCORES))],
                                    ins=[send.opt()], outs=[gath.opt()]).then_inc(sem, 1)
    kb.nc.gpsimd.wait_ge(sem, 1)


def build_fused(stop=None):
    nc = bass.Bass("TRN2", target_bir_lowering=False)
    nc.declared = []

    def I(n, shp, dt=F32):
        nc.declared.append(n)
        return nc.dram_tensor(n, shp, dt, kind="ExternalInput").ap()

    def dbg_out(src, shp, dt=F32):
        o = nc.dram_tensor("dbg", shp, dt, kind="ExternalOutput").ap()
        with nc.semaphore("dbgsem") as sm:
            nc.sync.dma_start(out=o, in_=src).then_inc(sm, 16)
            nc.sync.wait_ge(sm, 16)
        return nc
    xT = I("xT", [D, TOK])
    gains = I("gains", [128, NG_ALL])
    nffn = {None: 4, "A": 1, "B": 1, "C": 3, "D": 3}[stop]
    ffw = [(I(f"wg{i}", [FT, 128, KT, 128]), I(f"wu{i}", [FT, 128, KT, 128]), I(f"wd{i}", [2, KT, 128, FH, 128])) for i in range(nffn)]
    win = I("win", [KT, 128, KT, 128])
    if stop not in ("A", "B"):
        wglu, wout = I("wglu", [KT, 128, KT, 128]), I("wout", [KT, 128, KT, 128])
        wqk = I("wqk", [16, 128, KT, 128])
        wv = I("wv", [128, KT, D])
        pos, rmT, tri = I("pos", [32, TOK]), I("rmT", [32, 32]), I("tri", [128, 128])
    if stop is None:
        wo = I("wo", [KT, 128, KT, 128])
    dr = {}
    for n, shp in (("a_re", [64, GPC]), ("a_im", [64, GPC]), ("ldt", [64, GPC]),
                   ("b_re", [64, GPC, 16]), ("b_im", [64, GPC, 16]), ("c_re", [64, GPC, 16]), ("c_im", [64, GPC, 16]),
                   ("kvec", [64, NKV]), ("selLo", [64, 128]), ("selHi", [64, 128]), ("ident", [128, 128]),
                   ("swap", [128, 128]), ("tmask", [128, 128])):
        dr[n] = I(n, shp)
    idx_u, idx_y, idx_q, idx_o = I("idx_u", [128, 64], I32), I("idx_y", [128, 64], I32), I("idx_q", [128, 24], I32), I("idx_o", [128, 8], I32)
    outT = nc.dram_tensor("outT", [D, TOK], F32, kind="ExternalOutput").ap()
    T = lambda n, shp, dt=F32, **k: nc.dram_tensor(n, shp, dt, **k).ap()
    hspill, ukeep = T("hspill", [D, TOK]), T("ukeep", [D, TOK])
    usend, ugath = T("usend", [8192, 256]), T("ugath", [8 * 8192, 256], addr_space="Shared")
    ysend, ygath = T("ysend", [8192, 256]), T("ygath", [8 * 8192, 256], addr_space="Shared")
    qsend, qgath = T("qsend", [3072, TOK], BF16), T("qgath", [8 * 3072, TOK], BF16, addr_space="Shared")
    osend, ogath = T("osend", [1024, TOK], BF16), T("ogath", [8 * 1024, TOK], BF16, addr_space="Shared")

    with ExitStack() as es:
        kb = KB(nc, es)
        S = tok_state(kb, 16)
        S.ev_gains = S.misc.dma(kb, "sp", S.gains[:], gains[:, 0:16])
        load_hT(kb, S, xT)
        ffn(kb, S, *ffw[0], 0)
        store_hT(kb, S, hspill)
        rmsnorm(kb, S, 8)
        ust = Ring([kb.sb("ust", [128, 8, 256], F32R) for _ in range(2)])
        ust_slot = [DmaSlot(kb, "ust") for _ in range(2)]
        U = St()

        def evac_u(ot, c, bank, ev_mm):
            i = S.stage.get(kb, "act")
            kb.wait("act", ev_mm)
            e = kb.mark("act", nc.scalar.copy(S.stage.bufs[i][:], bank[:]))
            kb.wait("sp", e)
            ev = S.stage_slot[i].dma(kb, "sp", ukeep[ot * 128:(ot + 1) * 128, c * CH:(c + 1) * CH], S.stage.bufs[i][:])
            S.stage.release(i, ev)
            if c == 0:
                U.i = ust.get(kb, "dve")
            kb.wait("dve", ev_mm)
            e2 = kb.mark("dve", nc.vector.tensor_copy(ust.bufs[U.i][:, :, c * 64:(c + 1) * 64],
                                                      bank[:].rearrange("p (c s) -> p s c", s=8)))
            if c == NCH - 1:
                kb.wait("sp", e2)
                for s_ in range(8):
                    ev2 = ust_slot[U.i].dma(kb, "sp", usend[s_ * 1024 + ot * 128:s_ * 1024 + (ot + 1) * 128, :], ust.bufs[U.i][:, s_, :].bitcast(F32))
                ust.release(U.i, ev2)
            return [e, e2]
        proj_fm(kb, S, win, KT, evac_u)
        all_gather(kb, usend, ugath)
    nc.all_engine_barrier()
    if stop == "A":
        return dbg_out(ugath[0:8192, :], [8192, 256])

    with ExitStack() as es:
        kb = KB(nc, es)
        idxu = kb.sb("idxu", [128, 64], I32)
        ev_idx = DmaSlot(kb, "idxu").dma(kb, "sp", idxu[:], idx_u)

        def u_load(g, Ub, slot):
            kb.wait("pool", ev_idx)
            for cp in range(8):
                ev = gather_rows(kb, slot, Ub[:, cp * 256:(cp + 1) * 256], ugath, idxu[:, g * 8 + cp:g * 8 + cp + 1])
            return ev

        def y_store(g, bi, yst, slot):
            dst = ysend.rearrange("(c r) x -> r c x", c=8)[g * 128:(g + 1) * 128, 2 * bi:2 * bi + 2, :]
            return slot.dma(kb, "sp", dst, yst[:].rearrange("p (c x) -> p c x", x=256))
        s5_scan(kb, dr, None, u_load, y_store)
        all_gather(kb, ysend, ygath)
    nc.all_engine_barrier()
    if stop == "B":
        return dbg_out(ysend, [8192, 256])

    with ExitStack() as es:
        kb = KB(nc, es)
        S = tok_state(kb, C_GAINS)
        S.ev_gains = S.misc.dma(kb, "sp", S.gains[:], gains[:, 16:16 + C_GAINS])
        load_hT(kb, S, hspill)
        idxy = kb.sb("idxy", [128, 64], I32)
        ev_idx = DmaSlot(kb, "idxy").dma(kb, "sp", idxy[:], idx_y)
        ybufs = S.act[:, 7:11, :].bitcast(F32)
        yring = Ring([ybufs[:, 2 * i:2 * i + 2, :].rearrange("p a (t c) -> p (a t) c", c=256) for i in range(2)])
        yslots = [DmaSlot(kb, "ygat") for _ in range(2)]
        g_ev = [None] * NCH
        for f in range(KT):
            yi = yring.get(kb, "pool")
            yb = yring.bufs[yi]
            kb.wait("pool", ev_idx, S.act_free)
            for t_ in range(8):
                ev_y = gather_rows(kb, yslots[yi], yb[:, t_, :], ygath, idxy[:, f * 8 + t_:f * 8 + t_ + 1])
            for c in range(NCH):
                cs = slice(c * CH, (c + 1) * CH)
                i = S.stage.get(kb, "sp")
                ub = S.stage.bufs[i]
                ev_u = S.stage_slot[i].dma(kb, "sp", ub[:], ukeep[f * 128:(f + 1) * 128, cs])
                kb.wait("dve", ev_u, ev_y, S.ev_gains)
                ubv = ub[:].rearrange("p (c t) -> p c t", t=8)
                yv = yb[:, :, c * 64:(c + 1) * 64].rearrange("p t c -> p c t")
                e1 = kb.mark("dve", nc.vector.scalar_tensor_tensor(ubv, ubv, S.gains[:, f:f + 1], yv, ALU.mult, ALU.add))
                kb.wait("act", e1, S.hn_free)
                e2 = kb.mark("act", nc.scalar.activation(S.hn[:, f, cs], ub[:], AF.Gelu_apprx_tanh))
                S.stage.release(i, e2)
                g_ev[c] = e2
            yring.release(yi, e1)
        s5_post_proj(kb, S, wglu, wout, g_ev)
        ffn(kb, S, *ffw[1], 8)
        ffn(kb, S, *ffw[2], 16)
        rmsnorm(kb, S, 24)
        store_hT(kb, S, hspill)
        qst = Ring([kb.sb("qst", [128, CH], BF16) for _ in range(3)])
        fz = dict(ring=qst, slots=[DmaSlot(kb, "qst") for _ in range(3)],
                  qk_dst=lambda ot, c: qsend[ot * 128:(ot + 1) * 128, c * CH:(c + 1) * CH],
                  v_dst=lambda tt, half: qsend[2048:3072, :].rearrange("(h p) x -> p h x", p=128)[:, half * 4:(half + 1) * 4, tt * 128:(tt + 1) * 128])
        qkv(kb, S, wqk, wv, pos, rmT, None, None, gcol=32, fused=fz)
        all_gather(kb, qsend, qgath)
    nc.all_engine_barrier()
    if stop == "C":
        return dbg_out(qsend, [3072, TOK], BF16)

    with ExitStack() as es:
        kb = KB(nc, es)
        idxq = kb.sb("idxq", [128, 24], I32)
        ev_idx = DmaSlot(kb, "idxq").dma(kb, "sp", idxq[:], idx_q)

        def loader(qT, kT, va, seg):
            evs = []
            kb.wait("pool", ev_idx)
            for cp in range(8):
                cs = slice(cp * TOK, (cp + 1) * TOK)
                gather_rows(kb, seg[cp], kT[:, cs], qgath, idxq[:, 8 + cp:9 + cp])
                gather_rows(kb, seg[cp], qT[:, cs], qgath, idxq[:, cp:cp + 1])
                evs.append(gather_rows(kb, seg[cp], va[:, cp * 16:(cp + 1) * 16, 0:128],
                                       qgath.rearrange("r (t d) -> r t d", d=128), idxq[:, 16 + cp:17 + cp]))
            return evs
        o_store = dict(ident=dr["ident"],
                       dst=lambda q0: osend[(q0 // TOK) * 128:(q0 // TOK + 1) * 128, q0 % TOK:q0 % TOK + 256])
        attention(kb, None, None, None, tri, None, loader, o_store)
        all_gather(kb, osend, ogath)
    nc.all_engine_barrier()
    if stop == "D":
        return dbg_out(osend, [1024, TOK], BF16)

    with ExitStack() as es:
        kb = KB(nc, es)
        S = tok_state(kb, 8)
        S.ev_gains = S.misc.dma(kb, "sp", S.gains[:], gains[:, 16 + C_GAINS:NG_ALL])
        load_hT(kb, S, hspill)
        idxo = kb.sb("idxo", [128, 8], I32)
        ev_idx = DmaSlot(kb, "idxo").dma(kb, "sp", idxo[:], idx_o)

        def o_gather(f, dst, slot):
            kb.wait("pool", ev_idx)
            return gather_rows(kb, slot, dst, ogath, idxo[:, f:f + 1])
        attn_out(kb, S, None, wo, o_gather)
        ffn(kb, S, *ffw[3], 0)
        store_hT(kb, S, outT)
        finish_all(kb, "sp")
    return nc


def fused_in_maps(inp):
    invf, rmT = rope_consts()
    gains = np.zeros((128, NG_ALL), np.float32)
    gains[:, 0:8] = col_layout(inp["ffn_norm"][0, 0])
    gains[:, 8:16] = col_layout(inp["mix_norm"][0])
    o = 16
    gains[:, o + 0:o + 8] = col_layout(inp["s5_d"][0])
    gains[:, o + 8:o + 16] = col_layout(inp["ffn_norm"][0, 1])
    gains[:, o + 16:o + 24] = col_layout(inp["ffn_norm"][1, 0])
    gains[:, o + 24:o + 32] = col_layout(inp["mix_norm"][1])
    gains[:, o + 32] = inp["moba_q_norm"][0]
    gains[:, o + 33] = inp["moba_k_norm"][0]
    gains[0:32, o + 34] = invf
    gains[:, 16 + C_GAINS:NG_ALL] = col_layout(inp["ffn_norm"][1, 1])
    wqkv = inp["moba_w_qkv"][0]
    common = dict(gains=gains, rmT=rmT, tri=np.triu(np.ones((128, 128), np.float32)),
                  win=tile_w(inp["s5_w_in"][0]), wglu=tile_w(inp["s5_w_glu"][0]), wout=tile_w(inp["s5_w_out"][0]),
                  wo=tile_w(inp["moba_w_out"][0]), wqk=tile_w(wqkv[:, 0:2048]),
                  wv=np.ascontiguousarray(wqkv[:, 2048:3072].reshape(KT, 128, D).transpose(1, 0, 2)))
    for i, (l, j) in enumerate(((0, 0), (0, 1), (1, 0), (1, 1))):
        common[f"wg{i}"] = tile_w(inp["ffn_w_gate"][l, j])
        common[f"wu{i}"] = tile_w(inp["ffn_w_up"][l, j])
        common[f"wd{i}"] = tile_wd(inp["ffn_w_down"][l, j])
    common.update(s5_consts())
    x = inp["x"][0]
    p = np.arange(128)
    maps = []
    for c in range(NCORES):
        m = dict(common)
        gs = slice(c * GPC, (c + 1) * GPC)
        m["xT"] = np.ascontiguousarray(x[c * TOK:(c + 1) * TOK].T)
        m["pos"] = np.ascontiguousarray(np.tile(np.arange(c * TOK, (c + 1) * TOK, dtype=np.float32)[None, :], (32, 1)))
        m["a_re"] = np.ascontiguousarray(inp["s5_a_re"][0][gs].T)
        m["a_im"] = np.ascontiguousarray(inp["s5_a_im"][0][gs].T)
        m["ldt"] = np.ascontiguousarray(np.tile(inp["s5_log_dt"][0][gs][None, :], (64, 1)))
        m["b_re"] = np.ascontiguousarray(inp["s5_b_re"][0][gs].transpose(1, 0, 2))
        m["b_im"] = np.ascontiguousarray(inp["s5_b_im"][0][gs].transpose(1, 0, 2))
        m["c_re"] = np.ascontiguousarray(inp["s5_c_re"][0][gs].transpose(2, 0, 1))
        m["c_im"] = np.ascontiguousarray(inp["s5_c_im"][0][gs].transpose(2, 0, 1))
        iu = np.zeros((128, 64), np.int32)
        iy = np.zeros((128, 64), np.int32)
        iq = np.zeros((128, 24), np.int32)
        io = np.zeros((128, 8), np.int32)
        for a in range(8):
            for b in range(8):
                iu[:, a * 8 + b] = b * 8192 + (p // 16) * 1024 + (8 * c + a) * 16 + (p % 16)
                iy[:, a * 8 + b] = a * 8192 + c * 1024 + (p // 16) * 128 + b * 16 + (p % 16)
            iq[:, a] = a * 3072 + c * 128 + p
            iq[:, 8 + a] = a * 3072 + (8 + c) * 128 + p
            iq[:, 16 + a] = a * 3072 + 2048 + c * 128 + p
            io[:, a] = a * 1024 + c * 128 + p
        m["idx_u"], m["idx_y"], m["idx_q"], m["idx_o"] = iu, iy, iq, io
        maps.append(m)
    return maps


def kernel_fused(inp, stop=None):
    nc = build_fused(stop)
    maps = [{k: v for k, v in m.items() if k in nc.declared} for m in fused_in_maps(inp)]
    res = _run(nc, maps)
    if stop is not None:
        return [r["dbg"] for r in res.results]
    return np.concatenate([r["outT"].T for r in res.results], axis=0)


def kernel(**inp):
    inp = {k: np.asarray(v, dtype=np.float32) for k, v in inp.items()}
    out = kernel_unfused(inp)
    return out.reshape(1, SEQ, D).astype(np.float32)
```

```python
import numpy as np
import concourse.bass as bass
import concourse.mybir as mybir
from concourse.bass_utils import run_bass_kernel_spmd
from contextlib import ExitStack

F32 = mybir.dt.float32
BF16 = mybir.dt.bfloat16
F32R = mybir.dt.float32r
AF = mybir.ActivationFunctionType
ALU = mybir.AluOpType
AX = mybir.AxisListType

NCORES = 8
SEQ = 16384
D = 1024
DFF = 2816
TOK = SEQ // NCORES
CH = 512
NCH = TOK // CH
KT = D // 128
FT = DFF // 128
FH = FT // 2
EPS = 1e-6


class KB:
    uid = [0]

    def __init__(self, nc, es):
        self.nc, self.es = nc, es
        self.E = dict(pe=nc.tensor, act=nc.scalar, dve=nc.vector, pool=nc.gpsimd, sp=nc.sync)
        self.psem = {}
        self.pcnt = {}
        self.waited = {}
        self.nsem = 0
        self.ntile = 0
        self.slots = []

    def newsem(self, name):
        self.nsem += 1
        KB.uid[0] += 1
        return self.es.enter_context(self.nc.semaphore(f"{name}_{KB.uid[0]}"))

    def sb(self, name, shape, dt=F32):
        KB.uid[0] += 1
        return self.es.enter_context(self.nc.sbuf_tensor(f"{name}_{KB.uid[0]}", list(shape), dt))

    def ps(self, name, shape, dt=F32):
        KB.uid[0] += 1
        return self.es.enter_context(self.nc.psum_tensor(f"{name}_{KB.uid[0]}", list(shape), dt))

    def mark(self, e, ins):
        if e not in self.psem:
            self.psem[e] = self.newsem("p_" + e)
            self.pcnt[e] = 0
        self.pcnt[e] += 1
        ins.then_inc(self.psem[e], 1)
        return (self.psem[e], self.pcnt[e], e)

    def wait(self, e, *evs):
        for ev in evs:
            if ev is None:
                continue
            if isinstance(ev, list):
                self.wait(e, *ev)
                continue
            sem, val, _src = ev
            key = (e, sem.name if hasattr(sem, "name") else id(sem))
            if self.waited.get(key, 0) >= val:
                continue
            self.waited[key] = val
            self.E[e].wait_ge(sem, val)


def finish_all(kb, eng):
    for sl in kb.slots:
        if sl.cnt:
            kb.E[eng].wait_ge(sl.sem, sl.cnt)


class DmaSlot:
    def __init__(self, kb, name):
        self.sem = kb.newsem("d_" + name)
        self.cnt = 0
        kb.slots.append(self)

    def dma(self, kb, q, out, in_):
        kb.E[q].dma_start(out=out, in_=in_).then_inc(self.sem, 16)
        self.cnt += 16
        return (self.sem, self.cnt, "dma")


class Ring:
    def __init__(self, bufs):
        self.bufs = bufs
        self.rel = [[] for _ in bufs]
        self.i = 0

    def get(self, kb, eng):
        i = self.i
        self.i = (i + 1) % len(self.bufs)
        kb.wait(eng, self.rel[i])
        self.rel[i] = []
        return i

    def release(self, i, *evs):
        self.rel[i].extend(evs)


class St:
    pass


def tile_w(w):
    K, N = w.shape
    return np.ascontiguousarray(w.reshape(K // 128, 128, N // 128, 128).transpose(2, 1, 0, 3))


def tile_wd(w):
    a = w.reshape(2, FH, 128, KT, 128)
    return np.ascontiguousarray(a.transpose(0, 3, 2, 1, 4))


def col_layout(v):
    return np.ascontiguousarray(v.reshape(-1, 128).T)


def tok_state(kb, n_gain_cols):
    nc = kb.nc
    S = St()
    S.hT = kb.sb("hT", [128, KT, TOK], F32)
    S.hn = kb.sb("hn", [128, KT, TOK], BF16)
    S.act = kb.sb("act", [128, FH, TOK], BF16)
    S.ones = kb.sb("ones", [128, 128], BF16)
    S.gains = kb.sb("gains", [128, n_gain_cols], F32)
    S.psum = Ring([kb.ps("bank", [128, CH], F32) for _ in range(8)])
    S.wgu = Ring([(kb.sb("wg", [128, KT, 128], BF16), kb.sb("wu", [128, KT, 128], BF16)) for _ in range(3)])
    S.wgu_slot = [DmaSlot(kb, "wgu") for _ in range(3)]
    S.wd = Ring([kb.sb("wd", [128, FH, 128], BF16) for _ in range(3)])
    S.wd_slot = [DmaSlot(kb, "wd") for _ in range(3)]
    S.sq = Ring([kb.sb("sq", [128, CH], BF16) for _ in range(3)])
    S.rt = Ring([kb.sb("rt", [128, CH], F32) for _ in range(2)])
    S.rstd = Ring([kb.sb("rstd", [128, CH], F32) for _ in range(2)])
    S.sg = Ring([kb.sb("sg", [128, CH], BF16) for _ in range(3)])
    S.stage = Ring([kb.sb("stage", [128, CH], F32) for _ in range(3)])
    S.stage_slot = [DmaSlot(kb, "stage") for _ in range(3)]
    S.h_ev = [None] * NCH
    S.hn_ev = [None] * NCH
    S.hn_free = None
    S.act_free = None
    S.misc = DmaSlot(kb, "misc")
    S.ev_ones = kb.mark("pool", nc.gpsimd.memset(S.ones[:], 1.0))
    return S


def rmsnorm(kb, S, gcol0):
    nc = kb.nc
    for c in range(NCH):
        cs = slice(c * CH, (c + 1) * CH)
        b = S.psum.get(kb, "pe")
        bank = S.psum.bufs[b]
        for f in range(KT):
            i = S.sq.get(kb, "act")
            kb.wait("act", S.h_ev[c])
            e = kb.mark("act", nc.scalar.activation(S.sq.bufs[i][:], S.hT[:, f, cs], AF.Square))
            kb.wait("pe", e, S.ev_ones)
            e2 = kb.mark("pe", nc.tensor.matmul(bank[:], S.ones[:], S.sq.bufs[i][:], start=(f == 0), stop=(f == KT - 1)))
            S.sq.release(i, e2)
        ri = S.rt.get(kb, "act")
        kb.wait("act", e2)
        e3 = kb.mark("act", nc.scalar.activation(S.rt.bufs[ri][:], bank[:], AF.Sqrt, bias=EPS, scale=1.0 / D))
        S.psum.release(b, e3)
        si = S.rstd.get(kb, "dve")
        kb.wait("dve", e3)
        e4 = kb.mark("dve", nc.vector.reciprocal(S.rstd.bufs[si][:], S.rt.bufs[ri][:]))
        S.rt.release(ri, e4)
        kb.wait("dve", S.hn_free, S.h_ev[c], e4, S.ev_gains)
        for f in range(KT):
            ins = nc.vector.scalar_tensor_tensor(S.hn[:, f, cs], S.hT[:, f, cs], S.gains[:, gcol0 + f:gcol0 + f + 1],
                                                 S.rstd.bufs[si][:], ALU.mult, ALU.mult)
        e5 = kb.mark("dve", ins)
        S.rstd.release(si, e5)
        S.hn_ev[c] = e5


def load_wtile(kb, S, ring, slots, pairs):
    i = ring.get(kb, "pool")
    ev = None
    bufs = ring.bufs[i]
    if not isinstance(bufs, tuple):
        bufs = (bufs,)
    for buf, src in zip(bufs, pairs):
        ev = slots[i].dma(kb, "pool", buf[:], src)
    return i, ev


def ffn(kb, S, wg, wu, wd, gcol0):
    nc = kb.nc
    rmsnorm(kb, S, gcol0)
    for half in range(2):
        act_ev = [[None] * NCH for _ in range(FH)]
        for fl in range(FH):
            ft = half * FH + fl
            wi, ev_w = load_wtile(kb, S, S.wgu, S.wgu_slot, (wg[ft], wu[ft]))
            wgt, wut = S.wgu.bufs[wi]
            for cp in range(2):
                banks = [S.psum.get(kb, "pe") for _ in range(4)]
                kb.wait("pe", ev_w)
                for mi, w in enumerate((wgt, wut)):
                    for k in range(KT):
                        for cc in range(2):
                            c = cp * 2 + cc
                            kb.wait("pe", S.hn_ev[c])
                            ins = nc.tensor.matmul(S.psum.bufs[banks[mi * 2 + cc]][:], w[:, k, :],
                                                   S.hn[:, k, c * CH:(c + 1) * CH], start=(k == 0), stop=(k == KT - 1))
                ev_mm = kb.mark("pe", ins)
                S.hn_free = ev_mm
                if cp == 1:
                    S.wgu.release(wi, ev_mm)
                for cc in range(2):
                    c = cp * 2 + cc
                    ti = S.sg.get(kb, "act")
                    kb.wait("act", ev_mm)
                    e1 = kb.mark("act", nc.scalar.activation(S.sg.bufs[ti][:], S.psum.bufs[banks[cc]][:], AF.Silu))
                    kb.wait("dve", e1, ev_mm, S.act_free)
                    e2 = kb.mark("dve", nc.vector.tensor_tensor(S.act[:, fl, c * CH:(c + 1) * CH], S.sg.bufs[ti][:],
                                                                S.psum.bufs[banks[2 + cc]][:], ALU.mult))
                    S.sg.release(ti, e2)
                    S.psum.release(banks[cc], e1)
                    S.psum.release(banks[2 + cc], e2)
                    act_ev[fl][c] = e2
        for dm in range(KT):
            wi, ev_w = load_wtile(kb, S, S.wd, S.wd_slot, (wd[half, dm],))
            wdt = S.wd.bufs[wi]
            for c in range(NCH):
                cs = slice(c * CH, (c + 1) * CH)
                b = S.psum.get(kb, "pe")
                kb.wait("pe", ev_w)
                for t in range(FH):
                    kb.wait("pe", act_ev[t][c])
                    ins = nc.tensor.matmul(S.psum.bufs[b][:], wdt[:, t, :], S.act[:, t, cs], start=(t == 0), stop=(t == FH - 1))
                ev = kb.mark("pe", ins)
                kb.wait("dve", ev, S.h_ev[c])
                e3 = kb.mark("dve", nc.vector.scalar_tensor_tensor(S.hT[:, dm, cs], S.psum.bufs[b][:], 0.5, S.hT[:, dm, cs],
                                                                   ALU.mult, ALU.add))
                S.psum.release(b, e3)
                S.h_ev[c] = e3
            S.wd.release(wi, ev)
            S.act_free = ev


def proj_fm(kb, S, wt, n_tiles, evac, opnd=None, opnd_ev=None, t0=0, defer=False, ring=None, evac2=None):
    nc = kb.nc
    if opnd is None:
        opnd, opnd_ev = S.hn, S.hn_ev
    pend = []
    pend2 = []
    ring = ring or S.psum

    def run_pending(keep1, keep2):
        while len(pend) > keep1:
            pot, pcp, pbanks, pev = pend.pop(0)
            ctxs = []
            for cc in range(2):
                rel = evac(pot, pcp * 2 + cc, ring.bufs[pbanks[cc]], pev)
                if evac2 is not None:
                    rel, ctx = rel
                    ctxs.append(ctx)
                ring.release(pbanks[cc], rel)
            if evac2 is not None:
                pend2.append(ctxs)
            while len(pend2) > keep2:
                for ctx in pend2.pop(0):
                    evac2(ctx)
        while len(pend2) > keep2:
            for ctx in pend2.pop(0):
                evac2(ctx)

    for ot in range(t0, t0 + n_tiles):
        wi, ev_w = load_wtile(kb, S, S.wgu, S.wgu_slot, (wt[ot],))
        w = S.wgu.bufs[wi][0]
        for cp in range(2):
            banks = [ring.get(kb, "pe") for _ in range(2)]
            kb.wait("pe", ev_w)
            for k in range(KT):
                for cc in range(2):
                    c = cp * 2 + cc
                    kb.wait("pe", opnd_ev[c])
                    ins = nc.tensor.matmul(ring.bufs[banks[cc]][:], w[:, k, :], opnd[:, k, c * CH:(c + 1) * CH],
                                           start=(k == 0), stop=(k == KT - 1))
            ev_mm = kb.mark("pe", ins)
            if opnd is S.hn:
                S.hn_free = ev_mm
            else:
                S.act_free = ev_mm
            if cp == 1:
                S.wgu.release(wi, ev_mm)
            pend.append((ot, cp, banks, ev_mm))
            run_pending(1 if defer else 0, 1)
    run_pending(0, 0)


def store_fm(kb, S, out_dram, eng="act"):
    nc = kb.nc

    def evac(ot, c, bank, ev_mm):
        i = S.stage.get(kb, eng)
        kb.wait(eng, ev_mm)
        if eng == "act":
            e = kb.mark("act", nc.scalar.copy(S.stage.bufs[i][:], bank[:]))
        else:
            e = kb.mark("dve", nc.vector.tensor_copy(S.stage.bufs[i][:], bank[:]))
        kb.wait("sp", e)
        ev = S.stage_slot[i].dma(kb, "sp", out_dram[ot * 128:(ot + 1) * 128, c * CH:(c + 1) * CH], S.stage.bufs[i][:])
        S.stage.release(i, ev)
        return e
    return evac


def load_hT(kb, S, xT):
    src = xT.rearrange("(f p) t -> p f t", p=128)
    for c in range(NCH):
        cs = slice(c * CH, (c + 1) * CH)
        S.h_ev[c] = DmaSlot(kb, "hload").dma(kb, "sp", S.hT[:, :, cs], src[:, :, cs])


def store_hT(kb, S, outT):
    kb.wait("sp", [e for e in S.h_ev])
    evs = []
    for f in range(KT):
        evs.append(S.misc.dma(kb, "sp", outT[f * 128:(f + 1) * 128, :], S.hT[:, f, :]))
    return evs[-1]


def finish(kb, S):
    for sl in S.stage_slot + [S.misc]:
        if sl.cnt:
            kb.E["sp"].wait_ge(sl.sem, sl.cnt)


def build_phase_a():
    nc = bass.Bass("TRN2", target_bir_lowering=False)
    xT = nc.dram_tensor("xT", [D, TOK], F32, kind="ExternalInput").ap()
    gains = nc.dram_tensor("gains", [128, 16], F32, kind="ExternalInput").ap()
    wg = nc.dram_tensor("wg", [FT, 128, KT, 128], F32, kind="ExternalInput").ap()
    wu = nc.dram_tensor("wu", [FT, 128, KT, 128], F32, kind="ExternalInput").ap()
    wd = nc.dram_tensor("wd", [2, KT, 128, FH, 128], F32, kind="ExternalInput").ap()
    win = nc.dram_tensor("win", [KT, 128, KT, 128], F32, kind="ExternalInput").ap()
    h1T = nc.dram_tensor("h1T", [D, TOK], F32, kind="ExternalOutput").ap()
    uT = nc.dram_tensor("uT", [D, TOK], F32, kind="ExternalOutput").ap()
    with ExitStack() as es:
        kb = KB(nc, es)
        S = tok_state(kb, 16)
        S.ev_gains = S.misc.dma(kb, "sp", S.gains[:], gains)
        load_hT(kb, S, xT)
        ffn(kb, S, wg, wu, wd, 0)
        store_hT(kb, S, h1T)
        rmsnorm(kb, S, 8)
        proj_fm(kb, S, win, KT, store_fm(kb, S, uT))
        finish(kb, S)
    return nc


TRACE = False
LAST_NS = []


def _run(nc, in_maps):
    res = run_bass_kernel_spmd(nc, in_maps, core_ids=list(range(NCORES)), trace=TRACE)
    if TRACE:
        LAST_NS.append(res.exec_time_ns)
        LAST_NS.append(res.per_core_scope_times)
    return res


def run_phase_a(inp):
    nc = build_phase_a()
    x = inp["x"][0]
    gains = np.concatenate([col_layout(inp["ffn_norm"][0, 0]), col_layout(inp["mix_norm"][0])], axis=1)
    common = dict(gains=gains, wg=tile_w(inp["ffn_w_gate"][0, 0]), wu=tile_w(inp["ffn_w_up"][0, 0]),
                  wd=tile_wd(inp["ffn_w_down"][0, 0]), win=tile_w(inp["s5_w_in"][0]))
    in_maps = []
    for c in range(NCORES):
        m = dict(common)
        m["xT"] = np.ascontiguousarray(x[c * TOK:(c + 1) * TOK].T)
        in_maps.append(m)
    res = _run(nc, in_maps)
    h1 = np.concatenate([r["h1T"].T for r in res.results], axis=0)
    u = np.concatenate([r["uT"].T for r in res.results], axis=0)
    return h1, u


NBLK = SEQ // 256
NSEG = 8
SEGT = SEQ // NSEG
BIG = 1.0e30


def build_phase_d():
    nc = bass.Bass("TRN2", target_bir_lowering=False)
    qT_d = nc.dram_tensor("qT", [128, SEQ], F32, kind="ExternalInput").ap()
    kT_d = nc.dram_tensor("kT", [128, SEQ], F32, kind="ExternalInput").ap()
    v_d = nc.dram_tensor("v", [SEQ, 128], F32, kind="ExternalInput").ap()
    tri_d = nc.dram_tensor("tri", [128, 128], F32, kind="ExternalInput").ap()
    o_d = nc.dram_tensor("o", [SEQ, 128], F32, kind="ExternalOutput").ap()
    with ExitStack() as es:
        kb = KB(nc, es)
        attention(kb, qT_d, kT_d, v_d, tri_d, o_d)
    return nc


def attention(kb, qT_d, kT_d, v_d, tri_d, o_d, loader=None, o_store=None):
    nc = kb.nc
    qT = kb.sb("qT", [128, SEQ], BF16)
    kT = kb.sb("kT", [128, SEQ], BF16)
    va = kb.sb("va", [128, SEQ // 128, 130], BF16)
    tri = kb.sb("tri", [128, 128], BF16)
    kms = kb.sb("kms", [128, NBLK], F32)
    kmT = kb.sb("kmT", [128, NBLK], BF16)
    gate = Ring([kb.sb("gate", [128, 2, NBLK], F32) for _ in range(2)])
    m8 = Ring([kb.sb("m8", [128, 2, 8], F32) for _ in range(2)])
    sel = Ring([kb.sb("sel", [128, 2, NBLK], F32) for _ in range(2)])
    pT = Ring([kb.sb("pT", [128, 1024], BF16) for _ in range(3)])
    oacc = Ring([kb.sb("oacc", [128, 2, 130], F32) for _ in range(2)])
    rec = Ring([kb.sb("rec", [128, 2], F32) for _ in range(2)])
    ostage = Ring([kb.sb("ostage", [128, 2, 128], F32) for _ in range(2)])
    ostage_slot = [DmaSlot(kb, "ost") for _ in range(2)]
    psS = Ring([kb.ps("psS", [128, 1024], F32) for _ in range(2)])
    psO = Ring([kb.ps("psO", [128, 512], F32) for _ in range(3)])
    psG = Ring([kb.ps("psG", [128, 512], F32) for _ in range(1)])
    seg = [DmaSlot(kb, "seg") for _ in range(NSEG)]
    cslot = DmaSlot(kb, "const")

    ev_tri = cslot.dma(kb, "pool", tri[:], tri_d)
    ev_ones = kb.mark("pool", nc.gpsimd.memset(va[:, :, 128:130], 1.0))
    ev_ginit = None
    for g in gate.bufs:
        ev_ginit = kb.mark("pool", nc.gpsimd.memset(g[:], -BIG))
    seg_ev = []
    if loader is not None:
        seg_ev = loader(qT, kT, va, seg)
    else:
        v_t = v_d.rearrange("(t p) d -> p t d", p=128)
        for s in range(NSEG):
            cs = slice(s * SEGT, (s + 1) * SEGT)
            ts = slice(s * SEGT // 128, (s + 1) * SEGT // 128)
            seg[s].dma(kb, "pool", kT[:, cs], kT_d[:, cs])
            seg[s].dma(kb, "pool", qT[:, cs], qT_d[:, cs])
            seg_ev.append(seg[s].dma(kb, "pool", va[:, ts, 0:128], v_t[:, ts, :]))
    if o_store is not None:
        oTb = Ring([kb.sb("oTb", [128, 256], BF16) for _ in range(2)])
        oT_slot = [DmaSlot(kb, "oTb") for _ in range(2)]
        identA = kb.sb("identA", [128, 128], F32)
        ev_identA = DmaSlot(kb, "identA").dma(kb, "sp", identA[:], o_store["ident"])
    km_ev = []
    bps = NBLK // NSEG
    for s in range(NSEG):
        kb.wait("dve", seg_ev[s])
        e = kb.mark("dve", nc.vector.tensor_reduce(kms[:, s * bps:(s + 1) * bps],
                                                   kT[:, s * SEGT:(s + 1) * SEGT].rearrange("p (b k) -> p b k", k=256),
                                                   AX.X, ALU.add))
        kb.wait("dve", e)
        km_ev.append(kb.mark("dve", nc.vector.tensor_scalar(kmT[:, s * bps:(s + 1) * bps], kms[:, s * bps:(s + 1) * bps],
                                                            1.0 / 256, None, ALU.mult)))

    SKEW = 1
    tasks = []
    for j in range(NBLK):
        P = St()
        P.j, P.q0 = j, 256 * j
        P.si = P.ev_sel = P.ai = P.ev_acc = None
        tasks.append((P, -1))
        for b in range(0, j, 2):
            tasks.append((P, list(range(b, min(b + 2, j)))))

    def emit_gate(P):
        j, q0 = P.j, P.q0
        gb = psG.get(kb, "pe")
        kb.wait("pe", km_ev[(j - 1) // bps])
        for cq in range(2):
            ins = nc.tensor.matmul(psG.bufs[gb][:, cq * 64:cq * 64 + j], qT[:, q0 + cq * 128:q0 + (cq + 1) * 128],
                                   kmT[:, 0:j], start=True, stop=True)
        ev_g = kb.mark("pe", ins)
        gi = gate.get(kb, "dve")
        kb.wait("dve", ev_g, ev_ginit)
        eg = kb.mark("dve", nc.vector.tensor_copy(gate.bufs[gi][:, :, 0:j],
                                                  psG.bufs[gb][:, 0:128].rearrange("p (c b) -> p c b", b=64)[:, :, 0:j]))
        psG.release(gb, eg)
        mi = m8.get(kb, "dve")
        kb.wait("dve", eg)
        for cq in range(2):
            em = kb.mark("dve", nc.vector.max(m8.bufs[mi][:, cq, :], gate.bufs[gi][:, cq, :]))
        P.si = sel.get(kb, "dve")
        kb.wait("dve", em)
        for cq in range(2):
            P.ev_sel = kb.mark("dve", nc.vector.tensor_scalar(sel.bufs[P.si][:, cq, 0:j], gate.bufs[gi][:, cq, 0:j],
                                                              m8.bufs[mi][:, cq, 2:3], None, ALU.is_ge))
        gate.release(gi, P.ev_sel)
        m8.release(mi, P.ev_sel)

    def emit_s(P, b, T):
        j, q0 = P.j, P.q0
        kb.wait("pe", seg_ev[q0 // SEGT])
        qs = qT[:, q0:q0 + 256]
        sb_ = psS.get(kb, "pe")
        S_ = psS.bufs[sb_]
        T.pi = pT.get(kb, "act")
        P_ = pT.bufs[T.pi]
        if b == -1:
            if j >= 1:
                emit_gate(P)
            tA, tB = 2 * j, 2 * j + 1
            nc.tensor.matmul(S_[:, 0:256], kT[:, tA * 128:(tA + 1) * 128], qs, start=True, stop=True)
            ev_s = kb.mark("pe", nc.tensor.matmul(S_[:, 256:384], kT[:, tB * 128:(tB + 1) * 128], qT[:, q0 + 128:q0 + 256],
                                                  start=True, stop=True))
            kb.wait("act", ev_s)
            ev_p = kb.mark("act", nc.scalar.activation(P_[:, 0:384], S_[:, 0:384], AF.Exp))
            psS.release(sb_, ev_p)
            kb.wait("pool", ev_p, ev_tri)
            nc.gpsimd.tensor_tensor(P_[:, 0:128], P_[:, 0:128], tri[:], ALU.mult)
            T.ev_p = kb.mark("pool", nc.gpsimd.tensor_tensor(P_[:, 256:384], P_[:, 256:384], tri[:], ALU.mult))
        else:
            for i, bb in enumerate(b):
                nc.tensor.matmul(S_[:, i * 512:i * 512 + 256], kT[:, (2 * bb) * 128:(2 * bb + 1) * 128], qs, start=True, stop=True)
                ins = nc.tensor.matmul(S_[:, i * 512 + 256:i * 512 + 512], kT[:, (2 * bb + 1) * 128:(2 * bb + 2) * 128], qs,
                                       start=True, stop=True)
            ev_s = kb.mark("pe", ins)
            kb.wait("act", ev_s)
            w = 512 * len(b)
            T.ev_p = kb.mark("act", nc.scalar.activation(P_[:, 0:w], S_[:, 0:w], AF.Exp))
            psS.release(sb_, T.ev_p)

    def emit_rest(P, b, T):
        j, q0 = P.j, P.q0
        P_ = pT.bufs[T.pi]
        if b == -1:
            ob = psO.get(kb, "pe")
            O_ = psO.bufs[ob]
            kb.wait("pe", T.ev_p, ev_ones)
            tA, tB = 2 * j, 2 * j + 1
            nc.tensor.matmul(O_[:, 0:129], P_[:, 0:128], va[:, tA, 0:129], start=True, stop=True)
            nc.tensor.matmul(O_[:, 256:385], P_[:, 128:256], va[:, tA, 0:129], start=True, stop=False)
            ev_o = kb.mark("pe", nc.tensor.matmul(O_[:, 256:385], P_[:, 256:384], va[:, tB, 0:129], start=False, stop=True))
            pT.release(T.pi, ev_o)
            P.ai = oacc.get(kb, "dve")
            A_ = oacc.bufs[P.ai]
            kb.wait("dve", ev_o)
            P.ev_acc = kb.mark("dve", nc.vector.tensor_copy(A_[:, :, 0:129],
                                                            O_[:].rearrange("p (c x) -> p c x", x=256)[:, :, 0:129]))
            psO.release(ob, P.ev_acc)
            last = (j == 0)
        else:
            A_ = oacc.bufs[P.ai]
            kb.wait("pe", T.ev_p, ev_ones)
            for i, bb in enumerate(b):
                ob = psO.get(kb, "pe")
                O_ = psO.bufs[ob]
                o = i * 512
                nc.tensor.matmul(O_[:, 0:129], P_[:, o:o + 128], va[:, 2 * bb, 0:129], start=True, stop=False)
                nc.tensor.matmul(O_[:, 0:129], P_[:, o + 256:o + 384], va[:, 2 * bb + 1, 0:129], start=False, stop=True)
                nc.tensor.matmul(O_[:, 256:385], P_[:, o + 128:o + 256], va[:, 2 * bb, 0:129], start=True, stop=False)
                ev_o = kb.mark("pe", nc.tensor.matmul(O_[:, 256:385], P_[:, o + 384:o + 512], va[:, 2 * bb + 1, 0:129], start=False, stop=True))
                kb.wait("dve", ev_o, P.ev_sel, P.ev_acc)
                for cq in range(2):
                    P.ev_acc = kb.mark("dve", nc.vector.scalar_tensor_tensor(A_[:, cq, 0:129], O_[:, cq * 256:cq * 256 + 129],
                                                                             sel.bufs[P.si][:, cq, bb:bb + 1], A_[:, cq, 0:129],
                                                                             ALU.mult, ALU.add))
                psO.release(ob, P.ev_acc)
            pT.release(T.pi, ev_o)
            last = (b[-1] == j - 1)
        if last:
            if P.si is not None:
                sel.release(P.si, P.ev_acc)
            ri = rec.get(kb, "dve")
            kb.wait("dve", P.ev_acc)
            er = kb.mark("dve", nc.vector.reciprocal(rec.bufs[ri][:], A_[:, :, 128]))
            oi = ostage.get(kb, "dve")
            kb.wait("dve", er)
            for cq in range(2):
                eo = kb.mark("dve", nc.vector.tensor_scalar(ostage.bufs[oi][:, cq, :], A_[:, cq, 0:128],
                                                            rec.bufs[ri][:, cq:cq + 1], None, ALU.mult))
            oacc.release(P.ai, eo)
            rec.release(ri, eo)
            if o_store is not None:
                tb = psG.get(kb, "pe")
                kb.wait("pe", eo, ev_identA)
                for cq in range(2):
                    ins = nc.tensor.transpose(psG.bufs[tb][:, cq * 128:(cq + 1) * 128], ostage.bufs[oi][:, cq, :], identA[:])
                et = kb.mark("pe", ins)
                ostage.release(oi, et)
                ti = oTb.get(kb, "act")
                kb.wait("act", et)
                ec = kb.mark("act", nc.scalar.copy(oTb.bufs[ti][:], psG.bufs[tb][:, 0:256]))
                psG.release(tb, ec)
                kb.wait("sp", ec)
                ev = oT_slot[ti].dma(kb, "sp", o_store["dst"](q0), oTb.bufs[ti][:])
                oTb.release(ti, ev)
            else:
                kb.wait("sp", eo)
                ev = ostage_slot[oi].dma(kb, "sp", o_d[q0:q0 + 256, :].rearrange("(c p) d -> p c d", p=128), ostage.bufs[oi][:])
                ostage.release(oi, ev)

    tst = [St() for _ in tasks]
    for n in range(len(tasks) + SKEW):
        if n < len(tasks):
            P, b = tasks[n]
            emit_s(P, b, tst[n])
        if n >= SKEW:
            P, b = tasks[n - SKEW]
            emit_rest(P, b, tst[n - SKEW])
    for sl in ostage_slot + (oT_slot if o_store is not None else []):
        if sl.cnt:
            kb.E["sp"].wait_ge(sl.sem, sl.cnt)


def run_phase_d(q, k, v):
    nc = build_phase_d()
    tri = np.triu(np.ones((128, 128), np.float32))
    in_maps = []
    for h in range(NCORES):
        in_maps.append(dict(qT=np.ascontiguousarray(q[:, h, :].T), kT=np.ascontiguousarray(k[:, h, :].T),
                            v=np.ascontiguousarray(v[:, h, :]), tri=tri))
    res = _run(nc, in_maps)
    return np.concatenate([r["o"] for r in res.results], axis=1)


GPC = 8
NCK = SEQ // 8
NLV = 11
NKV = 24
SW = 384 + 2 * NLV
TWO_PI = 6.283185307179586
MAGIC = 12582912.0


def build_phase_b():
    nc = bass.Bass("TRN2", target_bir_lowering=False)
    dr = {}
    for n, shp in (("U", [GPC, 128, NCK]), ("a_re", [64, GPC]), ("a_im", [64, GPC]), ("ldt", [64, GPC]),
                   ("b_re", [64, GPC, 16]), ("b_im", [64, GPC, 16]), ("c_re", [64, GPC, 16]), ("c_im", [64, GPC, 16]),
                   ("kvec", [64, NKV]), ("selLo", [64, 128]), ("selHi", [64, 128]), ("ident", [128, 128]),
                   ("swap", [128, 128]), ("tmask", [128, 128])):
        dr[n] = nc.dram_tensor(n, shp, F32, kind="ExternalInput").ap()
    Y = nc.dram_tensor("Y", [GPC, 128, NCK], F32, kind="ExternalOutput").ap()
    with ExitStack() as es:
        kb = KB(nc, es)
        s5_scan(kb, dr, Y)
    return nc


def s5_scan(kb, dr, Y, u_load=None, y_store=None):
    nc = kb.nc
    G = GPC
    misc = DmaSlot(kb, "s5misc")
    P = {}
    ev_in = None
    for n in ("a_re", "a_im", "ldt", "b_re", "b_im", "c_re", "c_im", "kvec", "selLo", "selHi", "ident", "swap", "tmask"):
        shp = list(dr[n].shape)
        P[n] = kb.sb("p_" + n, shp, F32)
        ev_in = misc.dma(kb, "sp", P[n][:], dr[n])
    st = St()
    st.last = ev_in

    def V(fn, *a, **k):
        kb.wait("dve", st.last)
        st.last = kb.mark("dve", fn(*a, **k))

    def A(*a, **k):
        kb.wait("act", st.last)
        st.last = kb.mark("act", nc.scalar.activation(*a, **k))

    vt, vs, stt = nc.vector.tensor_tensor, nc.vector.tensor_scalar, nc.vector.scalar_tensor_tensor
    t = lambda n, shp: kb.sb("s5_" + n, shp, F32)
    dt, ar, th = t("dt", [64, G]), t("ar", [64, G]), t("th", [64, G])
    kar, kth, mag = t("kar", [64, G, NKV]), t("kth", [64, G, NKV]), t("mag", [64, G, NKV])
    r1, r2, sn, cs = t("r1", [64, G, NKV]), t("r2", [64, G, NKV]), t("sn", [64, G, NKV]), t("cs", [64, G, NKV])
    pr, pi_ = t("pr", [64, G, NKV]), t("pi", [64, G, NKV])
    A("dummy" and dt[:], P["ldt"][:], AF.Exp)
    V(vt, ar[:], P["a_re"][:], dt[:], ALU.mult)
    V(vt, th[:], P["a_im"][:], dt[:], ALU.mult)
    kv_b = P["kvec"][:].unsqueeze(1).to_broadcast([64, G, NKV])
    V(vt, kar[:], ar[:].unsqueeze(2).to_broadcast([64, G, NKV]), kv_b, ALU.mult)
    V(vt, kth[:], th[:].unsqueeze(2).to_broadcast([64, G, NKV]), kv_b, ALU.mult)
    A(mag[:], kar[:], AF.Exp)
    for off, dst in ((0.0, sn), (TWO_PI / 4, cs)):
        V(vs, r1[:], kth[:], off, 1.0 / TWO_PI, ALU.add, ALU.mult)
        V(vs, r2[:], r1[:], MAGIC, MAGIC, ALU.add, ALU.subtract)
        V(vs, r1[:], kth[:], off, None, ALU.add)
        V(stt, r1[:], r2[:], -TWO_PI, r1[:], ALU.mult, ALU.add)
        V(vs, r1[:], r1[:], -3.14159265, 3.14159265, ALU.max, ALU.min)
        A(dst[:], r1[:], AF.Sin)
    V(vt, pr[:], mag[:], cs[:], ALU.mult)
    V(vt, pi_[:], mag[:], sn[:], ALU.mult)
    lam_re, lam_im = pr[:, :, 16], pi_[:, :, 16]
    den, rden, nre, t1, t2, cre, cim = (t(n, [64, G]) for n in ("den", "rden", "nre", "t1", "t2", "cre", "cim"))
    V(vt, t1[:], P["a_re"][:], P["a_re"][:], ALU.mult)
    V(vt, den[:], P["a_im"][:], P["a_im"][:], ALU.mult)
    V(vt, den[:], den[:], t1[:], ALU.add)
    V(nc.vector.reciprocal, rden[:], den[:])
    V(vs, nre[:], lam_re, -1.0, None, ALU.add)
    V(vt, t1[:], nre[:], P["a_re"][:], ALU.mult)
    V(vt, t2[:], lam_im, P["a_im"][:], ALU.mult)
    V(vt, t1[:], t1[:], t2[:], ALU.add)
    V(vt, cre[:], t1[:], rden[:], ALU.mult)
    V(vt, t1[:], lam_im, P["a_re"][:], ALU.mult)
    V(vt, t2[:], nre[:], P["a_im"][:], ALU.mult)
    V(vt, t1[:], t1[:], t2[:], ALU.subtract)
    V(vt, cim[:], t1[:], rden[:], ALU.mult)
    bbr, bbi, u1, u2 = (t(n, [64, G, 16]) for n in ("bbr", "bbi", "u1", "u2"))
    cre_b = cre[:].unsqueeze(2).to_broadcast([64, G, 16])
    cim_b = cim[:].unsqueeze(2).to_broadcast([64, G, 16])
    V(vt, u1[:], P["b_re"][:], cre_b, ALU.mult)
    V(vt, u2[:], P["b_im"][:], cim_b, ALU.mult)
    V(vt, bbr[:], u1[:], u2[:], ALU.subtract)
    V(vt, u1[:], P["b_im"][:], cre_b, ALU.mult)
    V(vt, u2[:], P["b_re"][:], cim_b, ALU.mult)
    V(vt, bbi[:], u1[:], u2[:], ALU.add)
    LO, HI = t("LO", [64, G, SW]), t("HI", [64, G, SW])
    w1, w2 = t("w1", [64, G, 8, 16]), t("w2", [64, G, 8, 16])
    shp4 = [64, G, 8, 16]

    def cplx(k0, xr, xi, out_r, out_i, neg_im=False):
        pwr = pr[:, :, k0:k0 + 8].unsqueeze(3).to_broadcast(shp4)
        pwi = pi_[:, :, k0:k0 + 8].unsqueeze(3).to_broadcast(shp4)
        xr_b = xr[:].unsqueeze(2).to_broadcast(shp4)
        xi_b = xi[:].unsqueeze(2).to_broadcast(shp4)
        V(vt, w1[:], pwr, xr_b, ALU.mult)
        V(vt, w2[:], pwi, xi_b, ALU.mult)
        V(vt, out_r, w1[:], w2[:], ALU.subtract)
        V(vt, w1[:], pwr, xi_b, ALU.mult)
        V(vt, w2[:], pwi, xr_b, ALU.mult)
        if neg_im:
            V(vt, w1[:], w1[:], w2[:], ALU.add)
            V(vs, out_i, w1[:], -1.0, None, ALU.mult)
        else:
            V(vt, out_i, w1[:], w2[:], ALU.add)

    v4 = lambda T_, c0: T_[:, :, c0:c0 + 128].rearrange("p g (s m) -> p g s m", m=16)
    cplx(0, bbr, bbi, v4(LO, 0), v4(HI, 0))
    cplx(8, bbr, bbi, v4(LO, 128), v4(HI, 128))
    cplx(16, P["c_re"], P["c_im"], v4(LO, 256), v4(HI, 256), neg_im=True)
    qr, qi, q1, q2 = t("qr", [64, G, NLV]), t("qi", [64, G, NLV]), t("q1", [64, G]), t("q2", [64, G])
    V(nc.vector.tensor_copy, qr[:, :, 0], pr[:, :, 23])
    V(nc.vector.tensor_copy, qi[:, :, 0], pi_[:, :, 23])
    for j in range(NLV - 1):
        V(vt, q1[:], qr[:, :, j], qr[:, :, j], ALU.mult)
        V(vt, q2[:], qi[:, :, j], qi[:, :, j], ALU.mult)
        V(vt, qr[:, :, j + 1], q1[:], q2[:], ALU.subtract)
        V(stt, qi[:, :, j + 1], qr[:, :, j], 2.0, qi[:, :, j], ALU.mult, ALU.mult)
    V(nc.vector.tensor_copy, LO[:, :, 384:384 + NLV], qr[:])
    V(nc.vector.tensor_copy, HI[:, :, 384:384 + NLV], qr[:])
    V(nc.vector.tensor_copy, LO[:, :, 384 + NLV:SW], qi[:])
    V(vs, HI[:, :, 384 + NLV:SW], qi[:], -1.0, None, ALU.mult)
    ev_setup = st.last

    psum = Ring([kb.ps("s5bank", [128, 512], F32) for _ in range(8)])
    stk = [kb.sb("stk", [128, SW], F32) for _ in range(G)]
    BzT = [kb.sb("BzT", [128, 128], F32R) for _ in range(G)]
    TzT = [kb.sb("TzT", [128, 128], F32R) for _ in range(G)]
    CzT = [kb.sb("CzT", [128, 128], F32R) for _ in range(G)]
    AT = [kb.sb("AT", [128, NLV, 128], F32R) for _ in range(G)]
    atmp = Ring([kb.sb("atmp", [128, 128], F32) for _ in range(2)])
    mat_ev = [None] * G
    for g in range(G):
        b = psum.get(kb, "pe")
        kb.wait("pe", ev_setup)
        nc.tensor.matmul(psum.bufs[b][:, 0:SW], P["selLo"][:], LO[:, g, :], start=True, stop=False)
        e = kb.mark("pe", nc.tensor.matmul(psum.bufs[b][:, 0:SW], P["selHi"][:], HI[:, g, :], start=False, stop=True))
        kb.wait("act", e)
        e_stk = kb.mark("act", nc.scalar.copy(stk[g][:], psum.bufs[b][:, 0:SW]))
        psum.release(b, e_stk)
        b = psum.get(kb, "pe")
        kb.wait("pe", e_stk)
        nc.tensor.transpose(psum.bufs[b][:, 0:128], stk[g][:, 0:128], P["ident"][:])
        e = kb.mark("pe", nc.tensor.matmul(psum.bufs[b][:, 128:256], stk[g][:, 128:256], stk[g][:, 256:384], start=True, stop=True))
        kb.wait("act", e)
        e1 = kb.mark("act", nc.scalar.copy(BzT[g][:], psum.bufs[b][:, 0:128]))
        kb.wait("dve", e, e_stk)
        e2 = kb.mark("dve", nc.vector.tensor_tensor(TzT[g][:], psum.bufs[b][:, 128:256], P["tmask"][:], ALU.mult))
        psum.release(b, e1, e2)
        e3 = kb.mark("act", nc.scalar.copy(CzT[g][:], stk[g][:, 256:384]))
        for j in range(NLV):
            ti = atmp.get(kb, "dve")
            e4 = kb.mark("dve", nc.vector.tensor_scalar(atmp.bufs[ti][:], P["ident"][:], stk[g][:, 384 + j:385 + j], None, ALU.mult))
            kb.wait("dve", e4)
            e5 = kb.mark("dve", nc.vector.scalar_tensor_tensor(AT[g][:, j, :], P["swap"][:], stk[g][:, 384 + NLV + j:385 + NLV + j],
                                                               atmp.bufs[ti][:], ALU.mult, ALU.add))
            atmp.release(ti, e5)
        mat_ev[g] = [e1, e2, e3, e5]

    Ubuf = Ring([kb.sb("Ubuf", [128, NCK], F32R) for _ in range(3)])
    Uslot = [DmaSlot(kb, "U") for _ in range(3)]
    Xb = [kb.sb("Xb", [128, 2 + NCK], F32R) for _ in range(2)]
    ystage = Ring([kb.sb("ystage", [128, 512], F32) for _ in range(3)])
    yslot = [DmaSlot(kb, "y") for _ in range(3)]
    x_free = [None, None]
    zt = kb.sb("zt", [128, 2], F32)
    ez = kb.mark("pool", nc.gpsimd.memset(zt[:], 0.0))
    kb.wait("pool", ez)
    for xb in Xb:
        ev_pad = kb.mark("pool", nc.gpsimd.tensor_copy(xb[:, 0:2], zt[:]))
    NB = NCK // 512
    for g0 in range(0, G, 2):
        grp = []
        for gi in range(2):
            g = g0 + gi
            R = St()
            R.g, R.X = g, Xb[gi]
            R.ui = Ubuf.get(kb, "pool")
            if u_load is not None:
                R.ev_u = u_load(g, Ubuf.bufs[R.ui], Uslot[R.ui])
            else:
                R.ev_u = Uslot[R.ui].dma(kb, "pool", Ubuf.bufs[R.ui][:], dr["U"][g])
            R.U = Ubuf.bufs[R.ui]
            grp.append(R)
        for gi, R in enumerate(grp):
            R.ev_x = ev_pad
            evs = []
            for bi in range(NB):
                b = psum.get(kb, "pe")
                kb.wait("pe", R.ev_u, mat_ev[R.g])
                e = kb.mark("pe", nc.tensor.matmul(psum.bufs[b][:], BzT[R.g][:], R.U[:, bi * 512:(bi + 1) * 512], start=True, stop=True))
                kb.wait("act", e, x_free[gi])
                e2 = kb.mark("act", nc.scalar.copy(R.X[:, 2 + bi * 512:2 + (bi + 1) * 512], psum.bufs[b][:]))
                psum.release(b, e2)
                evs.append(e2)
            R.ev_x = [R.ev_x] + evs
        for j in range(NLV):
            d = 1 << j
            for gi, R in enumerate(grp):
                jobs = []
                for bi in range(NB):
                    lo = max(bi * 512, d) if d > 1 else bi * 512
                    hi = (bi + 1) * 512
                    if lo >= hi:
                        continue
                    b = psum.get(kb, "pe")
                    kb.wait("pe", R.ev_x)
                    e = kb.mark("pe", nc.tensor.matmul(psum.bufs[b][:, lo - bi * 512:512], AT[R.g][:, j, :],
                                                       R.X[:, 2 + lo - d:2 + hi - d], start=True, stop=True))
                    jobs.append((b, lo, hi, bi))
                evs = []
                kb.wait("dve", e, R.ev_x)
                for (b, lo, hi, bi) in jobs:
                    e2 = kb.mark("dve", nc.vector.tensor_tensor(R.X[:, 2 + lo:2 + hi], psum.bufs[b][:, lo - bi * 512:512],
                                                                R.X[:, 2 + lo:2 + hi], ALU.add))
                    psum.release(b, e2)
                    evs.append(e2)
                R.ev_x = evs
        for gi, R in enumerate(grp):
            for bi in range(NB):
                b = psum.get(kb, "pe")
                kb.wait("pe", R.ev_x)
                nc.tensor.matmul(psum.bufs[b][:], TzT[R.g][:], R.U[:, bi * 512:(bi + 1) * 512], start=True, stop=False)
                e = kb.mark("pe", nc.tensor.matmul(psum.bufs[b][:], CzT[R.g][:], R.X[:, 1 + bi * 512:1 + (bi + 1) * 512],
                                                   start=False, stop=True))
                yi = ystage.get(kb, "act")
                kb.wait("act", e)
                e2 = kb.mark("act", nc.scalar.copy(ystage.bufs[yi][:], psum.bufs[b][:]))
                psum.release(b, e2)
                kb.wait("sp", e2)
                if y_store is not None:
                    ev = y_store(R.g, bi, ystage.bufs[yi], yslot[yi])
                else:
                    ev = yslot[yi].dma(kb, "sp", Y[R.g, :, bi * 512:(bi + 1) * 512], ystage.bufs[yi][:])
                ystage.release(yi, ev)
            Ubuf.release(R.ui, e)
            x_free[gi] = e
    for sl in yslot:
        kb.E["sp"].wait_ge(sl.sem, sl.cnt)


def s5_consts():
    kvec = np.array([7, 6, 5, 4, 3, 2, 1, 0, -1, -2, -3, -4, -5, -6, -7, -8, 1, 2, 3, 4, 5, 6, 7, 8], np.float32)
    c = {}
    c["kvec"] = np.tile(kvec[None, :], (64, 1))
    c["selLo"] = np.concatenate([np.eye(64, dtype=np.float32), np.zeros((64, 64), np.float32)], axis=1)
    c["selHi"] = np.concatenate([np.zeros((64, 64), np.float32), np.eye(64, dtype=np.float32)], axis=1)
    c["ident"] = np.eye(128, dtype=np.float32)
    sw = np.zeros((128, 128), np.float32)
    sw[np.arange(64), np.arange(64) + 64] = 1.0
    sw[np.arange(64) + 64, np.arange(64)] = 1.0
    c["swap"] = sw
    i = np.arange(128)
    c["tmask"] = (i[None, :] // 16 >= i[:, None] // 16).astype(np.float32)
    return c


def s5_in_maps(inp, u):
    Uall = np.ascontiguousarray(u.reshape(NCK, 8, 64, 16).transpose(2, 1, 3, 0)).reshape(64, 128, NCK)
    cst = s5_consts()
    maps = []
    for c in range(NCORES):
        gs = slice(c * GPC, (c + 1) * GPC)
        m = dict(cst)
        m["U"] = np.ascontiguousarray(Uall[gs])
        m["a_re"] = np.ascontiguousarray(inp["s5_a_re"][0][gs].T)
        m["a_im"] = np.ascontiguousarray(inp["s5_a_im"][0][gs].T)
        m["ldt"] = np.ascontiguousarray(np.tile(inp["s5_log_dt"][0][gs][None, :], (64, 1)))
        m["b_re"] = np.ascontiguousarray(inp["s5_b_re"][0][gs].transpose(1, 0, 2))
        m["b_im"] = np.ascontiguousarray(inp["s5_b_im"][0][gs].transpose(1, 0, 2))
        m["c_re"] = np.ascontiguousarray(inp["s5_c_re"][0][gs].transpose(2, 0, 1))
        m["c_im"] = np.ascontiguousarray(inp["s5_c_im"][0][gs].transpose(2, 0, 1))
        maps.append(m)
    return maps


def run_phase_b(inp, u):
    nc = build_phase_b()
    res = _run(nc, s5_in_maps(inp, u))
    Yall = np.concatenate([r["Y"] for r in res.results], axis=0)
    return np.ascontiguousarray(Yall.reshape(64, 8, 16, NCK).transpose(3, 1, 0, 2)).reshape(SEQ, 1024)


C_GAINS = 8 * 5 + 8


def build_phase_c():
    nc = bass.Bass("TRN2", target_bir_lowering=False)
    I = lambda n, shp: nc.dram_tensor(n, shp, F32, kind="ExternalInput").ap()
    O = lambda n, shp: nc.dram_tensor(n, shp, F32, kind="ExternalOutput").ap()
    h1T, uT, yT = I("h1T", [D, TOK]), I("uT", [D, TOK]), I("yT", [D, TOK])
    gains = I("gains", [128, C_GAINS])
    wglu, wout = I("wglu", [KT, 128, KT, 128]), I("wout", [KT, 128, KT, 128])
    ffw = [(I(f"wg{i}", [FT, 128, KT, 128]), I(f"wu{i}", [FT, 128, KT, 128]), I(f"wd{i}", [2, KT, 128, FH, 128])) for i in range(2)]
    wqk = I("wqk", [16, 128, KT, 128])
    wv = I("wv", [128, KT, D])
    pos = I("pos", [32, TOK])
    rmT = I("rmT", [32, 32])
    h4T = O("h4T", [D, TOK])
    qkT = O("qkT", [16, 128, TOK])
    v_o = O("v", [TOK, D])
    with ExitStack() as es:
        kb = KB(nc, es)
        S = tok_state(kb, C_GAINS)
        S.ev_gains = S.misc.dma(kb, "sp", S.gains[:], gains)
        load_hT(kb, S, h1T)
        with nc.named_scope("s5post"):
            s5_post(kb, S, uT, yT, wglu, wout, gcol_d=0)
        with nc.named_scope("ffn01"):
            ffn(kb, S, *ffw[0], 8)
        with nc.named_scope("ffn10"):
            ffn(kb, S, *ffw[1], 16)
        with nc.named_scope("norm"):
            rmsnorm(kb, S, 24)
        store_hT(kb, S, h4T)
        with nc.named_scope("qkv"):
            qkv(kb, S, wqk, wv, pos, rmT, qkT, v_o, gcol=32)
        finish(kb, S)
    return nc


def s5_post(kb, S, uT, yT, wglu, wout, gcol_d):
    nc = kb.nc
    yu = S.act[:, 8:11, :].bitcast(F32)
    bufs = [yu[:, i, j * CH:(j + 1) * CH] for i in range(3) for j in range(2)]
    ring = Ring([(bufs[2 * i], bufs[2 * i + 1]) for i in range(3)])
    slots = [DmaSlot(kb, "yu") for _ in range(3)]
    g_ev = [None] * NCH
    for c in range(NCH):
        cs = slice(c * CH, (c + 1) * CH)
        for f in range(KT):
            i = ring.get(kb, "sp")
            kb.wait("sp", S.act_free)
            yb, ub = ring.bufs[i]
            slots[i].dma(kb, "sp", yb, yT[f * 128:(f + 1) * 128, cs])
            ev = slots[i].dma(kb, "sp", ub, uT[f * 128:(f + 1) * 128, cs])
            kb.wait("dve", ev, S.ev_gains)
            e1 = kb.mark("dve", nc.vector.scalar_tensor_tensor(yb, ub, S.gains[:, gcol_d + f:gcol_d + f + 1], yb, ALU.mult, ALU.add))
            kb.wait("act", e1, S.hn_free)
            e2 = kb.mark("act", nc.scalar.activation(S.hn[:, f, cs], yb, AF.Gelu_apprx_tanh))
            ring.release(i, e2)
        g_ev[c] = e2
    s5_post_proj(kb, S, wglu, wout, g_ev)


def s5_post_proj(kb, S, wglu, wout, g_ev):
    nc = kb.nc
    g2 = S.act[:, 0:KT, :]
    g2_ev = [None] * NCH

    def evac_glu(ot, c, bank, ev_mm):
        cs = slice(c * CH, (c + 1) * CH)
        ti = S.sg.get(kb, "act")
        kb.wait("act", ev_mm)
        e1 = kb.mark("act", nc.scalar.activation(S.sg.bufs[ti][:], bank[:], AF.Sigmoid))
        kb.wait("dve", e1, S.act_free)
        e2 = kb.mark("dve", nc.vector.tensor_tensor(g2[:, ot, cs], S.sg.bufs[ti][:], S.hn[:, ot, cs], ALU.mult))
        S.sg.release(ti, e2)
        g2_ev[c] = e2 if ot == KT - 1 else g2_ev[c]
        return e1
    proj_fm(kb, S, wglu, KT, evac_glu, opnd=S.hn, opnd_ev=g_ev)

    def evac_out(ot, c, bank, ev_mm):
        cs = slice(c * CH, (c + 1) * CH)
        kb.wait("dve", ev_mm, S.h_ev[c])
        e = kb.mark("dve", nc.vector.tensor_tensor(S.hT[:, ot, cs], bank[:], S.hT[:, ot, cs], ALU.add))
        S.h_ev[c] = e
        return e
    proj_fm(kb, S, wout, KT, evac_out, opnd=g2, opnd_ev=g2_ev)


def qkv(kb, S, wqk, wv, pos, rmT, qkT, v_o, gcol, fused=None):
    nc = kb.nc
    st = St()
    st.last = S.ev_gains

    def V(fn, *a, **k):
        kb.wait("dve", st.last)
        st.last = kb.mark("dve", fn(*a, **k))

    def A(*a, **k):
        kb.wait("act", st.last)
        st.last = kb.mark("act", nc.scalar.activation(*a, **k))
    vt, vs, stt = nc.vector.tensor_tensor, nc.vector.tensor_scalar, nc.vector.scalar_tensor_tensor
    kb.wait("sp", S.act_free)
    tabs = S.act[0:32, 4:8, :].bitcast(F32)
    Ct = tabs[:, 0:2, :].rearrange("p a b -> p (a b)")
    Sn = tabs[:, 2:4, :].rearrange("p a b -> p (a b)")
    wk = S.act[0:32, 8:11, :].bitcast(F32)
    ang = wk[:, 0:2, :].rearrange("p a b -> p (a b)")
    r1 = kb.sb("rope_r1", [32, TOK], F32)
    rm = kb.sb("rmT", [32, 32], F32R)
    gq = kb.sb("gq", [128, 2], F32)
    ev_pos = S.misc.dma(kb, "sp", r1[:], pos)
    ev_rm = S.misc.dma(kb, "pool", rm[:], rmT)
    st.last = [st.last, ev_pos, S.act_free]
    invf = S.gains[0:32, gcol + 2:gcol + 3]
    V(vs, ang, r1[:], invf, None, ALU.mult)
    for off, dst in ((0.0, Sn), (TWO_PI / 4, Ct)):
        V(vs, r1[:], ang, off, 1.0 / TWO_PI, ALU.add, ALU.mult)
        V(vs, r1[:], r1[:], MAGIC, MAGIC, ALU.add, ALU.subtract)
        V(vs, dst, ang, off, None, ALU.add)
        V(stt, dst, r1[:], -TWO_PI, dst, ALU.mult, ALU.add)
        V(vs, dst, dst, -3.14159265, 3.14159265, ALU.max, ALU.min)
        A(dst, dst, AF.Sin)
    V(vs, gq[:, 0:1], S.gains[:, gcol:gcol + 1], 128.0 ** -0.5, None, ALU.mult)
    V(nc.vector.tensor_copy, gq[:, 1:2], S.gains[:, gcol + 1:gcol + 2])
    ev_tab = st.last
    qn_ring = Ring([kb.sb("qn", [128, CH], F32R) for _ in range(4)])
    t1_ring = Ring([kb.sb("ropet", [32, CH], F32) for _ in range(2)])
    t2_ring = Ring([kb.sb("ropet2", [32, CH], F32) for _ in range(2)])

    mainr = Ring(S.psum.bufs[0:4])
    mainr.rel = [list(r) for r in S.psum.rel[0:4]]
    auxr = Ring(S.psum.bufs[4:8])
    auxr.rel = [list(r) for r in S.psum.rel[4:8]]

    def evac_qk(ot, c, bank, ev_mm):
        cs = slice(c * CH, (c + 1) * CH)
        which = 0 if ot < 8 else 1
        i = S.sq.get(kb, "act")
        kb.wait("act", ev_mm)
        e = kb.mark("act", nc.scalar.activation(S.sq.bufs[i][:], bank[:], AF.Square))
        b2 = auxr.get(kb, "pe")
        kb.wait("pe", e)
        e2 = kb.mark("pe", nc.tensor.matmul(auxr.bufs[b2][:], S.ones[:], S.sq.bufs[i][:], start=True, stop=True))
        S.sq.release(i, e2)
        ri = S.rt.get(kb, "act")
        kb.wait("act", e2)
        e3 = kb.mark("act", nc.scalar.activation(S.rt.bufs[ri][:], auxr.bufs[b2][:], AF.Sqrt, bias=EPS, scale=1.0 / 128))
        auxr.release(b2, e3)
        si = S.rstd.get(kb, "dve")
        kb.wait("dve", e3)
        e4 = kb.mark("dve", nc.vector.reciprocal(S.rstd.bufs[si][:], S.rt.bufs[ri][:]))
        S.rt.release(ri, e4)
        qi = qn_ring.get(kb, "dve")
        qn = qn_ring.bufs[qi]
        kb.wait("dve", e4, ev_mm, ev_tab)
        e5 = kb.mark("dve", nc.vector.scalar_tensor_tensor(qn[:], bank[:], gq[:, which:which + 1], S.rstd.bufs[si][:],
                                                           ALU.mult, ALU.mult))
        S.rstd.release(si, e5)
        return e5, (ot, c, qi, e5)

    def evac_qk2(ctx):
        ot, c, qi, e5 = ctx
        cs = slice(c * CH, (c + 1) * CH)
        qn = qn_ring.bufs[qi]
        b3 = auxr.get(kb, "pe")
        kb.wait("pe", e5, ev_rm)
        e6 = kb.mark("pe", nc.tensor.matmul(auxr.bufs[b3][0:32, :], rm[:], qn[0:32, :], start=True, stop=True))
        if fused is not None:
            oi = fused["ring"].get(kb, "act")
            ob = fused["ring"].bufs[oi]
        else:
            oi = S.stage.get(kb, "act")
            ob = S.stage.bufs[oi]
        kb.wait("act", e5)
        e7 = kb.mark("act", nc.scalar.copy(ob[:], qn[:]))
        ti = t1_ring.get(kb, "dve")
        kb.wait("dve", e5, ev_tab)
        e8 = kb.mark("dve", nc.vector.tensor_tensor(t1_ring.bufs[ti][:], qn[0:32, :], Ct[:, cs], ALU.mult))
        kb.wait("dve", e6, e7, e8)
        t2i = t2_ring.get(kb, "dve")
        e9 = kb.mark("dve", nc.vector.tensor_tensor(t2_ring.bufs[t2i][:], auxr.bufs[b3][0:32, :], Sn[:, cs], ALU.mult))
        kb.wait("dve", e9)
        e10 = kb.mark("dve", nc.vector.tensor_tensor(ob[0:32, :], t2_ring.bufs[t2i][:], t1_ring.bufs[ti][:], ALU.add))
        auxr.release(b3, e9)
        t1_ring.release(ti, e10)
        t2_ring.release(t2i, e10)
        qn_ring.release(qi, e6, e7, e8)
        kb.wait("sp", e10, e7)
        if fused is not None:
            ev = fused["slots"][oi].dma(kb, "sp", fused["qk_dst"](ot, c), ob[:])
            fused["ring"].release(oi, ev)
        else:
            ev = S.stage_slot[oi].dma(kb, "sp", qkT[ot, :, cs], ob[:])
            S.stage.release(oi, ev)
    proj_fm(kb, S, wqk, 16, evac_qk, defer=True, ring=mainr, evac2=evac_qk2)
    S.psum.rel = mainr.rel + auxr.rel
    S.psum.i = 0
    wvt = S.act[:, 0:4, :].rearrange("p a (b c) -> p (a b) c", c=D)
    kb.wait("pool", S.act_free)
    ev_wv = S.misc.dma(kb, "pool", wvt, wv)
    for tt in range(TOK // 128):
        c = tt * 128 // CH
        for half in range(2):
            b = S.psum.get(kb, "pe")
            kb.wait("pe", ev_wv, S.hn_ev[c])
            for k in range(KT):
                ins = nc.tensor.matmul(S.psum.bufs[b][:], S.hn[:, k, tt * 128:(tt + 1) * 128], wvt[:, k, half * 512:(half + 1) * 512],
                                       start=(k == 0), stop=(k == KT - 1))
            e = kb.mark("pe", ins)
            S.hn_free = e
            if fused is not None:
                oi = fused["ring"].get(kb, "act")
                kb.wait("act", e)
                e2 = kb.mark("act", nc.scalar.copy(fused["ring"].bufs[oi][:], S.psum.bufs[b][:]))
                S.psum.release(b, e2)
                kb.wait("sp", e2)
                ev = fused["slots"][oi].dma(kb, "sp", fused["v_dst"](tt, half),
                                            fused["ring"].bufs[oi][:].rearrange("p (h d) -> p h d", d=128))
                fused["ring"].release(oi, ev)
                continue
            oi = S.stage.get(kb, "act")
            kb.wait("act", e)
            e2 = kb.mark("act", nc.scalar.copy(S.stage.bufs[oi][:], S.psum.bufs[b][:]))
            S.psum.release(b, e2)
            kb.wait("sp", e2)
            ev = S.stage_slot[oi].dma(kb, "sp", v_o[tt * 128:(tt + 1) * 128, half * 512:(half + 1) * 512], S.stage.bufs[oi][:])
            S.stage.release(oi, ev)
    S.act_free = e


def rope_consts():
    inv_freq = (500000.0 ** (-np.arange(0, 32, 2, dtype=np.float32) / 32)).astype(np.float32)
    invf = np.concatenate([inv_freq, inv_freq])
    rmT = np.zeros((32, 32), np.float32)
    for i in range(16):
        rmT[16 + i, i] = -1.0
        rmT[i, 16 + i] = 1.0
    return invf, rmT


def run_phase_c(inp, h1, u, yscan):
    nc = build_phase_c()
    invf, rmT = rope_consts()
    gains = np.zeros((128, C_GAINS), np.float32)
    gains[:, 0:8] = col_layout(inp["s5_d"][0])
    gains[:, 8:16] = col_layout(inp["ffn_norm"][0, 1])
    gains[:, 16:24] = col_layout(inp["ffn_norm"][1, 0])
    gains[:, 24:32] = col_layout(inp["mix_norm"][1])
    gains[:, 32] = inp["moba_q_norm"][0]
    gains[:, 33] = inp["moba_k_norm"][0]
    gains[0:32, 34] = invf
    wqkv = inp["moba_w_qkv"][0]
    common = dict(gains=gains, wglu=tile_w(inp["s5_w_glu"][0]), wout=tile_w(inp["s5_w_out"][0]),
                  wg0=tile_w(inp["ffn_w_gate"][0, 1]), wu0=tile_w(inp["ffn_w_up"][0, 1]), wd0=tile_wd(inp["ffn_w_down"][0, 1]),
                  wg1=tile_w(inp["ffn_w_gate"][1, 0]), wu1=tile_w(inp["ffn_w_up"][1, 0]), wd1=tile_wd(inp["ffn_w_down"][1, 0]),
                  wqk=tile_w(wqkv[:, 0:2048]),
                  wv=np.ascontiguousarray(wqkv[:, 2048:3072].reshape(KT, 128, D).transpose(1, 0, 2)), rmT=rmT)
    maps = []
    for c in range(NCORES):
        ts = slice(c * TOK, (c + 1) * TOK)
        m = dict(common)
        m["h1T"] = np.ascontiguousarray(h1[ts].T)
        m["uT"] = np.ascontiguousarray(u[ts].T)
        m["yT"] = np.ascontiguousarray(yscan[ts].T)
        m["pos"] = np.ascontiguousarray(np.tile(np.arange(c * TOK, (c + 1) * TOK, dtype=np.float32)[None, :], (32, 1)))
        maps.append(m)
    res = _run(nc, maps)
    h4 = np.concatenate([r["h4T"].T for r in res.results], axis=0)
    qk = np.concatenate([r["qkT"] for r in res.results], axis=2)
    v = np.concatenate([r["v"] for r in res.results], axis=0)
    q = np.ascontiguousarray(qk[0:8].transpose(2, 0, 1))
    k = np.ascontiguousarray(qk[8:16].transpose(2, 0, 1))
    return h4, q, k, v.reshape(SEQ, 8, 128)


def build_phase_e():
    nc = bass.Bass("TRN2", target_bir_lowering=False)
    I = lambda n, shp: nc.dram_tensor(n, shp, F32, kind="ExternalInput").ap()
    h4T, oT = I("h4T", [D, TOK]), I("oT", [D, TOK])
    gains = I("gains", [128, 8])
    wo = I("wo", [KT, 128, KT, 128])
    wg, wu, wd = I("wg", [FT, 128, KT, 128]), I("wu", [FT, 128, KT, 128]), I("wd", [2, KT, 128, FH, 128])
    outT = nc.dram_tensor("outT", [D, TOK], F32, kind="ExternalOutput").ap()
    with ExitStack() as es:
        kb = KB(nc, es)
        S = tok_state(kb, 8)
        S.ev_gains = S.misc.dma(kb, "sp", S.gains[:], gains)
        load_hT(kb, S, h4T)
        attn_out(kb, S, oT, wo)
        ffn(kb, S, wg, wu, wd, 0)
        store_hT(kb, S, outT)
        finish(kb, S)
    return nc


def attn_out(kb, S, oT, wo, o_gather=None):
    nc = kb.nc
    oslot = DmaSlot(kb, "oT")
    kb.wait("pool", S.hn_free)
    for f in range(KT):
        if o_gather is not None:
            ev = o_gather(f, S.hn[:, f, :], oslot)
        else:
            ev = oslot.dma(kb, "pool", S.hn[:, f, :], oT[f * 128:(f + 1) * 128, :])
    o_ev = [ev] * NCH

    def evac_out(ot, c, bank, ev_mm):
        cs = slice(c * CH, (c + 1) * CH)
        kb.wait("dve", ev_mm, S.h_ev[c])
        e = kb.mark("dve", nc.vector.tensor_tensor(S.hT[:, ot, cs], bank[:], S.hT[:, ot, cs], ALU.add))
        S.h_ev[c] = e
        return e
    proj_fm(kb, S, wo, KT, evac_out, opnd=S.hn, opnd_ev=o_ev)


def run_phase_e(inp, h4, o):
    nc = build_phase_e()
    common = dict(gains=col_layout(inp["ffn_norm"][1, 1]), wo=tile_w(inp["moba_w_out"][0]),
                  wg=tile_w(inp["ffn_w_gate"][1, 1]), wu=tile_w(inp["ffn_w_up"][1, 1]), wd=tile_wd(inp["ffn_w_down"][1, 1]))
    maps = []
    for c in range(NCORES):
        ts = slice(c * TOK, (c + 1) * TOK)
        m = dict(common)
        m["h4T"] = np.ascontiguousarray(h4[ts].T)
        m["oT"] = np.ascontiguousarray(o[ts].T)
        maps.append(m)
    res = _run(nc, maps)
    return np.concatenate([r["outT"].T for r in res.results], axis=0)


def kernel_unfused(inp):
    h1, u = run_phase_a(inp)
    ysc = run_phase_b(inp, u)
    h4, q, k, v = run_phase_c(inp, h1, u, ysc)
    o = run_phase_d(q, k, v)
    out = run_phase_e(inp, h4, o)
    return out.reshape(1, SEQ, D).astype(np.float32)


import concourse.bass as _bass_mod
I32 = mybir.dt.int32
NG_ALL = 16 + C_GAINS + 8


def gather_rows(kb, slot, out_ap, gath_ap, idx_col):
    ins = kb.nc.gpsimd.indirect_dma_start(out=out_ap, out_offset=None, in_=gath_ap,
                                          in_offset=_bass_mod.IndirectOffsetOnAxis(ap=idx_col, axis=0))
    ins.then_inc(slot.sem, 16)
    slot.cnt += 16
    return (slot.sem, slot.cnt, "dma")


def all_gather(kb, send, gath):
    finish_all(kb, "pool")
    sem = kb.newsem("ag")
    kb.nc.gpsimd.collective_compute("AllGather", ALU.bypass, replica_groups=[list(range(NCORES))],
                                    ins=[send.opt()], outs=[gath.opt()]).then_inc(sem, 1)
    kb.nc.gpsimd.wait_ge(sem, 1)


def build_fused(stop=None):
    nc = bass.Bass("TRN2", target_bir_lowering=False)
    nc.declared = []

    def I(n, shp, dt=F32):
        nc.declared.append(n)
        return nc.dram_tensor(n, shp, dt, kind="ExternalInput").ap()

    def dbg_out(src, shp, dt=F32):
        o = nc.dram_tensor("dbg", shp, dt, kind="ExternalOutput").ap()
        with nc.semaphore("dbgsem") as sm:
            nc.sync.dma_start(out=o, in_=src).then_inc(sm, 16)
            nc.sync.wait_ge(sm, 16)
        return nc
    xT = I("xT", [D, TOK])
    gains = I("gains", [128, NG_ALL])
    nffn = {None: 4, "A": 1, "B": 1, "C": 3, "D": 3}[stop]
    ffw = [(I(f"wg{i}", [FT, 128, KT, 128]), I(f"wu{i}", [FT, 128, KT, 128]), I(f"wd{i}", [2, KT, 128, FH, 128])) for i in range(nffn)]
    win = I("win", [KT, 128, KT, 128])
    if stop not in ("A", "B"):
        wglu, wout = I("wglu", [KT, 128, KT, 128]), I("wout", [KT, 128, KT, 128])
        wqk = I("wqk", [16, 128, KT, 128])
        wv = I("wv", [128, KT, D])
        pos, rmT, tri = I("pos", [32, TOK]), I("rmT", [32, 32]), I("tri", [128, 128])
    if stop is None:
        wo = I("wo", [KT, 128, KT, 128])
    dr = {}
    for n, shp in (("a_re", [64, GPC]), ("a_im", [64, GPC]), ("ldt", [64, GPC]),
                   ("b_re", [64, GPC, 16]), ("b_im", [64, GPC, 16]), ("c_re", [64, GPC, 16]), ("c_im", [64, GPC, 16]),
                   ("kvec", [64, NKV]), ("selLo", [64, 128]), ("selHi", [64, 128]), ("ident", [128, 128]),
                   ("swap", [128, 128]), ("tmask", [128, 128])):
        dr[n] = I(n, shp)
    idx_u, idx_y, idx_q, idx_o = I("idx_u", [128, 64], I32), I("idx_y", [128, 64], I32), I("idx_q", [128, 24], I32), I("idx_o", [128, 8], I32)
    outT = nc.dram_tensor("outT", [D, TOK], F32, kind="ExternalOutput").ap()
    T = lambda n, shp, dt=F32, **k: nc.dram_tensor(n, shp, dt, **k).ap()
    hspill, ukeep = T("hspill", [D, TOK]), T("ukeep", [D, TOK])
    usend, ugath = T("usend", [8192, 256]), T("ugath", [8 * 8192, 256], addr_space="Shared")
    ysend, ygath = T("ysend", [8192, 256]), T("ygath", [8 * 8192, 256], addr_space="Shared")
    qsend, qgath = T("qsend", [3072, TOK], BF16), T("qgath", [8 * 3072, TOK], BF16, addr_space="Shared")
    osend, ogath = T("osend", [1024, TOK], BF16), T("ogath", [8 * 1024, TOK], BF16, addr_space="Shared")

    with ExitStack() as es:
        kb = KB(nc, es)
        S = tok_state(kb, 16)
        S.ev_gains = S.misc.dma(kb, "sp", S.gains[:], gains[:, 0:16])
        load_hT(kb, S, xT)
        ffn(kb, S, *ffw[0], 0)
        store_hT(kb, S, hspill)
        rmsnorm(kb, S, 8)
        ust = Ring([kb.sb("ust", [128, 8, 256], F32R) for _ in range(2)])
        ust_slot = [DmaSlot(kb, "ust") for _ in range(2)]
        U = St()

        def evac_u(ot, c, bank, ev_mm):
            i = S.stage.get(kb, "act")
            kb.wait("act", ev_mm)
            e = kb.mark("act", nc.scalar.copy(S.stage.bufs[i][:], bank[:]))
            kb.wait("sp", e)
            ev = S.stage_slot[i].dma(kb, "sp", ukeep[ot * 128:(ot + 1) * 128, c * CH:(c + 1) * CH], S.stage.bufs[i][:])
            S.stage.release(i, ev)
            if c == 0:
                U.i = ust.get(kb, "dve")
            kb.wait("dve", ev_mm)
            e2 = kb.mark("dve", nc.vector.tensor_copy(ust.bufs[U.i][:, :, c * 64:(c + 1) * 64],
                                                      bank[:].rearrange("p (c s) -> p s c", s=8)))
            if c == NCH - 1:
                kb.wait("sp", e2)
                for s_ in range(8):
                    ev2 = ust_slot[U.i].dma(kb, "sp", usend[s_ * 1024 + ot * 128:s_ * 1024 + (ot + 1) * 128, :], ust.bufs[U.i][:, s_, :].bitcast(F32))
                ust.release(U.i, ev2)
            return [e, e2]
        proj_fm(kb, S, win, KT, evac_u)
        all_gather(kb, usend, ugath)
    nc.all_engine_barrier()
    if stop == "A":
        return dbg_out(ugath[0:8192, :], [8192, 256])

    with ExitStack() as es:
        kb = KB(nc, es)
        idxu = kb.sb("idxu", [128, 64], I32)
        ev_idx = DmaSlot(kb, "idxu").dma(kb, "sp", idxu[:], idx_u)

        def u_load(g, Ub, slot):
            kb.wait("pool", ev_idx)
            for cp in range(8):
                ev = gather_rows(kb, slot, Ub[:, cp * 256:(cp + 1) * 256], ugath, idxu[:, g * 8 + cp:g * 8 + cp + 1])
            return ev

        def y_store(g, bi, yst, slot):
            dst = ysend.rearrange("(c r) x -> r c x", c=8)[g * 128:(g + 1) * 128, 2 * bi:2 * bi + 2, :]
            return slot.dma(kb, "sp", dst, yst[:].rearrange("p (c x) -> p c x", x=256))
        s5_scan(kb, dr, None, u_load, y_store)
        all_gather(kb, ysend, ygath)
    nc.all_engine_barrier()
    if stop == "B":
        return dbg_out(ysend, [8192, 256])

    with ExitStack() as es:
        kb = KB(nc, es)
        S = tok_state(kb, C_GAINS)
        S.ev_gains = S.misc.dma(kb, "sp", S.gains[:], gains[:, 16:16 + C_GAINS])
        load_hT(kb, S, hspill)
        idxy = kb.sb("idxy", [128, 64], I32)
        ev_idx = DmaSlot(kb, "idxy").dma(kb, "sp", idxy[:], idx_y)
        ybufs = S.act[:, 7:11, :].bitcast(F32)
        yring = Ring([ybufs[:, 2 * i:2 * i + 2, :].rearrange("p a (t c) -> p (a t) c", c=256) for i in range(2)])
        yslots = [DmaSlot(kb, "ygat") for _ in range(2)]
        g_ev = [None] * NCH
        for f in range(KT):
            yi = yring.get(kb, "pool")
            yb = yring.bufs[yi]
            kb.wait("pool", ev_idx, S.act_free)
            for t_ in range(8):
                ev_y = gather_rows(kb, yslots[yi], yb[:, t_, :], ygath, idxy[:, f * 8 + t_:f * 8 + t_ + 1])
            for c in range(NCH):
                cs = slice(c * CH, (c + 1) * CH)
                i = S.stage.get(kb, "sp")
                ub = S.stage.bufs[i]
                ev_u = S.stage_slot[i].dma(kb, "sp", ub[:], ukeep[f * 128:(f + 1) * 128, cs])
                kb.wait("dve", ev_u, ev_y, S.ev_gains)
                ubv = ub[:].rearrange("p (c t) -> p c t", t=8)
                yv = yb[:, :, c * 64:(c + 1) * 64].rearrange("p t c -> p c t")
                e1 = kb.mark("dve", nc.vector.scalar_tensor_tensor(ubv, ubv, S.gains[:, f:f + 1], yv, ALU.mult, ALU.add))
                kb.wait("act", e1, S.hn_free)
                e2 = kb.mark("act", nc.scalar.activation(S.hn[:, f, cs], ub[:], AF.Gelu_apprx_tanh))
                S.stage.release(i, e2)
                g_ev[c] = e2
            yring.release(yi, e1)
        s5_post_proj(kb, S, wglu, wout, g_ev)
        ffn(kb, S, *ffw[1], 8)
        ffn(kb, S, *ffw[2], 16)
        rmsnorm(kb, S, 24)
        store_hT(kb, S, hspill)
        qst = Ring([kb.sb("qst", [128, CH], BF16) for _ in range(3)])
        fz = dict(ring=qst, slots=[DmaSlot(kb, "qst") for _ in range(3)],
                  qk_dst=lambda ot, c: qsend[ot * 128:(ot + 1) * 128, c * CH:(c + 1) * CH],
                  v_dst=lambda tt, half: qsend[2048:3072, :].rearrange("(h p) x -> p h x", p=128)[:, half * 4:(half + 1) * 4, tt * 128:(tt + 1) * 128])
        qkv(kb, S, wqk, wv, pos, rmT, None, None, gcol=32, fused=fz)
        all_gather(kb, qsend, qgath)
    nc.all_engine_barrier()
    if stop == "C":
        return dbg_out(qsend, [3072, TOK], BF16)

    with ExitStack() as es:
        kb = KB(nc, es)
        idxq = kb.sb("idxq", [128, 24], I32)
        ev_idx = DmaSlot(kb, "idxq").dma(kb, "sp", idxq[:], idx_q)

        def loader(qT, kT, va, seg):
            evs = []
            kb.wait("pool", ev_idx)
            for cp in range(8):
                cs = slice(cp * TOK, (cp + 1) * TOK)
                gather_rows(kb, seg[cp], kT[:, cs], qgath, idxq[:, 8 + cp:9 + cp])
                gather_rows(kb, seg[cp], qT[:, cs], qgath, idxq[:, cp:cp + 1])
                evs.append(gather_rows(kb, seg[cp], va[:, cp * 16:(cp + 1) * 16, 0:128],
                                       qgath.rearrange("r (t d) -> r t d", d=128), idxq[:, 16 + cp:17 + cp]))
            return evs
        o_store = dict(ident=dr["ident"],
                       dst=lambda q0: osend[(q0 // TOK) * 128:(q0 // TOK + 1) * 128, q0 % TOK:q0 % TOK + 256])
        attention(kb, None, None, None, tri, None, loader, o_store)
        all_gather(kb, osend, ogath)
    nc.all_engine_barrier()
    if stop == "D":
        return dbg_out(osend, [1024, TOK], BF16)

    with ExitStack() as es:
        kb = KB(nc, es)
        S = tok_state(kb, 8)
        S.ev_gains = S.misc.dma(kb, "sp", S.gains[:], gains[:, 16 + C_GAINS:NG_ALL])
        load_hT(kb, S, hspill)
        idxo = kb.sb("idxo", [128, 8], I32)
        ev_idx = DmaSlot(kb, "idxo").dma(kb, "sp", idxo[:], idx_o)

        def o_gather(f, dst, slot):
            kb.wait("pool", ev_idx)
            return gather_rows(kb, slot, dst, ogath, idxo[:, f:f + 1])
        attn_out(kb, S, None, wo, o_gather)
        ffn(kb, S, *ffw[3], 0)
        store_hT(kb, S, outT)
        finish_all(kb, "sp")
    return nc


def fused_in_maps(inp):
    invf, rmT = rope_consts()
    gains = np.zeros((128, NG_ALL), np.float32)
    gains[:, 0:8] = col_layout(inp["ffn_norm"][0, 0])
    gains[:, 8:16] = col_layout(inp["mix_norm"][0])
    o = 16
    gains[:, o + 0:o + 8] = col_layout(inp["s5_d"][0])
    gains[:, o + 8:o + 16] = col_layout(inp["ffn_norm"][0, 1])
    gains[:, o + 16:o + 24] = col_layout(inp["ffn_norm"][1, 0])
    gains[:, o + 24:o + 32] = col_layout(inp["mix_norm"][1])
    gains[:, o + 32] = inp["moba_q_norm"][0]
    gains[:, o + 33] = inp["moba_k_norm"][0]
    gains[0:32, o + 34] = invf
    gains[:, 16 + C_GAINS:NG_ALL] = col_layout(inp["ffn_norm"][1, 1])
    wqkv = inp["moba_w_qkv"][0]
    common = dict(gains=gains, rmT=rmT, tri=np.triu(np.ones((128, 128), np.float32)),
                  win=tile_w(inp["s5_w_in"][0]), wglu=tile_w(inp["s5_w_glu"][0]), wout=tile_w(inp["s5_w_out"][0]),
                  wo=tile_w(inp["moba_w_out"][0]), wqk=tile_w(wqkv[:, 0:2048]),
                  wv=np.ascontiguousarray(wqkv[:, 2048:3072].reshape(KT, 128, D).transpose(1, 0, 2)))
    for i, (l, j) in enumerate(((0, 0), (0, 1), (1, 0), (1, 1))):
        common[f"wg{i}"] = tile_w(inp["ffn_w_gate"][l, j])
        common[f"wu{i}"] = tile_w(inp["ffn_w_up"][l, j])
        common[f"wd{i}"] = tile_wd(inp["ffn_w_down"][l, j])
    common.update(s5_consts())
    x = inp["x"][0]
    p = np.arange(128)
    maps = []
    for c in range(NCORES):
        m = dict(common)
        gs = slice(c * GPC, (c + 1) * GPC)
        m["xT"] = np.ascontiguousarray(x[c * TOK:(c + 1) * TOK].T)
        m["pos"] = np.ascontiguousarray(np.tile(np.arange(c * TOK, (c + 1) * TOK, dtype=np.float32)[None, :], (32, 1)))
        m["a_re"] = np.ascontiguousarray(inp["s5_a_re"][0][gs].T)
        m["a_im"] = np.ascontiguousarray(inp["s5_a_im"][0][gs].T)
        m["ldt"] = np.ascontiguousarray(np.tile(inp["s5_log_dt"][0][gs][None, :], (64, 1)))
        m["b_re"] = np.ascontiguousarray(inp["s5_b_re"][0][gs].transpose(1, 0, 2))
        m["b_im"] = np.ascontiguousarray(inp["s5_b_im"][0][gs].transpose(1, 0, 2))
        m["c_re"] = np.ascontiguousarray(inp["s5_c_re"][0][gs].transpose(2, 0, 1))
        m["c_im"] = np.ascontiguousarray(inp["s5_c_im"][0][gs].transpose(2, 0, 1))
        iu = np.zeros((128, 64), np.int32)
        iy = np.zeros((128, 64), np.int32)
        iq = np.zeros((128, 24), np.int32)
        io = np.zeros((128, 8), np.int32)
        for a in range(8):
            for b in range(8):
                iu[:, a * 8 + b] = b * 8192 + (p // 16) * 1024 + (8 * c + a) * 16 + (p % 16)
                iy[:, a * 8 + b] = a * 8192 + c * 1024 + (p // 16) * 128 + b * 16 + (p % 16)
            iq[:, a] = a * 3072 + c * 128 + p
            iq[:, 8 + a] = a * 3072 + (8 + c) * 128 + p
            iq[:, 16 + a] = a * 3072 + 2048 + c * 128 + p
            io[:, a] = a * 1024 + c * 128 + p
        m["idx_u"], m["idx_y"], m["idx_q"], m["idx_o"] = iu, iy, iq, io
        maps.append(m)
    return maps


def kernel_fused(inp, stop=None):
    nc = build_fused(stop)
    maps = [{k: v for k, v in m.items() if k in nc.declared} for m in fused_in_maps(inp)]
    res = _run(nc, maps)
    if stop is not None:
        return [r["dbg"] for r in res.results]
    return np.concatenate([r["outT"].T for r in res.results], axis=0)


def kernel(**inp):
    inp = {k: np.asarray(v, dtype=np.float32) for k, v in inp.items()}
    out = kernel_unfused(inp)
    return out.reshape(1, SEQ, D).astype(np.float32)
```
